# Optimizing a Trainium2 kernel written in Bass

```python
import jax, jax.numpy as jnp
from jax import lax
import numpy as np

D_MODEL = 1024
BATCH = 8
SEQ = 4096
DEPTH = 4
DEC_BATCH = 16
DEC_SEQ = 2048
PAST_LEN = 128

GRID_W = 64
MEM_LEN = 256
N_GROUPS = 4
BRANCH_W = D_MODEL // N_GROUPS
N_HEADS = 4
HEAD_DIM = BRANCH_W // N_HEADS
ATT_KV_HEADS = 2
ROPE_AXIS_DIM = HEAD_DIM // 2
ROPE_THETA = 10000.0
Q_BLOCK = 128
CHUNK = 64
CONV_K = 3
N_DIRS = 2
GLA_DK = HEAD_DIM // 2
GLA_RANK = 16
GLA_NORMALIZER = 16.0
EPS = 1e-6

SPLITS = (
    N_HEADS * HEAD_DIM,
    ATT_KV_HEADS * HEAD_DIM,
    ATT_KV_HEADS * HEAD_DIM,
    BRANCH_W,
    3 * N_HEADS * HEAD_DIM,
    N_DIRS * N_HEADS,
    N_DIRS * N_HEADS,
    BRANCH_W,
    N_HEADS * GLA_DK,
    N_HEADS * GLA_DK,
    N_HEADS * HEAD_DIM,
    N_DIRS * GLA_RANK,
    BRANCH_W,
    N_HEADS * HEAD_DIM,
    BRANCH_W,
)
IN_COLS = sum(SPLITS)

kernel_name = 'hybrid_parallel_head_encoder'


def rms_norm(x, g):
    xf = x.astype(jnp.float32)
    y = xf * lax.rsqrt(jnp.mean(xf * xf, axis=-1, keepdims=True) + EPS)
    return (y * g.astype(jnp.float32)).astype(x.dtype)


def l2_norm(x):
    xf = x.astype(jnp.float32)
    return (xf * lax.rsqrt(jnp.sum(xf * xf, axis=-1, keepdims=True) + EPS)).astype(x.dtype)


def axial_rope_tables(n_tokens):
    rows = n_tokens // GRID_W
    row_pos = jnp.repeat(jnp.arange(rows, dtype=jnp.float32), GRID_W)
    col_pos = jnp.tile(jnp.arange(GRID_W, dtype=jnp.float32), rows)
    inv_freq = ROPE_THETA ** (-jnp.arange(0, ROPE_AXIS_DIM, 2, dtype=jnp.float32) / ROPE_AXIS_DIM)
    ang = jnp.stack([row_pos, col_pos], axis=-1)[..., None] * inv_freq
    return jnp.cos(ang), jnp.sin(ang)


def apply_axial_rope(x, cos, sin):
    B, S, H, D = x.shape
    half = ROPE_AXIS_DIM // 2
    xf = x.astype(jnp.float32).reshape(B, S, H, 2, 2, half)
    x1, x2 = xf[..., 0, :], xf[..., 1, :]
    c, s = cos[:, None], sin[:, None]
    out = jnp.stack([x1 * c - x2 * s, x2 * c + x1 * s], axis=-2)
    return out.reshape(B, S, H, D).astype(x.dtype)


def block_attention(q, k, v):
    B, S, H, D = q.shape
    kvh = k.shape[2]
    nb = S // Q_BLOCK
    qb = q.reshape(B, nb, Q_BLOCK, kvh, H // kvh, D).transpose(1, 0, 2, 3, 4, 5)
    scale = D ** -0.5

    def one_block(qblk):
        s = jnp.einsum('bqkgd,blkd->bkgql', qblk, k).astype(jnp.float32) * scale
        p = jax.nn.softmax(s, axis=-1).astype(v.dtype)
        return jnp.einsum('bkgql,blkd->bqkgd', p, v)

    o = lax.map(one_block, qb)
    return o.transpose(1, 0, 2, 3, 4, 5).reshape(B, S, H * D)


def short_conv_centered(x, w):
    S = x.shape[1]
    pad = CONV_K // 2
    xp = jnp.pad(x, ((0, 0), (pad, pad), (0, 0)))
    out = xp[:, 0:S] * w[0]
    for i in range(1, CONV_K):
        out = out + xp[:, i:i + S] * w[i]
    return out


def _to_chunks(x):
    B, S, H = x.shape[:3]
    x = x.reshape(B, S // CHUNK, CHUNK, H, *x.shape[3:])
    return jnp.moveaxis(x, 3, 1)


def _from_chunks(x):
    x = jnp.moveaxis(x, 1, 3)
    B, N, C, H = x.shape[:4]
    return x.reshape(B, N * C, H, *x.shape[4:])


def gated_delta_rule(q, k, v, beta, g):
    out_dtype = v.dtype
    q, k, v, beta, g = [_to_chunks(t.astype(jnp.float32)) for t in (q, k, v, beta, g)]
    dk, dv = q.shape[-1], v.shape[-1]
    q = q * dk ** -0.5
    b = jnp.cumsum(g, axis=-1)
    incl = jnp.tril(jnp.ones((CHUNK, CHUNK), bool))
    strict = jnp.tril(jnp.ones((CHUNK, CHUNK), bool), -1)
    diff = b[..., :, None] - b[..., None, :]
    decay = jnp.where(incl, jnp.exp(jnp.where(incl, diff, 0.0)), 0.0)
    kk = jnp.einsum('bhnid,bhnjd->bhnij', k, k)
    lower = jnp.where(strict, beta[..., :, None] * kk * decay, 0.0)
    t_mat = lower + jnp.eye(CHUNK, dtype=jnp.float32)
    rhs = jnp.concatenate([v * beta[..., None], k * (beta * jnp.exp(b))[..., None]], axis=-1)
    sol = lax.linalg.triangular_solve(t_mat, rhs, left_side=True, lower=True)
    u, w = sol[..., :dv], sol[..., dv:]
    qk = jnp.einsum('bhnid,bhnjd->bhnij', q, k) * decay
    q_dec = q * jnp.exp(b)[..., None]
    b_last = b[..., -1:]
    k_dec = k * jnp.exp(b_last - b)[..., None]
    chunk_decay = jnp.exp(b_last[..., 0])
    xs = tuple(jnp.moveaxis(t, 2, 0) for t in (q_dec, qk, u, w, k_dec, chunk_decay))

    def step(state, inp):
        qd, a, u_c, w_c, kd, cd = inp
        v_new = u_c - jnp.einsum('bhcd,bhde->bhce', w_c, state)
        o = jnp.einsum('bhcd,bhde->bhce', qd, state) + jnp.einsum('bhij,bhje->bhie', a, v_new)
        state = cd[..., None, None] * state + jnp.einsum('bhcd,bhce->bhde', kd, v_new)
        return state, o

    B, H = q.shape[:2]
    s0 = jnp.zeros((B, H, dk, dv), jnp.float32)
    _, o = lax.scan(step, s0, xs)
    return _from_chunks(jnp.moveaxis(o, 0, 2)).astype(out_dtype)


def gla_chunked(q, k, v, gk):
    out_dtype = v.dtype
    q, k, v, gk = [_to_chunks(t.astype(jnp.float32)) for t in (q, k, v, gk)]
    dk, dv = q.shape[-1], v.shape[-1]
    q = q * dk ** -0.5
    b = jnp.cumsum(gk, axis=3)
    q_e = q * jnp.exp(b)
    k_e = k * jnp.exp(-b)
    incl = jnp.tril(jnp.ones((CHUNK, CHUNK), bool))
    attn = jnp.where(incl, jnp.einsum('bhnid,bhnjd->bhnij', q_e, k_e), 0.0)
    o = jnp.einsum('bhnij,bhnje->bhnie', attn, v)
    b_last = b[..., -1:, :]
    k_dec = k * jnp.exp(b_last - b)
    d_state = jnp.einsum('bhncd,bhnce->bhnde', k_dec, v)
    chunk_decay = jnp.exp(b_last[..., 0, :])

    def step(state, inp):
        ds, cd = inp
        return cd[..., None] * state + ds, state

    B, H = q.shape[:2]
    s0 = jnp.zeros((B, H, dk, dv), jnp.float32)
    _, starts = lax.scan(step, s0, (jnp.moveaxis(d_state, 2, 0), jnp.moveaxis(chunk_decay, 2, 0)))
    o = o + jnp.einsum('bhncd,nbhde->bhnce', q_e, starts)
    return _from_chunks(o).astype(out_dtype)


def encoder_layer(h, mem, cos, sin, norm_g, w_in, att_q_norm_g, att_k_norm_g, gdn_conv_w, gdn_a_log,
                  gdn_dt_bias, gdn_out_norm_g, gla_w_gate_up, gla_b_gate, gla_out_norm_g, mem_norm_g,
                  w_mem_kv, w_out):
    B, S, _ = h.shape
    f32 = jnp.float32
    rev = lambda t: jnp.flip(t, axis=1)
    heads = lambda t, d: t.reshape(B, S, -1, d)

    xn = rms_norm(h, norm_g)
    proj = jnp.einsum('bsd,de->bse', xn, w_in)
    offsets = np.cumsum(SPLITS)[:-1].tolist()
    (a_q, a_k, a_v, a_gate, d_qkv, d_beta, d_alpha, d_gate,
     l_q, l_k, l_v, l_low, l_gate, m_q, m_gate) = jnp.split(proj, offsets, axis=-1)

    qa = apply_axial_rope(rms_norm(heads(a_q, HEAD_DIM), att_q_norm_g), cos, sin)
    ka = apply_axial_rope(rms_norm(heads(a_k, HEAD_DIM), att_k_norm_g), cos, sin)
    ya = block_attention(qa, ka, heads(a_v, HEAD_DIM)) * jax.nn.silu(a_gate)

    qkv = jax.nn.silu(short_conv_centered(d_qkv, gdn_conv_w))
    qb, kb, vb = jnp.split(qkv, 3, axis=-1)
    qb, kb, vb = l2_norm(heads(qb, HEAD_DIM)), l2_norm(heads(kb, HEAD_DIM)), heads(vb, HEAD_DIM)
    beta = jax.nn.sigmoid(d_beta.astype(f32)).reshape(B, S, N_DIRS, N_HEADS)
    alpha = -jnp.exp(gdn_a_log.astype(f32)) * jax.nn.softplus(
        d_alpha.astype(f32).reshape(B, S, N_DIRS, N_HEADS) + gdn_dt_bias.astype(f32))
    ob = gated_delta_rule(qb, kb, vb, beta[:, :, 0], alpha[:, :, 0]) + rev(
        gated_delta_rule(rev(qb), rev(kb), rev(vb), rev(beta[:, :, 1]), rev(alpha[:, :, 1])))
    yb = (rms_norm(ob, gdn_out_norm_g) * jax.nn.silu(heads(d_gate, HEAD_DIM))).reshape(B, S, BRANCH_W)

    gate_logit = jnp.einsum('bsnr,nrk->bsnk', l_low.reshape(B, S, N_DIRS, GLA_RANK), gla_w_gate_up) + gla_b_gate
    gk = (jax.nn.log_sigmoid(gate_logit.astype(f32)) / GLA_NORMALIZER).reshape(B, S, N_DIRS, N_HEADS, GLA_DK)
    qc, kc, vc = heads(l_q, GLA_DK), heads(l_k, GLA_DK), heads(l_v, HEAD_DIM)
    oc = gla_chunked(qc, kc, vc, gk[:, :, 0]) + rev(gla_chunked(rev(qc), rev(kc), rev(vc), rev(gk[:, :, 1])))
    yc = (rms_norm(oc, gla_out_norm_g) * jax.nn.silu(heads(l_gate, HEAD_DIM))).reshape(B, S, BRANCH_W)

    n_mem = mem.shape[1]
    kv = jnp.einsum('bmd,de->bme', rms_norm(mem, mem_norm_g), w_mem_kv)
    km, vm = jnp.split(kv, 2, axis=-1)
    km = km.reshape(B, n_mem, N_HEADS, HEAD_DIM)
    vm = vm.reshape(B, n_mem, N_HEADS, HEAD_DIM)
    ym = block_attention(heads(m_q, HEAD_DIM), km, vm) * jax.nn.silu(m_gate)

    y = jnp.einsum('bse,ed->bsd', jnp.concatenate([ya, yb, yc, ym], axis=-1), w_out)
    return h + y


def encoder_trunk(x, mem, weights, final_norm_g):
    cos, sin = axial_rope_tables(x.shape[1])
    h = x
    for d in range(DEPTH):
        h = encoder_layer(h, mem, cos, sin, *[w[d] for w in weights])
    return rms_norm(h, final_norm_g)


def setup_inputs(seed: int = 0) -> dict:
    key = jax.random.key(seed)
    ks = jax.random.split(key, 24)
    nrm = jax.random.normal
    f32 = jnp.float32
    dt = jnp.exp(jax.random.uniform(ks[10], (DEPTH, N_DIRS, N_HEADS), f32, np.log(1e-3), np.log(1e-1)))
    return {
        'x_prompt': nrm(ks[0], (BATCH, SEQ, D_MODEL), f32),
        'x_sample': nrm(ks[1], (DEC_BATCH, DEC_SEQ, D_MODEL), f32),
        'mem_prompt': nrm(ks[2], (BATCH, MEM_LEN, D_MODEL), f32),
        'mem_sample': nrm(ks[3], (DEC_BATCH, MEM_LEN, D_MODEL), f32),
        'norm_g': 1.0 + 0.02 * nrm(ks[4], (DEPTH, D_MODEL), f32),
        'w_in': nrm(ks[5], (DEPTH, D_MODEL, IN_COLS), f32) * D_MODEL ** -0.5,
        'att_q_norm_g': 1.0 + 0.02 * nrm(ks[6], (DEPTH, HEAD_DIM), f32),
        'att_k_norm_g': 1.0 + 0.02 * nrm(ks[7], (DEPTH, HEAD_DIM), f32),
        'gdn_conv_w': nrm(ks[8], (DEPTH, CONV_K, 3 * N_HEADS * HEAD_DIM), f32) * CONV_K ** -0.5,
        'gdn_a_log': jnp.log(jax.random.uniform(ks[9], (DEPTH, N_DIRS, N_HEADS), f32, 1.0, 16.0)),
        'gdn_dt_bias': dt + jnp.log(-jnp.expm1(-dt)),
        'gdn_out_norm_g': 1.0 + 0.02 * nrm(ks[11], (DEPTH, HEAD_DIM), f32),
        'gla_w_gate_up': nrm(ks[12], (DEPTH, N_DIRS, GLA_RANK, N_HEADS * GLA_DK), f32) * GLA_RANK ** -0.5,
        'gla_b_gate': 0.1 * nrm(ks[13], (DEPTH, N_DIRS, N_HEADS * GLA_DK), f32),
        'gla_out_norm_g': 1.0 + 0.02 * nrm(ks[14], (DEPTH, HEAD_DIM), f32),
        'mem_norm_g': 1.0 + 0.02 * nrm(ks[15], (DEPTH, D_MODEL), f32),
        'w_mem_kv': nrm(ks[16], (DEPTH, D_MODEL, 2 * N_HEADS * HEAD_DIM), f32) * D_MODEL ** -0.5,
        'w_out': nrm(ks[17], (DEPTH, N_GROUPS * BRANCH_W, D_MODEL), f32) * (N_GROUPS * BRANCH_W) ** -0.5 * (2 * DEPTH) ** -0.5,
        'final_norm_g': 1.0 + 0.02 * nrm(ks[18], (D_MODEL,), f32),
    }


def reference(x_prompt, x_sample, mem_prompt, mem_sample, norm_g, w_in, att_q_norm_g, att_k_norm_g,
              gdn_conv_w, gdn_a_log, gdn_dt_bias, gdn_out_norm_g, gla_w_gate_up, gla_b_gate,
              gla_out_norm_g, mem_norm_g, w_mem_kv, w_out, final_norm_g):
    weights = (norm_g, w_in, att_q_norm_g, att_k_norm_g, gdn_conv_w, gdn_a_log, gdn_dt_bias,
               gdn_out_norm_g, gla_w_gate_up, gla_b_gate, gla_out_norm_g, mem_norm_g, w_mem_kv, w_out)
    y_prompt = encoder_trunk(x_prompt, mem_prompt, weights, final_norm_g)
    y_sample = encoder_trunk(x_sample, mem_sample, weights, final_norm_g)
    return (y_prompt, y_sample)
```

```python
from contextlib import ExitStack
import numpy as np
import concourse.bass as bass
import concourse.mybir as mybir
from concourse.bass_utils import run_bass_kernel_spmd

F32 = mybir.dt.float32
BF16 = mybir.dt.bfloat16
AF = mybir.ActivationFunctionType
ALU = mybir.AluOpType
AX = mybir.AxisListType

D = 1024
NCOL = 3120
EPS = 1e-6
NB = 2560
NF = 904
B_GA, B_GM, B_GD, B_GL, B_MQ, B_GQ, B_GK, B_GV, B_LV, B_Q = 0, 256, 512, 768, 1024, 1280, 1536, 1792, 2048, 2304
F_BG, F_LQ, F_LK, F_NGK, F_OB, F_OC = 0, 8, 136, 264, 392, 648
P_GQK, P_ALOG, P_DTB, P_GOG, P_GLG, P_GLB = 0, 384, 392, 400, 656, 912
NPV = 1168
C_ID, C_LE, C_LT, C_GE, C_GT, C_BD, C_SEL0, C_SEL1 = [128 * k for k in range(8)]
C_RM4 = 128 * 8
C_BD4 = C_RM4 + 4
C_ONE = C_BD4 + 256
NCST = C_ONE + 1


class _Op:
    __slots__ = ("eng", "idx", "fn", "waits", "signal", "kind", "dma_m", "tok")

    def __init__(self, eng, idx, fn, kind):
        self.eng = eng
        self.idx = idx
        self.fn = fn
        self.kind = kind
        self.waits = []
        self.signal = False
        self.dma_m = None
        self.tok = None


class Sched:
    ENGS = ("pe", "act", "dve", "pool", "sp")

    def __init__(self, nc, R=8, epoch=16000):
        self.nc = nc
        self.R = R
        self.epoch = epoch
        self.streams = {e: [] for e in self.ENGS}
        self.ndma = {e: 0 for e in self.ENGS}
        self.dma_ops = {e: [] for e in self.ENGS}
        self.last_w = {}
        self.readers = {}
        self.waited = {e: {p: -1 for p in self.ENGS} for e in self.ENGS}
        self.waited_dma = {e: {} for e in self.ENGS}

    def capture(self):
        self._cap = []

    def end_capture(self):
        lst = self._cap
        self._cap = None
        return lst

    def merge(self, lists):
        pos = [0] * len(lists)
        while True:
            best, bf = -1, 2.0
            for j, l in enumerate(lists):
                if pos[j] < len(l):
                    f = pos[j] / len(l)
                    if f < bf:
                        best, bf = j, f
            if best < 0:
                break
            self._add(*lists[best][pos[best]])
            pos[best] += 1

    def _add(self, eng, fn, reads, writes, kind):
        if getattr(self, "_cap", None) is not None:
            self._cap.append((eng, fn, list(reads), list(writes), kind))
            return None
        self.total = getattr(self, "total", 0) + 1
        if getattr(self, "desc", None) is not None:
            import sys as _sys
            fr = _sys._getframe(3)
            self.desc.append((self.total, eng, kind, fr.f_lineno, fr.f_back.f_lineno))
        if self.total > getattr(self, "limit", 1 << 60) or self.total in getattr(self, "skip", ()):
            return None
        excl = getattr(self, "excl", ())
        if excl and eng != "pe":
            extra = [k for k in reads if k in excl and k not in writes]
            if extra:
                writes = list(writes) + extra
        st = self.streams[eng]
        op = _Op(eng, len(st), fn, kind)
        st.append(op)
        deps = []
        for k in reads:
            w = self.last_w.get(k)
            if w is not None:
                deps.append((w, True))
        for k in writes:
            w = self.last_w.get(k)
            if w is not None:
                deps.append((w, False))
            for r in self.readers.get(k, {}).values():
                deps.append((r, False))
        for (p, raw) in deps:
            if p is op:
                continue
            if p.kind == "c":
                if p.eng == eng:
                    if eng == "pe":
                        continue
                if self.waited[eng][p.eng] >= p.idx:
                    continue
                self.waited[eng][p.eng] = p.idx
                p.signal = True
                op.waits.append(p)
            else:
                key = (p.eng, p.dma_m % self.R)
                if self.waited_dma[eng].get(key, -1) >= p.dma_m:
                    continue
                self.waited_dma[eng][key] = p.dma_m
                op.waits.append(p)
        if kind == "d":
            m = self.ndma[eng]
            self.ndma[eng] = m + 1
            op.dma_m = m
            self.dma_ops[eng].append(op)
            if m >= self.R:
                prev = self.dma_ops[eng][m - self.R]
                key = (eng, m % self.R)
                if self.waited_dma[eng].get(key, -1) < prev.dma_m:
                    self.waited_dma[eng][key] = prev.dma_m
                    op.waits.append(prev)
        for k in writes:
            self.last_w[k] = op
            self.readers[k] = {}
        for k in reads:
            self.readers.setdefault(k, {})[eng] = op
        return op

    def op(self, eng, fn, reads=(), writes=()):
        return self._add(eng, fn, reads, writes, "c")

    def dma(self, eng, fn, reads=(), writes=()):
        return self._add(eng, fn, reads, writes, "d")

    def emit(self, stack):
        nc = self.nc
        for e in self.ENGS:
            n = self.ndma[e]
            if n:
                last = self.dma_ops[e][max(0, n - self.R):]
                f = _Op(e, len(self.streams[e]), None, "f")
                f.waits = list(last)
                self.streams[e].append(f)
        csem = {}
        for e in self.ENGS:
            cnt = 0
            for op in self.streams[e]:
                if op.kind == "c" and op.signal:
                    ep = cnt // self.epoch
                    if (e, ep) not in csem:
                        csem[(e, ep)] = stack.enter_context(nc.semaphore(f"c_{e}_{ep}"))
                    op.tok = (csem[(e, ep)], cnt % self.epoch + 1)
                    cnt += 1
        dsem = {}
        for e in self.ENGS:
            for op in self.dma_ops[e]:
                s = op.dma_m % self.R
                if (e, s) not in dsem:
                    dsem[(e, s)] = stack.enter_context(nc.semaphore(f"d_{e}_{s}"))
                op.tok = (dsem[(e, s)], 16 * (op.dma_m // self.R + 1))
        block = stack.enter_context(nc.Block())

        def replay(e, eng):
            for op in self.streams[e]:
                for p in op.waits:
                    eng.wait_ge(p.tok[0], p.tok[1])
                if op.fn is None:
                    continue
                ins = op.fn(eng)
                if op.kind == "d":
                    ins.then_inc(op.tok[0], 16)
                elif op.signal:
                    ins.then_inc(op.tok[0], 1)

        if self.streams["pe"]:
            @block.tensor
            def _(eng):
                replay("pe", eng)
        if self.streams["act"]:
            @block.scalar
            def _(eng):
                replay("act", eng)
        if self.streams["dve"]:
            @block.vector
            def _(eng):
                replay("dve", eng)
        if self.streams["pool"]:
            @block.gpsimd
            def _(eng):
                replay("pool", eng)
        if self.streams["sp"]:
            @block.sync
            def _(eng):
                replay("sp", eng)


class V:
    __slots__ = ("ap", "key")

    def __init__(self, ap, key):
        self.ap = ap
        self.key = key

    def __getitem__(self, idx):
        return V(self.ap[idx], self.key)

    def r(self, pat, **kw):
        return V(self.ap.rearrange(pat, **kw), self.key)

    def bc(self, axis, shape):
        return V(self.ap.unsqueeze(axis).to_broadcast(list(shape)), self.key)


class TT:
    def __init__(self, t, key):
        self.t = t
        self.key = key

    def __getitem__(self, idx):
        return V(self.t[idx], self.key)

    def k(self, sub):
        return TT(self.t, (self.key, sub))


def _build(seqs, depth, final=True):
    nc = bass.Bass("TRN2", target_bir_lowering=False)
    NT = sum(seqs)
    NS = len(seqs)
    SMAX = max(seqs)
    roff = [sum(seqs[:i]) for i in range(NS)]

    def din(name, shape, dt=F32):
        return nc.dram_tensor(name, list(shape), dt, kind="ExternalInput").ap()

    x_d = TT(din("x", [NT, D]), "x")
    mem_d = TT(din("mem", [NS * 256, D]), "mem")
    win_d = din("w_in", [depth, D, NCOL])
    wout_d = din("w_out", [depth, D, D])
    wmem_d = din("w_mem", [depth, D, 512])
    ng_d = din("norm_g", [depth, D])
    mg_d = din("mem_g", [depth, D])
    pv_d = din("pvec", [depth, NPV])
    wup_d = din("wup", [depth, 32, 256])
    fng_d = din("fng", [D])
    cst_d = din("cst", [128, NCST])
    cstb_d = din("cstb", [128, 640])
    cw_d = din("convw", [depth, 2304])
    rc_d = TT(din("ropec", [SMAX, 384]), "ropec")
    rs_d = TT(din("ropes", [SMAX, 384]), "ropes")
    y_d = TT(nc.dram_tensor("y", [NT, D], F32, kind="ExternalOutput").ap(), "y")
    hbuf = TT(nc.dram_tensor("hbuf", [NT, D], F32, kind="Internal").ap(), "hbuf")
    stb = TT(nc.dram_tensor("stashb", [NT, NB], BF16, kind="Internal").ap(), "stb")
    stf = TT(nc.dram_tensor("stashf", [NT, NF], F32, kind="Internal").ap(), "stf")

    with ExitStack() as st:
        def sb(name, shape, dt=F32):
            return TT(st.enter_context(nc.sbuf_tensor("s_" + name, list(shape), dt)), name)

        def ps(name, shape, dt=F32):
            return TT(st.enter_context(nc.psum_tensor("ps_" + name, list(shape), dt)), name)

        S = Sched(nc)
        import os as _os
        if _os.environ.get("KLIMIT"):
            S.limit = int(_os.environ["KLIMIT"])
        if _os.environ.get("KDESC"):
            S.desc = []
        if _os.environ.get("KSKIP"):
            S.skip = set(int(v) for v in _os.environ["KSKIP"].split(","))

        def keys(*vs):
            return [v.key for v in vs if isinstance(v, V)]

        def apof(v):
            return v.ap if isinstance(v, V) else v

        def act(out, in_, func, scale=1.0, bias=0.0, accum=None):
            kw = dict(out=out.ap, in_=in_.ap, func=func, scale=apof(scale), bias=apof(bias))
            w = [out.key]
            if accum is not None:
                kw["accum_out"] = accum.ap
                w.append(accum.key)
            S.op("act", lambda e, kw=kw: e.activation(**kw), reads=keys(in_, scale, bias), writes=w)

        def ts(out, in0, s1, s2=None, op0=ALU.mult, op1=None, eng="dve"):
            kw = dict(out=out.ap, in0=in0.ap, scalar1=apof(s1), scalar2=apof(s2), op0=op0)
            if op1 is not None:
                kw["op1"] = op1
            S.op(eng, lambda e, kw=kw: e.tensor_scalar(**kw), reads=keys(in0, s1, s2), writes=[out.key])

        def tt(out, in0, in1, op=ALU.mult, eng="dve"):
            S.op(eng, lambda e: e.tensor_tensor(out=out.ap, in0=in0.ap, in1=in1.ap, op=op),
                 reads=keys(in0, in1), writes=[out.key])

        def stt(out, in0, scalar, in1, op0, op1, eng="dve"):
            S.op(eng, lambda e: e.scalar_tensor_tensor(out=out.ap, in0=in0.ap, scalar=apof(scalar), in1=in1.ap,
                                                       op0=op0, op1=op1),
                 reads=keys(in0, scalar, in1), writes=[out.key])

        def cp(out, in_, eng="dve"):
            if eng == "act":
                S.op("act", lambda e: e.copy(out=out.ap, in_=in_.ap), reads=[in_.key], writes=[out.key])
            else:
                S.op(eng, lambda e: e.tensor_copy(out=out.ap, in_=in_.ap), reads=[in_.key], writes=[out.key])

        def red(out, in_, op=ALU.add):
            S.op("dve", lambda e: e.tensor_reduce(out=out.ap, in_=in_.ap, axis=AX.X, op=op),
                 reads=[in_.key], writes=[out.key])

        def recip(out, in_):
            S.op("dve", lambda e: e.reciprocal(out=out.ap, in_=in_.ap), reads=[in_.key], writes=[out.key])

        def memset(out, val, eng="pool"):
            S.op(eng, lambda e: e.memset(out.ap, val), writes=[out.key])

        def mm(out, lhsT, rhs, start=True, stop=True):
            S.op("pe", lambda e: e.matmul(out.ap, lhsT=lhsT.ap, rhs=rhs.ap, start=start, stop=stop),
                 reads=keys(lhsT, rhs), writes=[out.key])

        def tr(out, in_, ident):
            S.op("pe", lambda e: e.transpose(out=out.ap, in_=in_.ap, identity=ident.ap),
                 reads=keys(in_, ident), writes=[out.key])

        def dma(eng, out, in_, slow=False):
            if slow:
                S.dma(eng, lambda e: e.dma_start(out=out.ap, in_=in_.ap, allow_slow_non_contiguous=True),
                      reads=[in_.key], writes=[out.key])
            else:
                S.dma(eng, lambda e: e.dma_start(out=out.ap, in_=in_.ap), reads=[in_.key], writes=[out.key])

        def rsqrt_(out, in_, scale, eps):
            act(out, in_, AF.Ln, scale=scale, bias=eps)
            act(out, out, AF.Exp, scale=-0.5)

        cst = sb("cst", [128, NCST])
        cstb = sb("cstb", [128, 5 * 128], BF16)
        win = sb("win", [128, 8, NCOL], BF16)
        wout = sb("wout", [128, 8, D], BF16)
        wmem = sb("wmem", [128, 8, 512], BF16)
        ngt = sb("ngt", [128, 8])
        mgt = sb("mgt", [128, 8])
        pv = sb("pv", [128, NPV])
        wup = sb("wup", [32, 256])
        negA = sb("negA", [128, 8])
        KT = sb("KT", [128, SMAX], BF16)
        qm = sb("qm", [128, 4, 128], BF16)
        Vaug = sb("Vaug", [128, SMAX // 128, 2, 65], BF16)
        kmT = sb("kmT", [128, 2, 2, 256], BF16)
        Vmaug = sb("Vmaug", [128, 2, 4, 65], BF16)
        hin = sb("hin", [128, D])
        ssq = sb("ssq", [128, 1])
        rstd = sb("rstd", [128, 1])
        xn = sb("xn", [128, D], BF16)
        xnT = sb("xnT", [128, D], BF16)
        rc = sb("rc", [128, 384])
        rs = sb("rs", [128, 384])
        ss8 = sb("ss8", [128, 8])
        rn8 = sb("rn8", [128, 8])
        qkrot = sb("qkrot", [128, 384], BF16)
        we = sb("we", [128, 768])
        xw = [sb(f"xw{k}", [128, 3, 768], BF16) for k in range(3)]
        lg2 = sb("lg2", [128, 16])
        bgf = [sb(f"bgf{k}", [128, 8]) for k in range(2)]
        ngkf = [sb(f"ngkf{k}", [128, 128]) for k in range(2)]
        low = sb("low", [128, 32])
        lowT = sb("lowT", [32, 128])
        gl = sb("gl", [128, 256])
        Tb = [sb(f"Tb{k}", [128, NB], BF16) for k in range(2)]
        Tf = [sb(f"Tf{k}", [128, NF]) for k in range(2)]
        wqB = sb("wqB", [128, 384])
        ss8b = sb("ss8b", [128, 8])
        rn8b = sb("rn8b", [128, 8])
        convw = sb("convw", [128, 2304], BF16)
        lgr = [sb(f"lgr{k}", [128, 16]) for k in range(2)]
        cv = sb("cv", [128, 768])
        gDN = sb("gDN", [128, 4, 128])
        gDA = sb("gDA", [128, 4, 128], BF16)
        gDNi = sb("gDNi", [128, 4, 128], BF16)
        gsm = sb("gsm", [128, 24])
        gcd = sb("gcd", [128, 2, 4])
        gN1 = sb("gN", [128, 4, 128], BF16)
        gG = sb("gG", [128, 4, 128])
        gA1 = sb("gA", [128, 4, 128], BF16)
        gP1 = sb("gP", [128, 4, 128], BF16)
        gN = [gN1, gN1]
        gA = [gA1, gA1]
        gP = [gP1, gP1]
        gkT = sb("gkT", [128, 2, 128], BF16)
        gkTm = sb("gkTm", [128, 2, 2, 128], BF16)
        gkdm = sb("gkdm", [128, 2, 256], BF16)
        gsm2 = sb("gsm2", [128, 8])
        lkdm = sb("lkdm", [128, 2, 128], BF16)
        gqT = sb("gqT", [128, 2, 128], BF16)
        gvk = sb("gvk", [128, 4, 128], BF16)
        guw = sb("guw", [128, 4, 128])
        gwb = sb("gwb", [128, 256], BF16)
        gwT = sb("gwT", [128, 2, 128], BF16)
        gqd = sb("gqd", [128, 256], BF16)
        gqdT = sb("gqdT", [128, 2, 128], BF16)
        gkd = sb("gkd", [128, 256], BF16)
        gqkm = sb("gqkm", [128, 4, 128], BF16)
        gvn = sb("gvn", [128, 256], BF16)
        gS = [[sb(f"gS{d}{p}", [128, 128]) for p in range(2)] for d in range(2)]
        gSb = [[sb(f"gSb{d}{p}", [128, 128], BF16) for p in range(2)] for d in range(2)]
        obcs = [sb(f"obc{k}", [128, 512]) for k in range(2)]
        ymbs = [sb(f"ymb{k}", [128, 256], BF16) for k in range(2)]
        ltmp = sb("ltmp", [128, 256])
        lex = sb("lex", [128, 3, 128])
        lcd = sb("lcd", [128, 2])
        lqe = sb("lqe", [128, 128], BF16)
        lke = sb("lke", [128, 128], BF16)
        lkd = sb("lkd", [128, 128], BF16)
        lqeT = sb("lqeT", [128, 128], BF16)
        lkeT = sb("lkeT", [128, 128], BF16)
        lqblk = sb("lqblk", [128, 4, 128], BF16)
        latt = sb("latt", [128, 4, 128], BF16)
        lS = [sb(f"lS{d}", [128, 256]) for d in range(2)]
        lSb = [sb(f"lSb{d}", [128, 256], BF16) for d in range(2)]
        os2 = sb("os2", [128, 512])
        osum = TT(os2.t[:, 0:256], "os2")
        sq = we
        ycat = xn
        ycatT = xnT
        junk = xn
        xc = cv
        wst = hin
        hout = hin
        pT = sb("pT", [128, 1024], BF16)
        wA = TT(pT.t[:, :].bitcast(F32), "pT")
        mqT = sb("mqT", [128, 2, 128], BF16)
        rs4 = sb("rs4", [128, 4])
        ya = osum
        kmtok = TT(qkrot.t[:, 0:256], "qkrot")

        ptr = ps("ptr", [128, 1024], BF16)
        pa = ps("pa", [128, 512])
        pb = ps("pb", [128, 512])
        p2 = ps("p2", [128, 512])
        p3 = ps("p3", [128, 512])
        sc = ps("sc", [128, 1024])
        p6 = ps("p6", [128, 512])
        sc0 = TT(sc.t[:, 0:512], "sc0")
        sc1 = TT(sc.t[:, 512:1024], "sc1")
        S.excl = {"ptr", "pa", "pb", "p2", "p3", "sc0", "sc1", "p6"}
        p6b = TT(p6.t[:, :].bitcast(BF16), "p6")
        rs4a = sb("rs4a", [128, 4])
        yaa = sb("yaa", [128, 256])

        ident = cst[:, C_ID:C_ID + 128]
        identb = cstb[:, 0:128]
        SHPb, SHNb, HPb, HNb = [cstb[:, 128 * k:128 * (k + 1)] for k in range(1, 5)]
        mLE, mLT, mGE, mGT, mBD = [cst[:, c:c + 128] for c in (C_LE, C_LT, C_GE, C_GT, C_BD)]
        SEL = [cst[:, C_SEL0:C_SEL0 + 128], cst[:, C_SEL1:C_SEL1 + 128]]
        RM4 = cst[:, C_RM4:C_RM4 + 4]
        BD4 = cst[:, C_BD4:C_BD4 + 256]
        ONE = cst[:, C_ONE:C_ONE + 1]

        dma("sp", cst[:, :], V(cst_d, "cst_d"))
        dma("sp", hin[:, 0:640], V(cstb_d, "cstb_d"))
        cp(cstb[:, :], hin[:, 0:640], eng="pool")
        memset(Vaug[:, :, :, :], 1.0)
        memset(Vmaug[:, :, :, :], 1.0)
        memset(gvn[:, :], 0.0)

        def rms_to_xnT(src, pt=None):
            pt = ptr if pt is None else pt
            act(junk[:, :], src, AF.Square, accum=ssq[:, :])
            rsqrt_(rstd[:, :], ssq[:, :], 1.0 / D, EPS)
            ts(xn[:, :], src, rstd[:, :], None, ALU.mult)
            for c in range(8):
                tr(pt[:, c * 128:(c + 1) * 128], xn[:, c * 128:(c + 1) * 128], identb)
            cp(xnT[:, :], pt[:, :], eng="act")

        def sigmoid_from(dst_e, src, n):
            act(dst_e, src, AF.Exp, scale=-1.0)
            ts(dst_e, dst_e, 1.0, None, ALU.add)
            recip(dst_e, dst_e)

        def gdn_core(d, qn, kn, vv, beta, g, o_out, first):
            M1, GS, NSm, AI, NIm = (mLE, mGT, mGT, mLE, mGE) if d == 0 else (mGE, mLT, mLT, mGE, mLE)
            if first:
                for p in range(2):
                    memset(gS[d][p][:, :], 0.0)
                    memset(gSb[d][p][:, :], 0.0)
            for p in range(2):
                tr(ptr[:, p * 128:(p + 1) * 128], kn[:, p * 128:(p + 1) * 128], identb)
                tr(ptr[:, 256 + p * 128:256 + (p + 1) * 128], qn[:, p * 128:(p + 1) * 128], identb)
            cp(gkT[:, :, :], ptr[:, 0:256].r("p (a b) -> p a b", a=2), eng="act")
            for q in range(2):
                ts(gkTm[:, :, q, :], ptr[:, 0:256].r("p (a b) -> p a b", a=2), mBD[:, 64 * q:64 * q + 1], None, ALU.mult)
            cp(gqT[:, :, :], ptr[:, 256:512].r("p (a b) -> p a b", a=2), eng="act")
            tt(gG[:, :, :], GS.bc(1, [128, 4, 128]), g.bc(2, [128, 4, 128]), ALU.mult)
            mm(p2[:, :], M1, gG[:, :, :].r("p a b -> p (a b)"))
            act(gDN[:, :, :].r("p a b -> p (a b)"), p2[:, :], AF.Exp)
            tt(gDNi[:, :, :], gDN[:, :, :], NIm.bc(1, [128, 4, 128]), ALU.mult, eng="pool")
            tt(gDN[:, :, :], gDN[:, :, :], NSm.bc(1, [128, 4, 128]), ALU.mult)
            for h in range(4):
                tr(ptr[:, h * 128:(h + 1) * 128], gDNi[:, h, :], identb)
            cp(gDA[:, :, :].r("p a b -> p (a b)"), ptr[:, 0:512], eng="act")
            mm(pa[:, 0:4], M1, g)
            mm(pa[:, 4:8], mBD, g)
            mm(pa[:, 8:12], SEL[0], g)
            mm(pa[:, 12:16], SEL[1], g)
            cp(gsm[:, 0:4], pa[:, 0:4])
            act(gsm[:, 4:8], pa[:, 0:4], AF.Exp)
            tt(gsm[:, 8:12], pa[:, 4:8], gsm[:, 0:4], ALU.subtract)
            act(gsm[:, 8:12], gsm[:, 8:12], AF.Exp)
            act(gcd[:, :, :].r("p a b -> p (a b)"), pa[:, 8:16], AF.Exp)
            ts(gsm[:, 12:16], beta, -1.0, None, ALU.mult)
            tt(gsm[:, 16:20], beta, gsm[:, 4:8], ALU.mult)
            for h in range(4):
                pr, hb = h // 2, (h % 2) * 64
                mm(p2[:, h * 128:(h + 1) * 128], gkT[:, pr, :], gkTm[:, pr, h % 2, :])
            for h in range(4):
                stt(gN[0][:, h, :], p2[:, h * 128:(h + 1) * 128], gsm[:, 12 + h:13 + h], gDN[:, h, :],
                    ALU.mult, ALU.mult)
            for h in range(4):
                tr(ptr[:, h * 128:(h + 1) * 128], gN[0][:, h, :], identb)
            cp(gA[0][:, :, :].r("p a b -> p (a b)"), ptr[:, 0:512], eng="act")
            tt(gP[0][:, :, :], gA[0][:, :, :], ident.bc(1, [128, 4, 128]), ALU.add)
            cur = 0
            pc = 0
            for lvl in range(5):
                nxt = 1 - cur
                last = lvl == 4
                for h in range(4):
                    mm(p2[:, h * 128:(h + 1) * 128], gA[cur][:, h, :], gN[cur][:, h, :])
                if not last:
                    for h in range(4):
                        mm(p3[:, h * 128:(h + 1) * 128], gN[cur][:, h, :], gA[cur][:, h, :])
                cp(gN[nxt][:, :, :].r("p a b -> p (a b)"), p2[:, :], eng="dve")
                if not last:
                    cp(gA[nxt][:, :, :].r("p a b -> p (a b)"), p3[:, :], eng="act")
                for h in range(4):
                    mm(pa[:, h * 128:(h + 1) * 128], gN[nxt][:, h, :], gP[pc][:, h, :])
                tt(gP[1 - pc][:, :, :].r("p a b -> p (a b)"), pa[:, :], gP[pc][:, :, :].r("p a b -> p (a b)"),
                   ALU.add)
                pc = 1 - pc
                cur = nxt
            Pm = gP[pc]
            tt(gvk[:, :, 0:64], vv.r("p (a b) -> p a b", a=4), beta.bc(2, [128, 4, 64]), ALU.mult, eng="pool")
            tt(gvk[:, :, 64:128], kn.r("p (a b) -> p a b", a=4), gsm[:, 16:20].bc(2, [128, 4, 64]), ALU.mult,
               eng="pool")
            for h in range(4):
                mm(p3[:, h * 128:(h + 1) * 128], Pm[:, h, :], gvk[:, h, :])
            cp(guw[:, :, :].r("p a b -> p (a b)"), p3[:, :], eng="act")
            cp(gwb[:, :].r("p (a b) -> p a b", a=4), guw[:, :, 64:128], eng="dve")
            for p in range(2):
                tr(ptr[:, 512 + p * 128:512 + (p + 1) * 128], gwb[:, p * 128:(p + 1) * 128], identb)
            cp(gwT[:, :, :], ptr[:, 512:768].r("p (a b) -> p a b", a=2), eng="act")
            tt(gqd[:, :].r("p (a b) -> p a b", a=4), qn.r("p (a b) -> p a b", a=4),
               gsm[:, 4:8].bc(2, [128, 4, 64]), ALU.mult, eng="pool")
            for p in range(2):
                tr(ptr[:, p * 128:(p + 1) * 128], gqd[:, p * 128:(p + 1) * 128], identb)
            cp(gqdT[:, :, :], ptr[:, 0:256].r("p (a b) -> p a b", a=2), eng="act")
            for c in range(2):
                ts(gsm2[:, 4 * c:4 * c + 4], gsm[:, 8:12], mBD[:, 64 * c:64 * c + 1], None, ALU.mult)
                tt(gkdm[:, c, :].r("p (a b) -> p a b", a=4), kn.r("p (a b) -> p a b", a=4),
                   gsm2[:, 4 * c:4 * c + 4].bc(2, [128, 4, 64]), ALU.mult, eng="pool")
            for h in range(4):
                pr, hb = h // 2, (h % 2) * 64
                mm(p2[:, h * 128:(h + 1) * 128], gkTm[:, pr, h % 2, :], gqT[:, pr, :])
            tt(gqkm[:, :, :].r("p a b -> p (a b)"), p2[:, :], gDA[:, :, :].r("p a b -> p (a b)"), ALU.mult)
            for c in ((0, 1) if d == 0 else (1, 0)):
                r0, r1 = 64 * c, 64 * c + 64
                for p in range(2):
                    mm(p3[:, p * 128:(p + 1) * 128], gwT[:, p, :], gSb[d][p][:, :])
                tt(gvn[r0:r1, :].r("p (a b) -> p a b", a=4), guw[r0:r1, :, 0:64],
                   p3[r0:r1, 0:256].r("p (a b) -> p a b", a=4), ALU.subtract)
                for h in range(4):
                    pr, hb = h // 2, (h % 2) * 64
                    mm(pa[:, h * 64:(h + 1) * 64], gqdT[:, pr, :], gSb[d][pr][:, hb:hb + 64], start=True, stop=False)
                    mm(pa[:, h * 64:(h + 1) * 64], gqkm[:, h, :], gvn[:, h * 64:(h + 1) * 64],
                       start=False, stop=True)
                cp(o_out[r0:r1, :], pa[r0:r1, 0:256], eng="act")
                for p in range(2):
                    mm(p2[:, p * 128:(p + 1) * 128], gkdm[:, c, p * 128:(p + 1) * 128],
                       gvn[:, p * 128:(p + 1) * 128])
                for p in range(2):
                    for q in range(2):
                        h = 2 * p + q
                        b0, b1 = 64 * q, 64 * q + 64
                        stt(gS[d][p][b0:b1, b0:b1], gS[d][p][b0:b1, b0:b1], gcd[b0:b1, c, h:h + 1],
                            p2[b0:b1, p * 128 + b0:p * 128 + b1], ALU.mult, ALU.add)
                    cp(gSb[d][p][:, :], gS[d][p][:, :], eng="dve")

        def gla_core(d, lq, lk, lv, ngk, o_out, first):
            M1, GS, AI = (mLE, mGT, mLE) if d == 0 else (mGE, mLT, mGE)
            if first:
                memset(lS[d][:, :], 0.0)
                memset(lSb[d][:, :], 0.0)
            mm(sc1[:, 0:128], M1, ngk)
            mm(sc1[:, 128:256], GS, ngk)
            for c in range(2):
                mm(sc1[:, 256 + c:257 + c], ngk, mBD[:, 64 * c:64 * c + 1])
            act(lex[:, 0, :], sc1[:, 0:128], AF.Exp, scale=-1.0 / 16)
            act(lex[:, 1, :], sc1[:, 0:128], AF.Exp, scale=1.0 / 16)
            act(lex[:, 2, :], sc1[:, 128:256], AF.Exp, scale=-1.0 / 16)
            act(lcd[:, :], sc1[:, 256:258], AF.Exp, scale=-1.0 / 16)
            tt(lqe[:, :], lq, lex[:, 0, :], ALU.mult)
            tt(lke[:, :], lk, lex[:, 1, :], ALU.mult, eng="pool")
            tt(lkd[:, :], lk, lex[:, 2, :], ALU.mult, eng="pool")
            for c in range(2):
                ts(lkdm[:, c, :], lkd[:, :], mBD[:, 64 * c:64 * c + 1], None, ALU.mult)
            tr(ptr[:, 768:896], lqe[:, :], identb)
            tr(ptr[:, 896:1024], lke[:, :], identb)
            cp(lqeT[:, :], ptr[:, 768:896], eng="act")
            cp(lkeT[:, :], ptr[:, 896:1024], eng="act")
            tt(lqblk[:, :, :], ptr[:, 768:896].bc(1, [128, 4, 128]), RM4.bc(2, [128, 4, 128]), ALU.mult)
            mm(pb[:, :], lkeT[:, :], lqblk[:, :, :].r("p a b -> p (a b)"))
            tt(latt[:, :, :], pb[:, :].r("p (a b) -> p a b", a=4), AI.bc(1, [128, 4, 128]), ALU.mult)
            for c in ((0, 1) if d == 0 else (1, 0)):
                r0, r1 = 64 * c, 64 * c + 64
                for h in range(4):
                    mm(sc1[:, h * 64:(h + 1) * 64], lqeT[:, :], lSb[d][:, h * 64:(h + 1) * 64], start=True, stop=False)
                    mm(sc1[:, h * 64:(h + 1) * 64], latt[:, h, :], lv[:, h * 64:(h + 1) * 64],
                       start=False, stop=True)
                cp(o_out[r0:r1, :], sc1[r0:r1, 0:256], eng="act")
                mm(sc1[:, 256:512], lkdm[:, c, :], lv)
                tt(ltmp[:, :], sc1[:, 256:512], BD4, ALU.mult)
                stt(lS[d][:, :], lS[d][:, :], lcd[:, c:c + 1], ltmp[:, :], ALU.mult, ALU.add)
                cp(lSb[d][:, :], lS[d][:, :], eng="dve")

        for l in range(depth):
            last_layer = (l == depth - 1)
            dma("sp", ngt[:, :], V(ng_d[l].rearrange("(c p) -> p c", p=128), "ng_d"), slow=True)
            dma("sp", mgt[:, :], V(mg_d[l].rearrange("(c p) -> p c", p=128), "mg_d"), slow=True)
            dma("sp", pv[:, :], V(pv_d[l].partition_broadcast(128), "pv_d"))
            dma("sp", wup[:, :], V(wup_d[l], "wup_d"))
            for c in range(8):
                for (j0, j1) in ((0, 1024), (1024, 2048), (2048, 3072), (3072, 3120)):
                    dma("sp", wst[:, 0:j1 - j0], V(win_d[l, c * 128:(c + 1) * 128, j0:j1], "win_d"))
                    ts(win[:, c, j0:j1], wst[:, 0:j1 - j0], ngt[:, c:c + 1], None, ALU.mult)
                dma("sp", wst[:, 0:1024], V(wout_d[l, c * 128:(c + 1) * 128, :], "wout_d"))
                cp(wout[:, c, :], wst[:, 0:1024], eng="pool")
                dma("sp", wst[:, 0:512], V(wmem_d[l, c * 128:(c + 1) * 128, :], "wmem_d"))
                ts(wmem[:, c, :], wst[:, 0:512], mgt[:, c:c + 1], None, ALU.mult)
            for j in range(3):
                dma("sp", wst[:, 0:768], V(cw_d[l, j * 768:(j + 1) * 768].partition_broadcast(128), "cw_d"))
                cp(convw[:, j * 768:(j + 1) * 768], wst[:, 0:768], eng="pool")
            act(negA[:, :], pv[:, P_ALOG:P_ALOG + 8], AF.Exp)
            ts(negA[:, :], negA[:, :], -1.0, None, ALU.mult)

            for s in range(NS):
                Sq = seqs[s]
                n = Sq // 128
                R0 = roff[s]
                src_h = x_d if l == 0 else hbuf

                for mt in range(2):
                    dma("sp", hin[:, :], mem_d[s * 256 + mt * 128:s * 256 + (mt + 1) * 128, :])
                    rms_to_xnT(hin[:, :])
                    for c in range(8):
                        mm(pa[:, :], xnT[:, c * 128:(c + 1) * 128], wmem[:, c, :], start=(c == 0), stop=(c == 7))
                    cp(kmtok[:, :], pa[:, 0:256], eng="act")
                    cp(Vmaug[:, mt, :, 0:64], pa[:, 256:512].r("p (a b) -> p a b", a=4), eng="dve")
                    for p in range(2):
                        tr(ptr[:, p * 128:(p + 1) * 128], kmtok[:, p * 128:(p + 1) * 128], identb)
                    for q in range(2):
                        ts(kmT[:, :, q, mt * 128:(mt + 1) * 128], ptr[:, 0:256].r("p (a b) -> p a b", a=2),
                           mBD[:, 64 * q:64 * q + 1], None, ALU.mult)

                def P_stage(i, phase):
                    par = i % 2
                    rows = slice(R0 + i * 128, R0 + (i + 1) * 128)
                    tb, tf = Tb[par], Tf[par]
                    if phase == 0:
                        dma("sp", hin[:, :], src_h.k(R0 // 128 + i)[rows, :])
                        dma("sp", rc[:, :], rc_d[i * 128:(i + 1) * 128, :])
                        dma("sp", rs[:, :], rs_d[i * 128:(i + 1) * 128, :])
                        rms_to_xnT(hin[:, :], p6b)
                    groups = [(0, 512), (512, 1024), (1024, 1536), (1536, 2048), (2048, 2320), (2320, 2832),
                              (2832, 3120)]
                    for gi, (c0, c1) in enumerate(groups):
                        if (gi in (3, 4)) != (phase == 0):
                            continue
                        pp = p6 if gi % 2 == 0 else sc0
                        w = c1 - c0
                        for c in range(8):
                            mm(pp[:, 0:w], xnT[:, c * 128:(c + 1) * 128], win[:, c, c0:c1], start=(c == 0),
                               stop=(c == 7))
                        if gi == 0:
                            act(wA[:, 0:384], pp[:, 0:384], AF.Square)
                            red(ss8b[:, 0:6], wA[:, 0:384].r("p (a b) -> p a b", a=6))
                            rsqrt_(rn8b[:, 0:6], ss8b[:, 0:6], 1.0 / 64, EPS)
                            tt(wqB[:, :].r("p (a b) -> p a b", a=6), pp[:, 0:384].r("p (a b) -> p a b", a=6),
                               rn8b[:, 0:6].bc(2, [128, 6, 64]), ALU.mult)
                            cp(Vaug[:, i, :, 0:64], pp[:, 384:512].r("p (a b) -> p a b", a=2), eng="dve")
                            tt(wqB[:, :], wqB[:, :], pv[:, P_GQK:P_GQK + 384], ALU.mult, eng="pool")
                            v5 = lambda t: t[:, 0:384].r("p (a b c) -> p a b c", a=12, b=2)
                            tt(v5(wA)[:, :, 0, :], v5(wqB)[:, :, 1, :], v5(rs)[:, :, 0, :], ALU.mult, eng="pool")
                            tt(v5(wA)[:, :, 1, :], v5(wqB)[:, :, 0, :], v5(rs)[:, :, 1, :], ALU.mult, eng="pool")
                            tt(wqB[:, :], wqB[:, :], rc[:, :], ALU.mult, eng="pool")
                            tt(tb[:, B_Q:B_Q + 256], wqB[:, 0:256], wA[:, 0:256], ALU.add)
                            tt(qkrot[:, 0:128], wqB[:, 256:384], wA[:, 256:384], ALU.add)
                            tr(p6b[:, 0:128], qkrot[:, 0:128], identb)
                            cp(KT[:, i * 128:(i + 1) * 128], p6b[:, 0:128], eng="act")
                        elif gi in (1, 2):
                            sigmoid_from(wA[:, 0:512], pp[:, 0:512], 512)
                            off = B_GA if gi == 1 else B_GD
                            tt(tb[:, off:off + 512], pp[:, 0:512], wA[:, 0:512], ALU.mult)
                        elif gi == 3:
                            for k3 in range(3):
                                tt(xw[i % 3][:, k3, 0:512], pp[:, 0:512], convw[:, 768 * k3:768 * k3 + 512], ALU.mult)
                        elif gi == 4:
                            cp(lgr[par][:, :], pp[:, 256:272], eng="dve")
                            for k3 in range(3):
                                tt(xw[i % 3][:, k3, 512:768], pp[:, 0:256], convw[:, 768 * k3 + 512:768 * (k3 + 1)],
                                   ALU.mult)
                        elif gi == 5:
                            act(tf[:, F_LQ:F_LQ + 128], pp[:, 0:128], AF.Copy, scale=32.0 ** -0.5)
                            cp(tf[:, F_LK:F_LK + 128], pp[:, 128:256], eng="act")
                            cp(tb[:, B_LV:B_LV + 256], pp[:, 256:512], eng="act")
                        else:
                            cp(tb[:, B_MQ:B_MQ + 256], pp[:, 0:256], eng="act")
                            cp(low[:, :], pp[:, 256:288], eng="dve")
                            tr(sc0[0:32, 0:128], low[:, :], ident)
                            cp(lowT[:, :], sc0[0:32, 0:128], eng="act")
                            mm(sc0[:, 128:384], lowT[:, :], wup[:, :])
                            tt(gl[:, :], sc0[:, 128:384], pv[:, P_GLB:P_GLB + 256], ALU.add)
                            act(gl[:, :], gl[:, :], AF.Exp, scale=-1.0)
                            act(ngkf[par][:, :], gl[:, 0:128], AF.Ln, bias=1.0)
                            act(tf[:, F_NGK:F_NGK + 128], gl[:, 128:256], AF.Ln, bias=1.0)

                    if phase == 1:
                        lg = lgr[par]
                        sigmoid_from(lg2[:, 0:8], lg[:, 0:8], 8)
                        tt(lg2[:, 8:16], lg[:, 8:16], pv[:, P_DTB:P_DTB + 8], ALU.add)
                        act(lg2[:, 8:16], lg2[:, 8:16], AF.Exp)
                        act(lg2[:, 8:16], lg2[:, 8:16], AF.Ln, bias=1.0)
                        tt(lg2[:, 8:16], lg2[:, 8:16], negA[:, :], ALU.mult)
                        cp(bgf[par][:, 0:4], lg2[:, 0:4], eng="pool")
                        cp(bgf[par][:, 4:8], lg2[:, 8:12], eng="pool")
                        cp(tf[:, F_BG:F_BG + 4], lg2[:, 4:8], eng="pool")
                        cp(tf[:, F_BG + 4:F_BG + 8], lg2[:, 12:16], eng="pool")

                def post(i, extra=None):
                    par = i % 2
                    rows = slice(R0 + i * 128, R0 + (i + 1) * 128)
                    tb, tf = Tb[par], Tf[par]
                    S.capture()
                    for (c0, c1, pp) in ((0, 512, p2), (512, 768, p3)):
                        w = c1 - c0
                        ops = [(SHPb, xw[i % 3][:, 0, c0:c1]), (identb, xw[i % 3][:, 1, c0:c1]),
                               (SHNb, xw[i % 3][:, 2, c0:c1])]
                        if i > 0:
                            ops.append((HPb, xw[(i - 1) % 3][:, 0, c0:c1]))
                        if i < n - 1:
                            ops.append((HNb, xw[(i + 1) % 3][:, 2, c0:c1]))
                        for k3, (lh, rh) in enumerate(ops):
                            mm(pp[:, 0:w], lh, rh, start=(k3 == 0), stop=(k3 == len(ops) - 1))
                        sigmoid_from(we[:, c0:c1], pp[:, 0:w], w)
                        tt(cv[:, c0:c1], pp[:, 0:w], we[:, c0:c1], ALU.mult)
                    act(sq[:, 0:512], cv[:, 0:512], AF.Square)
                    red(ss8[:, :], sq[:, 0:512].r("p (a b) -> p a b", a=8))
                    rsqrt_(rn8[:, :], ss8[:, :], 1.0, EPS)
                    ts(rn8[:, 0:4], rn8[:, 0:4], 0.125, None, ALU.mult)
                    tt(tb[:, B_GQ:B_GQ + 512].r("p (a b) -> p a b", a=8), cv[:, 0:512].r("p (a b) -> p a b", a=8),
                       rn8[:, :].bc(2, [128, 8, 64]), ALU.mult)
                    cp(tb[:, B_GV:B_GV + 256], cv[:, 512:768], eng="act")
                    gdn_core(0, tb[:, B_GQ:B_GQ + 256], tb[:, B_GK:B_GK + 256], tb[:, B_GV:B_GV + 256],
                             bgf[par][:, 0:4], bgf[par][:, 4:8], tf[:, F_OB:F_OB + 256], first=(i == 0))
                    LG1 = S.end_capture()
                    S.capture()
                    gla_core(0, tf[:, F_LQ:F_LQ + 128], tf[:, F_LK:F_LK + 128], tb[:, B_LV:B_LV + 256],
                             ngkf[par][:, :], tf[:, F_OC:F_OC + 256], first=(i == 0))
                    LL1 = S.end_capture()
                    S.merge([LG1, LL1] + ([extra] if extra else []))
                    dma("pool", stb.k(R0 // 128 + i)[rows, :], tb[:, :])
                    dma("pool", stf.k(R0 // 128 + i)[rows, :], tf[:, :])

                P_stage(0, 0)
                for i in range(n + 1):
                    LP = None
                    if i < n:
                        S.capture()
                        P_stage(i, 1)
                        if i + 1 < n:
                            P_stage(i + 1, 0)
                        LP = S.end_capture()
                    if i >= 1:
                        post(i - 1, LP)
                    elif LP:
                        S.merge([LP])

                if last_layer and final:
                    dma("sp", cv[:, 0:768], V(fng_d[0:768].partition_broadcast(128), "fng_d"))
                    dma("sp", we[:, 512:768], V(fng_d[768:1024].partition_broadcast(128), "fng_d"))
                def tail(t):
                    rows_t = slice(R0 + t * 128, R0 + (t + 1) * 128)
                    dma("sp", hin[:, :], src_h.k(R0 // 128 + t)[rows_t, :])
                    tt(os2[:, :], obcs[t % 2][:, :], Tf[t % 2][:, F_OB:F_OB + 512], ALU.add)
                    act(sq[:, 0:512], os2[:, :], AF.Square)
                    red(ss8[:, 0:8], sq[:, 0:512].r("p (a b) -> p a b", a=8))
                    rsqrt_(rn8[:, 0:8], ss8[:, 0:8], 1.0 / 64, EPS)
                    tt(os2[:, :].r("p (a b) -> p a b", a=8), os2[:, :].r("p (a b) -> p a b", a=8),
                       rn8[:, 0:8].bc(2, [128, 8, 64]), ALU.mult)
                    tt(os2[:, :], os2[:, :], pv[:, P_GOG:P_GOG + 512], ALU.mult, eng="pool")
                    tt(ycat[:, 256:768], os2[:, :], Tb[t % 2][:, B_GD:B_GD + 512], ALU.mult)
                    for c in range(8):
                        src_c = ycat[:, c * 128:(c + 1) * 128] if c < 6 else ymbs[t % 2][:, (c - 6) * 128:(c - 5) * 128]
                        tr(p6b[:, c * 128:(c + 1) * 128], src_c, identb)
                    cp(ycatT[:, :], p6b[:, :], eng="act")
                    for half, pp in ((0, sc0), (1, p6)):
                        for c in range(8):
                            mm(pp[:, :], ycatT[:, c * 128:(c + 1) * 128], wout[:, c, half * 512:(half + 1) * 512],
                               start=(c == 0), stop=(c == 7))
                        tt(hout[:, half * 512:(half + 1) * 512], pp[:, :], hin[:, half * 512:(half + 1) * 512], ALU.add)
                    if last_layer and final:
                        act(junk[:, :], hout[:, :], AF.Square, accum=ssq[:, :])
                        rsqrt_(rstd[:, :], ssq[:, :], 1.0 / D, EPS)
                        ts(hout[:, :], hout[:, :], rstd[:, :], None, ALU.mult)
                        tt(hout[:, 0:768], hout[:, 0:768], cv[:, 0:768], ALU.mult, eng="pool")
                        tt(hout[:, 768:1024], hout[:, 768:1024], we[:, 512:768], ALU.mult, eng="pool")
                        dma("pool", y_d.k(R0 // 128 + t)[rows_t, :], hout[:, :])
                    elif last_layer:
                        dma("pool", y_d.k(R0 // 128 + t)[rows_t, :], hout[:, :])
                    else:
                        dma("pool", hbuf.k(R0 // 128 + t)[rows_t, :], hout[:, :])
                nkt = n
                prev = None
                for i in range(n - 1, -1, -1):
                    rows = slice(R0 + i * 128, R0 + (i + 1) * 128)
                    Tb2, Tf2 = Tb[i % 2], Tf[i % 2]
                    dma("sp", Tb2[:, :], stb.k(R0 // 128 + i)[rows, :])
                    dma("sp", Tf2[:, :], stf.k(R0 // 128 + i)[rows, :])
                    first = (i == n - 1)
                    S.capture()
                    gdn_core(1, Tb2[:, B_GQ:B_GQ + 256], Tb2[:, B_GK:B_GK + 256], Tb2[:, B_GV:B_GV + 256],
                             Tf2[:, F_BG:F_BG + 4], Tf2[:, F_BG + 4:F_BG + 8], obcs[i % 2][:, 0:256], first=first)
                    LG = S.end_capture()
                    S.capture()
                    gla_core(1, Tf2[:, F_LQ:F_LQ + 128], Tf2[:, F_LK:F_LK + 128], Tb2[:, B_LV:B_LV + 256],
                             Tf2[:, F_NGK:F_NGK + 128], obcs[i % 2][:, 256:512], first=first)
                    for p in range(2):
                        tr(ptr[:, 768 + p * 128:768 + (p + 1) * 128], Tb2[:, B_MQ + p * 128:B_MQ + (p + 1) * 128], identb)
                    cp(mqT[:, :, :], ptr[:, 768:1024].r("p (a b) -> p a b", a=2), eng="act")
                    for hp in range(2):
                        for hq in range(2):
                            h = 2 * hp + hq
                            pr, _hb = h // 2, (h % 2) * 64
                            for kt in range(2):
                                mm(sc1[:, (hq * 2 + kt) * 128:(hq * 2 + kt + 1) * 128],
                                   kmT[:, pr, h % 2, kt * 128:(kt + 1) * 128], mqT[:, pr, :])
                        act(pT[:, 512:1024], sc1[:, :], AF.Exp, scale=0.125)
                        for hq in range(2):
                            h = 2 * hp + hq
                            for kt in range(2):
                                mm(pb[:, h * 65:(h + 1) * 65], pT[:, 512 + (hq * 2 + kt) * 128:512 + (hq * 2 + kt + 1) * 128],
                                   Vmaug[:, kt, h, 0:65], start=(kt == 0), stop=(kt == 1))
                    recip(rs4[:, :], pb[:, 0:260].r("p (a b) -> p a b", a=4)[:, :, 64])
                    tt(ymbs[i % 2][:, :].r("p (a b) -> p a b", a=4), pb[:, 0:260].r("p (a b) -> p a b", a=4)[:, :, 0:64],
                       rs4[:, :].bc(2, [128, 4, 64]), ALU.mult)
                    tt(ymbs[i % 2][:, :], ymbs[i % 2][:, :], Tb2[:, B_GM:B_GM + 256], ALU.mult, eng="pool")
                    LL = S.end_capture()
                    S.capture()
                    if prev is not None:
                        tail(prev)
                    for g2 in range(2):
                        tr(p6b[:, 768 + g2 * 128:768 + (g2 + 1) * 128], Tb2[:, B_Q + g2 * 128:B_Q + (g2 + 1) * 128], identb)
                    for h in range(4):
                        ts(qm[:, h, :], p6b[:, 768 + (h % 2) * 128:768 + (h % 2 + 1) * 128],
                           mBD[:, 64 * (h // 2):64 * (h // 2) + 1], None, ALU.mult)
                    for h in range(4):
                        kv, g2 = h // 2, h % 2
                        kb = 64 * kv
                        for k0 in range(0, nkt, 4):
                            kn_ = min(4, nkt - k0)
                            for kk in range(kn_):
                                kt = k0 + kk
                                mm(sc0[:, kk * 128:(kk + 1) * 128], KT[:, kt * 128:(kt + 1) * 128], qm[:, h, :])
                            act(pT[:, 0:kn_ * 128], sc0[:, 0:kn_ * 128], AF.Exp, scale=0.125)
                            for kk in range(kn_):
                                kt = k0 + kk
                                mm(p6[:, h * 65:(h + 1) * 65], pT[:, kk * 128:(kk + 1) * 128], Vaug[:, kt, kv, 0:65],
                                   start=(kt == 0), stop=(kt == nkt - 1))
                    recip(rs4a[:, :], p6[:, 0:260].r("p (a b) -> p a b", a=4)[:, :, 64])
                    tt(yaa[:, :].r("p (a b) -> p a b", a=4), p6[:, 0:260].r("p (a b) -> p a b", a=4)[:, :, 0:64],
                       rs4a[:, :].bc(2, [128, 4, 64]), ALU.mult)
                    tt(ycat[:, 0:256], yaa[:, :], Tb2[:, B_GA:B_GA + 256], ALU.mult, eng="pool")
                    LA = S.end_capture()
                    S.merge([LG, LL, LA])
                    prev = i
                tail(prev)
        if _os.environ.get("KDESC"):
            lo, hi = [int(v) for v in _os.environ["KDESC"].split(",")]
            for dd in S.desc:
                if lo <= dd[0] <= hi:
                    print("OP", dd)
        if _os.environ.get("KLIMIT"):
            print("TOTAL OPS", S.total)
        S.emit(st)
    return nc


def _consts():
    c = np.zeros((128, NCST), np.float32)
    p = np.arange(128)[:, None]
    f = np.arange(128)[None, :]
    bd = (p // 64 == f // 64)
    c[:, C_ID:C_ID + 128] = (p == f)
    c[:, C_LE:C_LE + 128] = bd & (p <= f)
    c[:, C_LT:C_LT + 128] = bd & (p < f)
    c[:, C_GE:C_GE + 128] = bd & (p >= f)
    c[:, C_GT:C_GT + 128] = bd & (p > f)
    c[:, C_BD:C_BD + 128] = bd
    c[:, C_SEL0:C_SEL0 + 128] = (p < 64) & (f >= 0)
    c[:, C_SEL1:C_SEL1 + 128] = (p >= 64) & (f >= 0)
    c[:, C_RM4:C_RM4 + 4] = (p // 32 == np.arange(4)[None, :])
    c[:, C_BD4:C_BD4 + 256] = (p // 32 == np.arange(256)[None, :] // 64)
    c[:, C_ONE] = 1.0
    cb = np.zeros((128, 640), np.float32)
    cb[:, 0:128] = (p == f)
    cb[:, 128:256] = (p == f - 1)
    cb[:, 256:384] = (p == f + 1)
    cb[:, 384:512] = (p == 127) & (f == 0)
    cb[:, 512:640] = (p == 0) & (f == 127)
    return c, cb


def _rope_tables(smax):
    t = np.arange(smax)
    pos = np.stack([t // 64, t % 64], -1).astype(np.float32)
    inv = (10000.0 ** (-np.arange(0, 32, 2, dtype=np.float32) / 32)).astype(np.float32)
    ang = pos[:, :, None] * inv[None, None, :]
    cos, sin = np.cos(ang).astype(np.float32), np.sin(ang).astype(np.float32)
    C = np.zeros((smax, 2, 2, 16), np.float32)
    Sg = np.zeros((smax, 2, 2, 16), np.float32)
    C[:, :, 0, :] = cos
    C[:, :, 1, :] = cos
    Sg[:, :, 0, :] = -sin
    Sg[:, :, 1, :] = sin
    C = np.tile(C.reshape(smax, 1, 64), (1, 6, 1)).reshape(smax, 384)
    Sg = np.tile(Sg.reshape(smax, 1, 64), (1, 6, 1)).reshape(smax, 384)
    return np.ascontiguousarray(C), np.ascontiguousarray(Sg)


def _col_perm():
    o = dict(a_q=0, a_k=256, a_v=384, a_gate=512, d_qkv=768, d_beta=1536, d_alpha=1544, d_gate=1552,
             l_q=1808, l_k=1936, l_v=2064, l_low=2320, l_gate=2352, m_q=2608, m_gate=2864)
    r = lambda a, n: list(range(a, a + n))
    aq = []
    for g in range(2):
        for kv in range(2):
            h = kv * 2 + g
            aq += r(o["a_q"] + h * 64, 64)
    perm = (aq + r(o["a_k"], 128) + r(o["a_v"], 128)
            + r(o["a_gate"], 256) + r(o["m_gate"], 256)
            + r(o["d_gate"], 256) + r(o["l_gate"], 256)
            + r(o["d_qkv"], 512)
            + r(o["d_qkv"] + 512, 256) + r(o["d_beta"], 8) + r(o["d_alpha"], 8)
            + r(o["l_q"], 128) + r(o["l_k"], 128) + r(o["l_v"], 256)
            + r(o["m_q"], 256) + r(o["l_low"], 32))
    assert len(perm) == NCOL and len(set(perm)) == NCOL
    return np.array(perm)


def _prep_shared(inp, depth):
    f = lambda a: np.ascontiguousarray(np.asarray(a, dtype=np.float32))
    perm = _col_perm()
    w_in = f(np.asarray(inp["w_in"])[:depth][:, :, perm])
    pvec = np.zeros((depth, NPV), np.float32)
    wup = np.zeros((depth, 32, 256), np.float32)
    for l in range(depth):
        pvec[l, P_GQK:P_GQK + 256] = np.tile(np.asarray(inp["att_q_norm_g"])[l], 4)
        pvec[l, P_GQK + 256:P_GQK + 384] = np.tile(np.asarray(inp["att_k_norm_g"])[l], 2)
        pvec[l, P_ALOG:P_ALOG + 8] = np.asarray(inp["gdn_a_log"])[l].reshape(-1)
        pvec[l, P_DTB:P_DTB + 8] = np.asarray(inp["gdn_dt_bias"])[l].reshape(-1)
        pvec[l, P_GOG:P_GOG + 256] = np.tile(np.asarray(inp["gdn_out_norm_g"])[l], 4)
        pvec[l, P_GLB:P_GLB + 256] = np.asarray(inp["gla_b_gate"])[l].reshape(-1)
        pvec[l, P_GLG:P_GLG + 256] = np.tile(np.asarray(inp["gla_out_norm_g"])[l], 4)
        for d in range(2):
            wup[l, d * 16:(d + 1) * 16, d * 128:(d + 1) * 128] = np.asarray(inp["gla_w_gate_up"])[l, d]
    return dict(w_in=w_in, w_out=f(np.asarray(inp["w_out"])[:depth]), w_mem=f(np.asarray(inp["w_mem_kv"])[:depth]),
                norm_g=f(np.asarray(inp["norm_g"])[:depth]), mem_g=f(np.asarray(inp["mem_norm_g"])[:depth]),
                pvec=pvec, wup=wup, fng=f(inp["final_norm_g"]), cst=_consts()[0], cstb=_consts()[1],
                convw=f(np.asarray(inp["gdn_conv_w"])[:depth].reshape(depth, -1)))


def kernel(**inputs):
    depth = 4
    xp = np.asarray(inputs["x_prompt"], np.float32)
    xs = np.asarray(inputs["x_sample"], np.float32)
    mp = np.asarray(inputs["mem_prompt"], np.float32)
    ms = np.asarray(inputs["mem_sample"], np.float32)
    Sp, Ss = xp.shape[1], xs.shape[1]
    seqs = [Sp, Ss, Ss]
    shared = _prep_shared(inputs, depth)
    C, Sg = _rope_tables(max(seqs))
    shared["ropec"], shared["ropes"] = C, Sg
    nc = _build(seqs, depth)
    in_maps = []
    for c in range(8):
        m = dict(shared)
        m["x"] = np.ascontiguousarray(np.concatenate([xp[c], xs[2 * c], xs[2 * c + 1]], 0))
        m["mem"] = np.ascontiguousarray(np.concatenate([mp[c], ms[2 * c], ms[2 * c + 1]], 0))
        in_maps.append(m)
    res = run_bass_kernel_spmd(nc, in_maps, core_ids=list(range(8)))
    yp = np.stack([res.results[c]["y"][:Sp] for c in range(8)], 0)
    ysm = np.stack([res.results[c // 2]["y"][Sp + (c % 2) * Ss:Sp + (c % 2 + 1) * Ss] for c in range(16)], 0)
    return (yp.astype(np.float32), ysm.astype(np.float32))
```

```python
from contextlib import ExitStack
import numpy as np
import concourse.bass as bass
import concourse.mybir as mybir
from concourse.bass_utils import run_bass_kernel_spmd

F32 = mybir.dt.float32
BF16 = mybir.dt.bfloat16
AF = mybir.ActivationFunctionType
ALU = mybir.AluOpType
AX = mybir.AxisListType

D = 1024
NCOL = 3120
EPS = 1e-6
NB = 2560
NF = 904
B_GA, B_GM, B_GD, B_GL, B_MQ, B_GQ, B_GK, B_GV, B_LV, B_Q = 0, 256, 512, 768, 1024, 1280, 1536, 1792, 2048, 2304
F_BG, F_LQ, F_LK, F_NGK, F_OB, F_OC = 0, 8, 136, 264, 392, 648
P_GQK, P_ALOG, P_DTB, P_GOG, P_GLG, P_GLB = 0, 384, 392, 400, 656, 912
NPV = 1168
C_ID, C_LE, C_LT, C_GE, C_GT, C_BD, C_SEL0, C_SEL1 = [128 * k for k in range(8)]
C_RM4 = 128 * 8
C_BD4 = C_RM4 + 4
C_ONE = C_BD4 + 256
NCST = C_ONE + 1


class _Op:
    __slots__ = ("eng", "idx", "fn", "waits", "signal", "kind", "dma_m", "tok")

    def __init__(self, eng, idx, fn, kind):
        self.eng = eng
        self.idx = idx
        self.fn = fn
        self.kind = kind
        self.waits = []
        self.signal = False
        self.dma_m = None
        self.tok = None


class Sched:
    ENGS = ("pe", "act", "dve", "pool", "sp")

    def __init__(self, nc, R=8, epoch=16000):
        self.nc = nc
        self.R = R
        self.epoch = epoch
        self.streams = {e: [] for e in self.ENGS}
        self.ndma = {e: 0 for e in self.ENGS}
        self.dma_ops = {e: [] for e in self.ENGS}
        self.last_w = {}
        self.readers = {}
        self.waited = {e: {p: -1 for p in self.ENGS} for e in self.ENGS}
        self.waited_dma = {e: {} for e in self.ENGS}

    def capture(self):
        self._cap = []

    def end_capture(self):
        lst = self._cap
        self._cap = None
        return lst

    def merge(self, lists):
        pos = [0] * len(lists)
        while True:
            best, bf = -1, 2.0
            for j, l in enumerate(lists):
                if pos[j] < len(l):
                    f = pos[j] / len(l)
                    if f < bf:
                        best, bf = j, f
            if best < 0:
                break
            self._add(*lists[best][pos[best]])
            pos[best] += 1

    def _add(self, eng, fn, reads, writes, kind):
        if getattr(self, "_cap", None) is not None:
            self._cap.append((eng, fn, list(reads), list(writes), kind))
            return None
        self.total = getattr(self, "total", 0) + 1
        if getattr(self, "desc", None) is not None:
            import sys as _sys
            fr = _sys._getframe(3)
            self.desc.append((self.total, eng, kind, fr.f_lineno, fr.f_back.f_lineno))
        if self.total > getattr(self, "limit", 1 << 60) or self.total in getattr(self, "skip", ()):
            return None
        excl = getattr(self, "excl", ())
        if excl and eng != "pe":
            extra = [k for k in reads if k in excl and k not in writes]
            if extra:
                writes = list(writes) + extra
        st = self.streams[eng]
        op = _Op(eng, len(st), fn, kind)
        st.append(op)
        deps = []
        for k in reads:
            w = self.last_w.get(k)
            if w is not None:
                deps.append((w, True))
        for k in writes:
            w = self.last_w.get(k)
            if w is not None:
                deps.append((w, False))
            for r in self.readers.get(k, {}).values():
                deps.append((r, False))
        for (p, raw) in deps:
            if p is op:
                continue
            if p.kind == "c":
                if p.eng == eng:
                    if eng == "pe":
                        continue
                if self.waited[eng][p.eng] >= p.idx:
                    continue
                self.waited[eng][p.eng] = p.idx
                p.signal = True
                op.waits.append(p)
            else:
                key = (p.eng, p.dma_m % self.R)
                if self.waited_dma[eng].get(key, -1) >= p.dma_m:
                    continue
                self.waited_dma[eng][key] = p.dma_m
                op.waits.append(p)
        if kind == "d":
            m = self.ndma[eng]
            self.ndma[eng] = m + 1
            op.dma_m = m
            self.dma_ops[eng].append(op)
            if m >= self.R:
                prev = self.dma_ops[eng][m - self.R]
                key = (eng, m % self.R)
                if self.waited_dma[eng].get(key, -1) < prev.dma_m:
                    self.waited_dma[eng][key] = prev.dma_m
                    op.waits.append(prev)
        for k in writes:
            self.last_w[k] = op
            self.readers[k] = {}
        for k in reads:
            self.readers.setdefault(k, {})[eng] = op
        return op

    def op(self, eng, fn, reads=(), writes=()):
        return self._add(eng, fn, reads, writes, "c")

    def dma(self, eng, fn, reads=(), writes=()):
        return self._add(eng, fn, reads, writes, "d")

    def emit(self, stack):
        nc = self.nc
        for e in self.ENGS:
            n = self.ndma[e]
            if n:
                last = self.dma_ops[e][max(0, n - self.R):]
                f = _Op(e, len(self.streams[e]), None, "f")
                f.waits = list(last)
                self.streams[e].append(f)
        csem = {}
        for e in self.ENGS:
            cnt = 0
            for op in self.streams[e]:
                if op.kind == "c" and op.signal:
                    ep = cnt // self.epoch
                    if (e, ep) not in csem:
                        csem[(e, ep)] = stack.enter_context(nc.semaphore(f"c_{e}_{ep}"))
                    op.tok = (csem[(e, ep)], cnt % self.epoch + 1)
                    cnt += 1
        dsem = {}
        for e in self.ENGS:
            for op in self.dma_ops[e]:
                s = op.dma_m % self.R
                if (e, s) not in dsem:
                    dsem[(e, s)] = stack.enter_context(nc.semaphore(f"d_{e}_{s}"))
                op.tok = (dsem[(e, s)], 16 * (op.dma_m // self.R + 1))
        block = stack.enter_context(nc.Block())

        def replay(e, eng):
            for op in self.streams[e]:
                for p in op.waits:
                    eng.wait_ge(p.tok[0], p.tok[1])
                if op.fn is None:
                    continue
                ins = op.fn(eng)
                if op.kind == "d":
                    ins.then_inc(op.tok[0], 16)
                elif op.signal:
                    ins.then_inc(op.tok[0], 1)

        if self.streams["pe"]:
            @block.tensor
            def _(eng):
                replay("pe", eng)
        if self.streams["act"]:
            @block.scalar
            def _(eng):
                replay("act", eng)
        if self.streams["dve"]:
            @block.vector
            def _(eng):
                replay("dve", eng)
        if self.streams["pool"]:
            @block.gpsimd
            def _(eng):
                replay("pool", eng)
        if self.streams["sp"]:
            @block.sync
            def _(eng):
                replay("sp", eng)


class V:
    __slots__ = ("ap", "key")

    def __init__(self, ap, key):
        self.ap = ap
        self.key = key

    def __getitem__(self, idx):
        return V(self.ap[idx], self.key)

    def r(self, pat, **kw):
        return V(self.ap.rearrange(pat, **kw), self.key)

    def bc(self, axis, shape):
        return V(self.ap.unsqueeze(axis).to_broadcast(list(shape)), self.key)


class TT:
    def __init__(self, t, key):
        self.t = t
        self.key = key

    def __getitem__(self, idx):
        return V(self.t[idx], self.key)

    def k(self, sub):
        return TT(self.t, (self.key, sub))


def _build(seqs, depth, final=True):
    nc = bass.Bass("TRN2", target_bir_lowering=False)
    NT = sum(seqs)
    NS = len(seqs)
    SMAX = max(seqs)
    roff = [sum(seqs[:i]) for i in range(NS)]

    def din(name, shape, dt=F32):
        return nc.dram_tensor(name, list(shape), dt, kind="ExternalInput").ap()

    x_d = TT(din("x", [NT, D]), "x")
    mem_d = TT(din("mem", [NS * 256, D]), "mem")
    win_d = din("w_in", [depth, D, NCOL])
    wout_d = din("w_out", [depth, D, D])
    wmem_d = din("w_mem", [depth, D, 512])
    ng_d = din("norm_g", [depth, D])
    mg_d = din("mem_g", [depth, D])
    pv_d = din("pvec", [depth, NPV])
    wup_d = din("wup", [depth, 32, 256])
    fng_d = din("fng", [D])
    cst_d = din("cst", [128, NCST])
    cstb_d = din("cstb", [128, 640])
    cw_d = din("convw", [depth, 2304])
    rc_d = TT(din("ropec", [SMAX, 384]), "ropec")
    rs_d = TT(din("ropes", [SMAX, 384]), "ropes")
    y_d = TT(nc.dram_tensor("y", [NT, D], F32, kind="ExternalOutput").ap(), "y")
    hbuf = TT(nc.dram_tensor("hbuf", [NT, D], F32, kind="Internal").ap(), "hbuf")
    stb = TT(nc.dram_tensor("stashb", [NT, NB], BF16, kind="Internal").ap(), "stb")
    stf = TT(nc.dram_tensor("stashf", [NT, NF], F32, kind="Internal").ap(), "stf")

    with ExitStack() as st:
        def sb(name, shape, dt=F32):
            return TT(st.enter_context(nc.sbuf_tensor("s_" + name, list(shape), dt)), name)

        def ps(name, shape, dt=F32):
            return TT(st.enter_context(nc.psum_tensor("ps_" + name, list(shape), dt)), name)

        S = Sched(nc)
        import os as _os
        if _os.environ.get("KLIMIT"):
            S.limit = int(_os.environ["KLIMIT"])
        if _os.environ.get("KDESC"):
            S.desc = []
        if _os.environ.get("KSKIP"):
            S.skip = set(int(v) for v in _os.environ["KSKIP"].split(","))

        def keys(*vs):
            return [v.key for v in vs if isinstance(v, V)]

        def apof(v):
            return v.ap if isinstance(v, V) else v

        def act(out, in_, func, scale=1.0, bias=0.0, accum=None):
            kw = dict(out=out.ap, in_=in_.ap, func=func, scale=apof(scale), bias=apof(bias))
            w = [out.key]
            if accum is not None:
                kw["accum_out"] = accum.ap
                w.append(accum.key)
            S.op("act", lambda e, kw=kw: e.activation(**kw), reads=keys(in_, scale, bias), writes=w)

        def ts(out, in0, s1, s2=None, op0=ALU.mult, op1=None, eng="dve"):
            kw = dict(out=out.ap, in0=in0.ap, scalar1=apof(s1), scalar2=apof(s2), op0=op0)
            if op1 is not None:
                kw["op1"] = op1
            S.op(eng, lambda e, kw=kw: e.tensor_scalar(**kw), reads=keys(in0, s1, s2), writes=[out.key])

        def tt(out, in0, in1, op=ALU.mult, eng="dve"):
            S.op(eng, lambda e: e.tensor_tensor(out=out.ap, in0=in0.ap, in1=in1.ap, op=op),
                 reads=keys(in0, in1), writes=[out.key])

        def stt(out, in0, scalar, in1, op0, op1, eng="dve"):
            S.op(eng, lambda e: e.scalar_tensor_tensor(out=out.ap, in0=in0.ap, scalar=apof(scalar), in1=in1.ap,
                                                       op0=op0, op1=op1),
                 reads=keys(in0, scalar, in1), writes=[out.key])

        def cp(out, in_, eng="dve"):
            if eng == "act":
                S.op("act", lambda e: e.copy(out=out.ap, in_=in_.ap), reads=[in_.key], writes=[out.key])
            else:
                S.op(eng, lambda e: e.tensor_copy(out=out.ap, in_=in_.ap), reads=[in_.key], writes=[out.key])

        def red(out, in_, op=ALU.add):
            S.op("dve", lambda e: e.tensor_reduce(out=out.ap, in_=in_.ap, axis=AX.X, op=op),
                 reads=[in_.key], writes=[out.key])

        def recip(out, in_):
            S.op("dve", lambda e: e.reciprocal(out=out.ap, in_=in_.ap), reads=[in_.key], writes=[out.key])

        def memset(out, val, eng="pool"):
            S.op(eng, lambda e: e.memset(out.ap, val), writes=[out.key])

        def mm(out, lhsT, rhs, start=True, stop=True, skip=False):
            S.op("pe", lambda e: e.matmul(out.ap, lhsT=lhsT.ap, rhs=rhs.ap, start=start, stop=stop,
                                          skip_group_check=skip),
                 reads=keys(lhsT, rhs), writes=[out.key])

        def tr(out, in_, ident):
            S.op("pe", lambda e: e.transpose(out=out.ap, in_=in_.ap, identity=ident.ap),
                 reads=keys(in_, ident), writes=[out.key])

        def dma(eng, out, in_, slow=False):
            if slow:
                S.dma(eng, lambda e: e.dma_start(out=out.ap, in_=in_.ap, allow_slow_non_contiguous=True),
                      reads=[in_.key], writes=[out.key])
            else:
                S.dma(eng, lambda e: e.dma_start(out=out.ap, in_=in_.ap), reads=[in_.key], writes=[out.key])

        def rsqrt_(out, in_, scale, eps):
            act(out, in_, AF.Ln, scale=scale, bias=eps)
            act(out, out, AF.Exp, scale=-0.5)

        cst = sb("cst", [128, NCST])
        cstb = sb("cstb", [128, 5 * 128], BF16)
        win = sb("win", [128, 8, NCOL], BF16)
        wout = sb("wout", [128, 8, D], BF16)
        wmem = sb("wmem", [128, 8, 512], BF16)
        ngt = sb("ngt", [128, 8])
        mgt = sb("mgt", [128, 8])
        pv = sb("pv", [128, NPV])
        wup = sb("wup", [32, 256])
        negA = sb("negA", [128, 8])
        KT = sb("KT", [128, SMAX], BF16)
        qm = sb("qm", [128, 4, 128], BF16)
        Vaug = sb("Vaug", [128, SMAX // 128, 2, 65], BF16)
        kmT = sb("kmT", [128, 2, 2, 256], BF16)
        Vmaug = sb("Vmaug", [128, 2, 4, 65], BF16)
        hin = sb("hin", [128, D])
        ssq = sb("ssq", [128, 1])
        rstd = sb("rstd", [128, 1])
        xn = sb("xn", [128, D], BF16)
        xnT = sb("xnT", [128, D], BF16)
        rc = sb("rc", [128, 384])
        rs = sb("rs", [128, 384])
        ss8 = sb("ss8", [128, 8])
        rn8 = sb("rn8", [128, 8])
        qkrot = sb("qkrot", [128, 384], BF16)
        we = sb("we", [128, 768])
        xw = [sb(f"xw{k}", [128, 3, 768], BF16) for k in range(3)]
        lg2 = sb("lg2", [128, 16])
        bgf = [sb(f"bgf{k}", [128, 8]) for k in range(2)]
        ngkf = [sb(f"ngkf{k}", [128, 128]) for k in range(2)]
        low = sb("low", [128, 32])
        lowT = sb("lowT", [32, 128])
        gl = sb("gl", [128, 256])
        Tb = [sb(f"Tb{k}", [128, NB], BF16) for k in range(2)]
        Tf = [sb(f"Tf{k}", [128, NF]) for k in range(2)]
        wqB = sb("wqB", [128, 384])
        ss8b = sb("ss8b", [128, 8])
        rn8b = sb("rn8b", [128, 8])
        convw = sb("convw", [128, 2304], BF16)
        lgr = [sb(f"lgr{k}", [128, 16]) for k in range(2)]
        cv = sb("cv", [128, 768])
        gDN = sb("gDN", [128, 4, 128])
        gDA = sb("gDA", [128, 4, 128])
        gsm = sb("gsm", [128, 24])
        gcd = sb("gcd", [128, 2, 4])
        gN1 = sb("gN", [128, 4, 128], BF16)
        gG = sb("gG", [128, 4, 128])
        gA1 = sb("gA", [128, 4, 128], BF16)
        gP1 = sb("gP", [128, 4, 128], BF16)
        gN = [gN1, gN1]
        gA = [gA1, gA1]
        gP = [gP1, gP1]
        gkT = sb("gkT", [128, 2, 128], BF16)
        gkTm = sb("gkTm", [128, 2, 2, 128], BF16)
        gkdm = sb("gkdm", [128, 2, 256], BF16)
        gsm2 = sb("gsm2", [128, 8])
        lkdm = sb("lkdm", [128, 2, 128], BF16)
        gqT = sb("gqT", [128, 2, 128], BF16)
        gvk = sb("gvk", [128, 4, 128], BF16)
        guw = sb("guw", [128, 4, 128])
        gwb = sb("gwb", [128, 256], BF16)
        gwT = sb("gwT", [128, 2, 128], BF16)
        gqd = sb("gqd", [128, 256], BF16)
        gqdT = sb("gqdT", [128, 2, 128], BF16)
        gkd = sb("gkd", [128, 256], BF16)
        gqkm = sb("gqkm", [128, 4, 128], BF16)
        gvn = sb("gvn", [128, 256], BF16)
        gS = [[sb(f"gS{d}{p}", [128, 128]) for p in range(2)] for d in range(2)]
        gSb = [[sb(f"gSb{d}{p}", [128, 128], BF16) for p in range(2)] for d in range(2)]
        obcs = [sb(f"obc{k}", [128, 512]) for k in range(2)]
        ymbs = [sb(f"ymb{k}", [128, 256], BF16) for k in range(2)]
        ltmp = sb("ltmp", [128, 256])
        lex = sb("lex", [128, 3, 128])
        lcd = sb("lcd", [128, 2])
        lqe = sb("lqe", [128, 128], BF16)
        lke = sb("lke", [128, 128], BF16)
        lkd = sb("lkd", [128, 128], BF16)
        lqeT = sb("lqeT", [128, 128], BF16)
        lkeT = sb("lkeT", [128, 128], BF16)
        lqblk = sb("lqblk", [128, 4, 128], BF16)
        latt = sb("latt", [128, 4, 128], BF16)
        lS = [sb(f"lS{d}", [128, 256]) for d in range(2)]
        lSb = [sb(f"lSb{d}", [128, 256], BF16) for d in range(2)]
        os2 = sb("os2", [128, 512])
        osum = TT(os2.t[:, 0:256], "os2")
        sq = we
        ycat = xn
        ycatT = xnT
        junk = xn
        xc = cv
        wst = hin
        hout = hin
        pT = sb("pT", [128, 1024], BF16)
        wA = TT(pT.t[:, :].bitcast(F32), "pT")
        mqT = sb("mqT", [128, 2, 128], BF16)
        rs4 = sb("rs4", [128, 4])
        ya = osum
        kmtok = TT(qkrot.t[:, 0:256], "qkrot")

        ptr = ps("ptr", [128, 1024], BF16)
        pa = ps("pa", [128, 512])
        pb = ps("pb", [128, 512])
        p2 = ps("p2", [128, 512])
        p3 = ps("p3", [128, 512])
        sc = ps("sc", [128, 1024])
        p6 = ps("p6", [128, 512])
        sc0 = TT(sc.t[:, 0:512], "sc0")
        sc1 = TT(sc.t[:, 512:1024], "sc1")
        S.excl = {"ptr", "pa", "pb", "p2", "p3", "sc0", "sc1", "p6"}
        p6b = TT(p6.t[:, :].bitcast(BF16), "p6")
        rs4a = sb("rs4a", [128, 4])
        yaa = sb("yaa", [128, 256])

        ident = cst[:, C_ID:C_ID + 128]
        identb = cstb[:, 0:128]
        SHPb, SHNb, HPb, HNb = [cstb[:, 128 * k:128 * (k + 1)] for k in range(1, 5)]
        mLE, mLT, mGE, mGT, mBD = [cst[:, c:c + 128] for c in (C_LE, C_LT, C_GE, C_GT, C_BD)]
        SEL = [cst[:, C_SEL0:C_SEL0 + 128], cst[:, C_SEL1:C_SEL1 + 128]]
        RM4 = cst[:, C_RM4:C_RM4 + 4]
        BD4 = cst[:, C_BD4:C_BD4 + 256]
        ONE = cst[:, C_ONE:C_ONE + 1]

        dma("sp", cst[:, :], V(cst_d, "cst_d"))
        dma("sp", hin[:, 0:640], V(cstb_d, "cstb_d"))
        cp(cstb[:, :], hin[:, 0:640], eng="pool")
        memset(Vaug[:, :, :, :], 1.0)
        memset(Vmaug[:, :, :, :], 1.0)
        memset(gvn[:, :], 0.0)

        def rms_to_xnT(src, pt=None):
            pt = ptr if pt is None else pt
            act(junk[:, :], src, AF.Square, accum=ssq[:, :])
            rsqrt_(rstd[:, :], ssq[:, :], 1.0 / D, EPS)
            ts(xn[:, :], src, rstd[:, :], None, ALU.mult)
            for c in range(8):
                tr(pt[:, c * 128:(c + 1) * 128], xn[:, c * 128:(c + 1) * 128], identb)
            cp(xnT[:, :], pt[:, :], eng="act")

        def sigmoid_from(dst_e, src, n):
            act(dst_e, src, AF.Exp, scale=-1.0)
            ts(dst_e, dst_e, 1.0, None, ALU.add)
            recip(dst_e, dst_e)

        def gdn_core(d, qn, kn, vv, beta, g, o_out, first):
            M1, GS, NSm, AI = (mLE, mGT, mGT, mLE) if d == 0 else (mGE, mLT, mLT, mGE)
            if first:
                for p in range(2):
                    memset(gS[d][p][:, :], 0.0)
                    memset(gSb[d][p][:, :], 0.0)
            for p in range(2):
                tr(ptr[:, p * 128:(p + 1) * 128], kn[:, p * 128:(p + 1) * 128], identb)
                tr(ptr[:, 256 + p * 128:256 + (p + 1) * 128], qn[:, p * 128:(p + 1) * 128], identb)
            cp(gkT[:, :, :], ptr[:, 0:256].r("p (a b) -> p a b", a=2), eng="act")
            for q in range(2):
                ts(gkTm[:, :, q, :], ptr[:, 0:256].r("p (a b) -> p a b", a=2), mBD[:, 64 * q:64 * q + 1], None, ALU.mult)
            cp(gqT[:, :, :], ptr[:, 256:512].r("p (a b) -> p a b", a=2), eng="act")
            tt(gG[:, :, :], GS.bc(1, [128, 4, 128]), g.bc(2, [128, 4, 128]), ALU.mult)
            mm(p2[:, :], M1, gG[:, :, :].r("p a b -> p (a b)"))
            for h in range(4):
                mm(p3[:, h * 128:(h + 1) * 128], gG[:, h, :], M1)
            act(gDN[:, :, :].r("p a b -> p (a b)"), p2[:, :], AF.Exp)
            act(gDA[:, :, :].r("p a b -> p (a b)"), p3[:, :], AF.Exp)
            tt(gDN[:, :, :], gDN[:, :, :], NSm.bc(1, [128, 4, 128]), ALU.mult)
            tt(gDA[:, :, :], gDA[:, :, :], AI.bc(1, [128, 4, 128]), ALU.mult, eng="pool")
            mm(pa[:, 0:4], M1, g)
            mm(pa[:, 4:8], mBD, g)
            mm(pa[:, 8:12], SEL[0], g)
            mm(pa[:, 12:16], SEL[1], g)
            cp(gsm[:, 0:4], pa[:, 0:4])
            act(gsm[:, 4:8], pa[:, 0:4], AF.Exp)
            tt(gsm[:, 8:12], pa[:, 4:8], gsm[:, 0:4], ALU.subtract)
            act(gsm[:, 8:12], gsm[:, 8:12], AF.Exp)
            act(gcd[:, :, :].r("p a b -> p (a b)"), pa[:, 8:16], AF.Exp)
            ts(gsm[:, 12:16], beta, -1.0, None, ALU.mult)
            tt(gsm[:, 16:20], beta, gsm[:, 4:8], ALU.mult)
            for h in range(4):
                pr, hb = h // 2, (h % 2) * 64
                mm(p2[:, h * 128:(h + 1) * 128], gkT[:, pr, :], gkTm[:, pr, h % 2, :])
            for h in range(4):
                stt(gN[0][:, h, :], p2[:, h * 128:(h + 1) * 128], gsm[:, 12 + h:13 + h], gDN[:, h, :],
                    ALU.mult, ALU.mult)
            for h in range(4):
                tr(ptr[:, h * 128:(h + 1) * 128], gN[0][:, h, :], identb)
            cp(gA[0][:, :, :].r("p a b -> p (a b)"), ptr[:, 0:512], eng="act")
            tt(gP[0][:, :, :], gA[0][:, :, :], ident.bc(1, [128, 4, 128]), ALU.add)
            cur = 0
            pc = 0
            for lvl in range(5):
                nxt = 1 - cur
                last = lvl == 4
                for h in range(4):
                    mm(p2[:, h * 128:(h + 1) * 128], gA[cur][:, h, :], gN[cur][:, h, :])
                if not last:
                    for h in range(4):
                        mm(p3[:, h * 128:(h + 1) * 128], gN[cur][:, h, :], gA[cur][:, h, :])
                cp(gN[nxt][:, :, :].r("p a b -> p (a b)"), p2[:, :], eng="dve")
                if not last:
                    cp(gA[nxt][:, :, :].r("p a b -> p (a b)"), p3[:, :], eng="act")
                for h in range(4):
                    mm(pa[:, h * 128:(h + 1) * 128], gN[nxt][:, h, :], gP[pc][:, h, :])
                tt(gP[1 - pc][:, :, :].r("p a b -> p (a b)"), pa[:, :], gP[pc][:, :, :].r("p a b -> p (a b)"),
                   ALU.add)
                pc = 1 - pc
                cur = nxt
            Pm = gP[pc]
            tt(gvk[:, :, 0:64], vv.r("p (a b) -> p a b", a=4), beta.bc(2, [128, 4, 64]), ALU.mult, eng="pool")
            tt(gvk[:, :, 64:128], kn.r("p (a b) -> p a b", a=4), gsm[:, 16:20].bc(2, [128, 4, 64]), ALU.mult,
               eng="pool")
            for h in range(4):
                mm(p3[:, h * 128:(h + 1) * 128], Pm[:, h, :], gvk[:, h, :])
            cp(guw[:, :, :].r("p a b -> p (a b)"), p3[:, :], eng="act")
            cp(gwb[:, :].r("p (a b) -> p a b", a=4), guw[:, :, 64:128], eng="dve")
            for p in range(2):
                tr(ptr[:, 512 + p * 128:512 + (p + 1) * 128], gwb[:, p * 128:(p + 1) * 128], identb)
            cp(gwT[:, :, :], ptr[:, 512:768].r("p (a b) -> p a b", a=2), eng="act")
            tt(gqd[:, :].r("p (a b) -> p a b", a=4), qn.r("p (a b) -> p a b", a=4),
               gsm[:, 4:8].bc(2, [128, 4, 64]), ALU.mult, eng="pool")
            for p in range(2):
                tr(ptr[:, p * 128:(p + 1) * 128], gqd[:, p * 128:(p + 1) * 128], identb)
            cp(gqdT[:, :, :], ptr[:, 0:256].r("p (a b) -> p a b", a=2), eng="act")
            for c in range(2):
                ts(gsm2[:, 4 * c:4 * c + 4], gsm[:, 8:12], mBD[:, 64 * c:64 * c + 1], None, ALU.mult)
                tt(gkdm[:, c, :].r("p (a b) -> p a b", a=4), kn.r("p (a b) -> p a b", a=4),
                   gsm2[:, 4 * c:4 * c + 4].bc(2, [128, 4, 64]), ALU.mult, eng="pool")
            for h in range(4):
                pr, hb = h // 2, (h % 2) * 64
                mm(p2[:, h * 128:(h + 1) * 128], gkTm[:, pr, h % 2, :], gqT[:, pr, :])
            tt(gqkm[:, :, :].r("p a b -> p (a b)"), p2[:, :], gDA[:, :, :].r("p a b -> p (a b)"), ALU.mult)
            for c in ((0, 1) if d == 0 else (1, 0)):
                r0, r1 = 64 * c, 64 * c + 64
                for p in range(2):
                    mm(p3[:, p * 128:(p + 1) * 128], gwT[:, p, :], gSb[d][p][:, :])
                tt(gvn[r0:r1, :].r("p (a b) -> p a b", a=4), guw[r0:r1, :, 0:64],
                   p3[r0:r1, 0:256].r("p (a b) -> p a b", a=4), ALU.subtract)
                for h in range(4):
                    pr, hb = h // 2, (h % 2) * 64
                    mm(pa[:, h * 64:(h + 1) * 64], gqdT[:, pr, :], gSb[d][pr][:, hb:hb + 64], start=True, stop=False)
                    mm(pa[:, h * 64:(h + 1) * 64], gqkm[:, h, :], gvn[:, h * 64:(h + 1) * 64],
                       start=False, stop=True)
                cp(o_out[r0:r1, :], pa[r0:r1, 0:256], eng="act")
                for p in range(2):
                    mm(p2[:, p * 128:(p + 1) * 128], gkdm[:, c, p * 128:(p + 1) * 128],
                       gvn[:, p * 128:(p + 1) * 128])
                for p in range(2):
                    for q in range(2):
                        h = 2 * p + q
                        b0, b1 = 64 * q, 64 * q + 64
                        stt(gS[d][p][b0:b1, b0:b1], gS[d][p][b0:b1, b0:b1], gcd[b0:b1, c, h:h + 1],
                            p2[b0:b1, p * 128 + b0:p * 128 + b1], ALU.mult, ALU.add)
                    cp(gSb[d][p][:, :], gS[d][p][:, :], eng="dve")

        def gla_core(d, lq, lk, lv, ngk, o_out, first):
            M1, GS, AI = (mLE, mGT, mLE) if d == 0 else (mGE, mLT, mGE)
            if first:
                memset(lS[d][:, :], 0.0)
                memset(lSb[d][:, :], 0.0)
            mm(sc1[:, 0:128], M1, ngk)
            mm(sc1[:, 128:256], GS, ngk)
            for c in range(2):
                mm(sc1[:, 256 + c:257 + c], ngk, mBD[:, 64 * c:64 * c + 1])
            act(lex[:, 0, :], sc1[:, 0:128], AF.Exp, scale=-1.0 / 16)
            act(lex[:, 1, :], sc1[:, 0:128], AF.Exp, scale=1.0 / 16)
            act(lex[:, 2, :], sc1[:, 128:256], AF.Exp, scale=-1.0 / 16)
            act(lcd[:, :], sc1[:, 256:258], AF.Exp, scale=-1.0 / 16)
            tt(lqe[:, :], lq, lex[:, 0, :], ALU.mult)
            tt(lke[:, :], lk, lex[:, 1, :], ALU.mult, eng="pool")
            tt(lkd[:, :], lk, lex[:, 2, :], ALU.mult, eng="pool")
            for c in range(2):
                ts(lkdm[:, c, :], lkd[:, :], mBD[:, 64 * c:64 * c + 1], None, ALU.mult)
            tr(ptr[:, 768:896], lqe[:, :], identb)
            tr(ptr[:, 896:1024], lke[:, :], identb)
            cp(lqeT[:, :], ptr[:, 768:896], eng="act")
            cp(lkeT[:, :], ptr[:, 896:1024], eng="act")
            tt(lqblk[:, :, :], ptr[:, 768:896].bc(1, [128, 4, 128]), RM4.bc(2, [128, 4, 128]), ALU.mult)
            mm(pb[:, :], lkeT[:, :], lqblk[:, :, :].r("p a b -> p (a b)"))
            tt(latt[:, :, :], pb[:, :].r("p (a b) -> p a b", a=4), AI.bc(1, [128, 4, 128]), ALU.mult)
            for c in ((0, 1) if d == 0 else (1, 0)):
                r0, r1 = 64 * c, 64 * c + 64
                for h in range(4):
                    mm(sc1[:, h * 64:(h + 1) * 64], lqeT[:, :], lSb[d][:, h * 64:(h + 1) * 64], start=True, stop=False)
                    mm(sc1[:, h * 64:(h + 1) * 64], latt[:, h, :], lv[:, h * 64:(h + 1) * 64],
                       start=False, stop=True)
                cp(o_out[r0:r1, :], sc1[r0:r1, 0:256], eng="act")
                mm(sc1[:, 256:512], lkdm[:, c, :], lv)
                tt(ltmp[:, :], sc1[:, 256:512], BD4, ALU.mult)
                stt(lS[d][:, :], lS[d][:, :], lcd[:, c:c + 1], ltmp[:, :], ALU.mult, ALU.add)
                cp(lSb[d][:, :], lS[d][:, :], eng="dve")

        for l in range(depth):
            last_layer = (l == depth - 1)
            dma("sp", ngt[:, :], V(ng_d[l].rearrange("(c p) -> p c", p=128), "ng_d"), slow=True)
            dma("sp", mgt[:, :], V(mg_d[l].rearrange("(c p) -> p c", p=128), "mg_d"), slow=True)
            dma("sp", pv[:, :], V(pv_d[l].partition_broadcast(128), "pv_d"))
            dma("sp", wup[:, :], V(wup_d[l], "wup_d"))
            for c in range(8):
                for (j0, j1) in ((0, 1024), (1024, 2048), (2048, 3072), (3072, 3120)):
                    dma("sp", wst[:, 0:j1 - j0], V(win_d[l, c * 128:(c + 1) * 128, j0:j1], "win_d"))
                    ts(win[:, c, j0:j1], wst[:, 0:j1 - j0], ngt[:, c:c + 1], None, ALU.mult)
                dma("sp", wst[:, 0:1024], V(wout_d[l, c * 128:(c + 1) * 128, :], "wout_d"))
                cp(wout[:, c, :], wst[:, 0:1024], eng="pool")
                dma("sp", wst[:, 0:512], V(wmem_d[l, c * 128:(c + 1) * 128, :], "wmem_d"))
                ts(wmem[:, c, :], wst[:, 0:512], mgt[:, c:c + 1], None, ALU.mult)
            for j in range(3):
                dma("sp", wst[:, 0:768], V(cw_d[l, j * 768:(j + 1) * 768].partition_broadcast(128), "cw_d"))
                cp(convw[:, j * 768:(j + 1) * 768], wst[:, 0:768], eng="pool")
            act(negA[:, :], pv[:, P_ALOG:P_ALOG + 8], AF.Exp)
            ts(negA[:, :], negA[:, :], -1.0, None, ALU.mult)

            for s in range(NS):
                Sq = seqs[s]
                n = Sq // 128
                R0 = roff[s]
                src_h = x_d if l == 0 else hbuf

                for mt in range(2):
                    dma("sp", hin[:, :], mem_d[s * 256 + mt * 128:s * 256 + (mt + 1) * 128, :])
                    rms_to_xnT(hin[:, :])
                    for c in range(8):
                        mm(pa[:, :], xnT[:, c * 128:(c + 1) * 128], wmem[:, c, :], start=(c == 0), stop=(c == 7))
                    cp(kmtok[:, :], pa[:, 0:256], eng="act")
                    cp(Vmaug[:, mt, :, 0:64], pa[:, 256:512].r("p (a b) -> p a b", a=4), eng="dve")
                    for p in range(2):
                        tr(ptr[:, p * 128:(p + 1) * 128], kmtok[:, p * 128:(p + 1) * 128], identb)
                    for q in range(2):
                        ts(kmT[:, :, q, mt * 128:(mt + 1) * 128], ptr[:, 0:256].r("p (a b) -> p a b", a=2),
                           mBD[:, 64 * q:64 * q + 1], None, ALU.mult)

                def P_stage(i, phase):
                    par = i % 2
                    rows = slice(R0 + i * 128, R0 + (i + 1) * 128)
                    tb, tf = Tb[par], Tf[par]
                    if phase == 0:
                        dma("sp", hin[:, :], src_h.k(R0 // 128 + i)[rows, :])
                        dma("sp", rc[:, :], rc_d[i * 128:(i + 1) * 128, :])
                        dma("sp", rs[:, :], rs_d[i * 128:(i + 1) * 128, :])
                        rms_to_xnT(hin[:, :], p6b)
                    groups = [(0, 512), (512, 1024), (1024, 1536), (1536, 2048), (2048, 2320), (2320, 2832),
                              (2832, 3120)]
                    for gi, (c0, c1) in enumerate(groups):
                        if (gi in (3, 4)) != (phase == 0):
                            continue
                        pp = p6 if gi % 2 == 0 else sc0
                        w = c1 - c0
                        for c in range(8):
                            mm(pp[:, 0:w], xnT[:, c * 128:(c + 1) * 128], win[:, c, c0:c1], start=(c == 0),
                               stop=(c == 7))
                        if gi == 0:
                            act(wA[:, 0:384], pp[:, 0:384], AF.Square)
                            red(ss8b[:, 0:6], wA[:, 0:384].r("p (a b) -> p a b", a=6))
                            rsqrt_(rn8b[:, 0:6], ss8b[:, 0:6], 1.0 / 64, EPS)
                            tt(wqB[:, :].r("p (a b) -> p a b", a=6), pp[:, 0:384].r("p (a b) -> p a b", a=6),
                               rn8b[:, 0:6].bc(2, [128, 6, 64]), ALU.mult)
                            cp(Vaug[:, i, :, 0:64], pp[:, 384:512].r("p (a b) -> p a b", a=2), eng="dve")
                            tt(wqB[:, :], wqB[:, :], pv[:, P_GQK:P_GQK + 384], ALU.mult, eng="pool")
                            v5 = lambda t: t[:, 0:384].r("p (a b c) -> p a b c", a=12, b=2)
                            tt(v5(wA)[:, :, 0, :], v5(wqB)[:, :, 1, :], v5(rs)[:, :, 0, :], ALU.mult, eng="pool")
                            tt(v5(wA)[:, :, 1, :], v5(wqB)[:, :, 0, :], v5(rs)[:, :, 1, :], ALU.mult, eng="pool")
                            tt(wqB[:, :], wqB[:, :], rc[:, :], ALU.mult, eng="pool")
                            tt(tb[:, B_Q:B_Q + 256], wqB[:, 0:256], wA[:, 0:256], ALU.add)
                            tt(qkrot[:, 0:128], wqB[:, 256:384], wA[:, 256:384], ALU.add)
                            tr(p6b[:, 0:128], qkrot[:, 0:128], identb)
                            cp(KT[:, i * 128:(i + 1) * 128], p6b[:, 0:128], eng="act")
                        elif gi in (1, 2):
                            sigmoid_from(wA[:, 0:512], pp[:, 0:512], 512)
                            off = B_GA if gi == 1 else B_GD
                            tt(tb[:, off:off + 512], pp[:, 0:512], wA[:, 0:512], ALU.mult)
                        elif gi == 3:
                            for k3 in range(3):
                                tt(xw[i % 3][:, k3, 0:512], pp[:, 0:512], convw[:, 768 * k3:768 * k3 + 512], ALU.mult)
                        elif gi == 4:
                            cp(lgr[par][:, :], pp[:, 256:272], eng="dve")
                            for k3 in range(3):
                                tt(xw[i % 3][:, k3, 512:768], pp[:, 0:256], convw[:, 768 * k3 + 512:768 * (k3 + 1)],
                                   ALU.mult)
                        elif gi == 5:
                            act(tf[:, F_LQ:F_LQ + 128], pp[:, 0:128], AF.Copy, scale=32.0 ** -0.5)
                            cp(tf[:, F_LK:F_LK + 128], pp[:, 128:256], eng="act")
                            cp(tb[:, B_LV:B_LV + 256], pp[:, 256:512], eng="act")
                        else:
                            cp(tb[:, B_MQ:B_MQ + 256], pp[:, 0:256], eng="act")
                            cp(low[:, :], pp[:, 256:288], eng="dve")
                            tr(sc0[0:32, 0:128], low[:, :], ident)
                            cp(lowT[:, :], sc0[0:32, 0:128], eng="act")
                            mm(sc0[:, 128:384], lowT[:, :], wup[:, :])
                            tt(gl[:, :], sc0[:, 128:384], pv[:, P_GLB:P_GLB + 256], ALU.add)
                            act(gl[:, :], gl[:, :], AF.Exp, scale=-1.0)
                            act(ngkf[par][:, :], gl[:, 0:128], AF.Ln, bias=1.0)
                            act(tf[:, F_NGK:F_NGK + 128], gl[:, 128:256], AF.Ln, bias=1.0)

                    if phase == 1:
                        lg = lgr[par]
                        sigmoid_from(lg2[:, 0:8], lg[:, 0:8], 8)
                        tt(lg2[:, 8:16], lg[:, 8:16], pv[:, P_DTB:P_DTB + 8], ALU.add)
                        act(lg2[:, 8:16], lg2[:, 8:16], AF.Exp)
                        act(lg2[:, 8:16], lg2[:, 8:16], AF.Ln, bias=1.0)
                        tt(lg2[:, 8:16], lg2[:, 8:16], negA[:, :], ALU.mult)
                        cp(bgf[par][:, 0:4], lg2[:, 0:4], eng="pool")
                        cp(bgf[par][:, 4:8], lg2[:, 8:12], eng="pool")
                        cp(tf[:, F_BG:F_BG + 4], lg2[:, 4:8], eng="pool")
                        cp(tf[:, F_BG + 4:F_BG + 8], lg2[:, 12:16], eng="pool")

                def post(i, extra=None):
                    par = i % 2
                    rows = slice(R0 + i * 128, R0 + (i + 1) * 128)
                    tb, tf = Tb[par], Tf[par]
                    S.capture()
                    for (c0, c1, pp) in ((0, 512, p2), (512, 768, p3)):
                        w = c1 - c0
                        ops = [(SHPb, xw[i % 3][:, 0, c0:c1]), (identb, xw[i % 3][:, 1, c0:c1]),
                               (SHNb, xw[i % 3][:, 2, c0:c1])]
                        if i > 0:
                            ops.append((HPb, xw[(i - 1) % 3][:, 0, c0:c1]))
                        if i < n - 1:
                            ops.append((HNb, xw[(i + 1) % 3][:, 2, c0:c1]))
                        for k3, (lh, rh) in enumerate(ops):
                            mm(pp[:, 0:w], lh, rh, start=(k3 == 0), stop=(k3 == len(ops) - 1))
                        sigmoid_from(we[:, c0:c1], pp[:, 0:w], w)
                        tt(cv[:, c0:c1], pp[:, 0:w], we[:, c0:c1], ALU.mult)
                    act(sq[:, 0:512], cv[:, 0:512], AF.Square)
                    red(ss8[:, :], sq[:, 0:512].r("p (a b) -> p a b", a=8))
                    rsqrt_(rn8[:, :], ss8[:, :], 1.0, EPS)
                    ts(rn8[:, 0:4], rn8[:, 0:4], 0.125, None, ALU.mult)
                    tt(tb[:, B_GQ:B_GQ + 512].r("p (a b) -> p a b", a=8), cv[:, 0:512].r("p (a b) -> p a b", a=8),
                       rn8[:, :].bc(2, [128, 8, 64]), ALU.mult)
                    cp(tb[:, B_GV:B_GV + 256], cv[:, 512:768], eng="act")
                    gdn_core(0, tb[:, B_GQ:B_GQ + 256], tb[:, B_GK:B_GK + 256], tb[:, B_GV:B_GV + 256],
                             bgf[par][:, 0:4], bgf[par][:, 4:8], tf[:, F_OB:F_OB + 256], first=(i == 0))
                    LG1 = S.end_capture()
                    S.capture()
                    gla_core(0, tf[:, F_LQ:F_LQ + 128], tf[:, F_LK:F_LK + 128], tb[:, B_LV:B_LV + 256],
                             ngkf[par][:, :], tf[:, F_OC:F_OC + 256], first=(i == 0))
                    LL1 = S.end_capture()
                    S.merge([LG1, LL1] + ([extra] if extra else []))
                    dma("pool", stb.k(R0 // 128 + i)[rows, :], tb[:, :])
                    dma("pool", stf.k(R0 // 128 + i)[rows, :], tf[:, :])

                P_stage(0, 0)
                for i in range(n + 1):
                    LP = None
                    if i < n:
                        S.capture()
                        P_stage(i, 1)
                        if i + 1 < n:
                            P_stage(i + 1, 0)
                        LP = S.end_capture()
                    if i >= 1:
                        post(i - 1, LP)
                    elif LP:
                        S.merge([LP])

                if last_layer and final:
                    dma("sp", cv[:, 0:768], V(fng_d[0:768].partition_broadcast(128), "fng_d"))
                    dma("sp", we[:, 512:768], V(fng_d[768:1024].partition_broadcast(128), "fng_d"))
                def tail(t):
                    rows_t = slice(R0 + t * 128, R0 + (t + 1) * 128)
                    dma("sp", hin[:, :], src_h.k(R0 // 128 + t)[rows_t, :])
                    tt(os2[:, :], obcs[t % 2][:, :], Tf[t % 2][:, F_OB:F_OB + 512], ALU.add)
                    act(sq[:, 0:512], os2[:, :], AF.Square)
                    red(ss8[:, 0:8], sq[:, 0:512].r("p (a b) -> p a b", a=8))
                    rsqrt_(rn8[:, 0:8], ss8[:, 0:8], 1.0 / 64, EPS)
                    tt(os2[:, :].r("p (a b) -> p a b", a=8), os2[:, :].r("p (a b) -> p a b", a=8),
                       rn8[:, 0:8].bc(2, [128, 8, 64]), ALU.mult)
                    tt(os2[:, :], os2[:, :], pv[:, P_GOG:P_GOG + 512], ALU.mult, eng="pool")
                    tt(ycat[:, 256:768], os2[:, :], Tb[t % 2][:, B_GD:B_GD + 512], ALU.mult)
                    for c in range(8):
                        src_c = ycat[:, c * 128:(c + 1) * 128] if c < 6 else ymbs[t % 2][:, (c - 6) * 128:(c - 5) * 128]
                        tr(p6b[:, c * 128:(c + 1) * 128], src_c, identb)
                    cp(ycatT[:, :], p6b[:, :], eng="act")
                    for half, pp in ((0, sc0), (1, p6)):
                        for c in range(8):
                            mm(pp[:, :], ycatT[:, c * 128:(c + 1) * 128], wout[:, c, half * 512:(half + 1) * 512],
                               start=(c == 0), stop=(c == 7))
                        tt(hout[:, half * 512:(half + 1) * 512], pp[:, :], hin[:, half * 512:(half + 1) * 512], ALU.add)
                    if last_layer and final:
                        act(junk[:, :], hout[:, :], AF.Square, accum=ssq[:, :])
                        rsqrt_(rstd[:, :], ssq[:, :], 1.0 / D, EPS)
                        ts(hout[:, :], hout[:, :], rstd[:, :], None, ALU.mult)
                        tt(hout[:, 0:768], hout[:, 0:768], cv[:, 0:768], ALU.mult, eng="pool")
                        tt(hout[:, 768:1024], hout[:, 768:1024], we[:, 512:768], ALU.mult, eng="pool")
                        dma("pool", y_d.k(R0 // 128 + t)[rows_t, :], hout[:, :])
                    elif last_layer:
                        dma("pool", y_d.k(R0 // 128 + t)[rows_t, :], hout[:, :])
                    else:
                        dma("pool", hbuf.k(R0 // 128 + t)[rows_t, :], hout[:, :])
                nkt = n
                prev = None
                for i in range(n - 1, -1, -1):
                    rows = slice(R0 + i * 128, R0 + (i + 1) * 128)
                    Tb2, Tf2 = Tb[i % 2], Tf[i % 2]
                    dma("sp", Tb2[:, :], stb.k(R0 // 128 + i)[rows, :])
                    dma("sp", Tf2[:, :], stf.k(R0 // 128 + i)[rows, :])
                    first = (i == n - 1)
                    S.capture()
                    gdn_core(1, Tb2[:, B_GQ:B_GQ + 256], Tb2[:, B_GK:B_GK + 256], Tb2[:, B_GV:B_GV + 256],
                             Tf2[:, F_BG:F_BG + 4], Tf2[:, F_BG + 4:F_BG + 8], obcs[i % 2][:, 0:256], first=first)
                    LG = S.end_capture()
                    S.capture()
                    gla_core(1, Tf2[:, F_LQ:F_LQ + 128], Tf2[:, F_LK:F_LK + 128], Tb2[:, B_LV:B_LV + 256],
                             Tf2[:, F_NGK:F_NGK + 128], obcs[i % 2][:, 256:512], first=first)
                    for p in range(2):
                        tr(ptr[:, 768 + p * 128:768 + (p + 1) * 128], Tb2[:, B_MQ + p * 128:B_MQ + (p + 1) * 128], identb)
                    cp(mqT[:, :, :], ptr[:, 768:1024].r("p (a b) -> p a b", a=2), eng="act")
                    for hp in range(2):
                        for hq in range(2):
                            h = 2 * hp + hq
                            pr, _hb = h // 2, (h % 2) * 64
                            for kt in range(2):
                                mm(sc1[:, (hq * 2 + kt) * 128:(hq * 2 + kt + 1) * 128],
                                   kmT[:, pr, h % 2, kt * 128:(kt + 1) * 128], mqT[:, pr, :])
                        act(pT[:, 512:1024], sc1[:, :], AF.Exp, scale=0.125)
                        for hq in range(2):
                            h = 2 * hp + hq
                            for kt in range(2):
                                mm(pb[:, h * 65:(h + 1) * 65], pT[:, 512 + (hq * 2 + kt) * 128:512 + (hq * 2 + kt + 1) * 128],
                                   Vmaug[:, kt, h, 0:65], start=(kt == 0), stop=(kt == 1))
                    recip(rs4[:, :], pb[:, 0:260].r("p (a b) -> p a b", a=4)[:, :, 64])
                    tt(ymbs[i % 2][:, :].r("p (a b) -> p a b", a=4), pb[:, 0:260].r("p (a b) -> p a b", a=4)[:, :, 0:64],
                       rs4[:, :].bc(2, [128, 4, 64]), ALU.mult)
                    tt(ymbs[i % 2][:, :], ymbs[i % 2][:, :], Tb2[:, B_GM:B_GM + 256], ALU.mult, eng="pool")
                    LL = S.end_capture()
                    S.capture()
                    if prev is not None:
                        tail(prev)
                    for g2 in range(2):
                        tr(p6b[:, 768 + g2 * 128:768 + (g2 + 1) * 128], Tb2[:, B_Q + g2 * 128:B_Q + (g2 + 1) * 128], identb)
                    for h in range(4):
                        ts(qm[:, h, :], p6b[:, 768 + (h % 2) * 128:768 + (h % 2 + 1) * 128],
                           mBD[:, 64 * (h // 2):64 * (h // 2) + 1], None, ALU.mult)
                    for kt in range(nkt):
                        mm(sc0[:, :], KT[:, kt * 128:(kt + 1) * 128], qm[:, :, :].r("p a b -> p (a b)"))
                        act(pT[:, 0:512], sc0[:, :], AF.Exp, scale=0.125)
                        for h in range(4):
                            mm(p6[:, h * 65:(h + 1) * 65], pT[:, h * 128:(h + 1) * 128], Vaug[:, kt, h // 2, 0:65],
                               start=(kt == 0 and h == 0), stop=(kt == nkt - 1), skip=True)
                    recip(rs4a[:, :], p6[:, 0:260].r("p (a b) -> p a b", a=4)[:, :, 64])
                    tt(yaa[:, :].r("p (a b) -> p a b", a=4), p6[:, 0:260].r("p (a b) -> p a b", a=4)[:, :, 0:64],
                       rs4a[:, :].bc(2, [128, 4, 64]), ALU.mult)
                    tt(ycat[:, 0:256], yaa[:, :], Tb2[:, B_GA:B_GA + 256], ALU.mult, eng="pool")
                    LA = S.end_capture()
                    S.merge([LG, LL, LA])
                    prev = i
                tail(prev)
        if _os.environ.get("KDESC"):
            lo, hi = [int(v) for v in _os.environ["KDESC"].split(",")]
            for dd in S.desc:
                if lo <= dd[0] <= hi:
                    print("OP", dd)
        if _os.environ.get("KLIMIT"):
            print("TOTAL OPS", S.total)
        S.emit(st)
    return nc


def _consts():
    c = np.zeros((128, NCST), np.float32)
    p = np.arange(128)[:, None]
    f = np.arange(128)[None, :]
    bd = (p // 64 == f // 64)
    c[:, C_ID:C_ID + 128] = (p == f)
    c[:, C_LE:C_LE + 128] = bd & (p <= f)
    c[:, C_LT:C_LT + 128] = bd & (p < f)
    c[:, C_GE:C_GE + 128] = bd & (p >= f)
    c[:, C_GT:C_GT + 128] = bd & (p > f)
    c[:, C_BD:C_BD + 128] = bd
    c[:, C_SEL0:C_SEL0 + 128] = (p < 64) & (f >= 0)
    c[:, C_SEL1:C_SEL1 + 128] = (p >= 64) & (f >= 0)
    c[:, C_RM4:C_RM4 + 4] = (p // 32 == np.arange(4)[None, :])
    c[:, C_BD4:C_BD4 + 256] = (p // 32 == np.arange(256)[None, :] // 64)
    c[:, C_ONE] = 1.0
    cb = np.zeros((128, 640), np.float32)
    cb[:, 0:128] = (p == f)
    cb[:, 128:256] = (p == f - 1)
    cb[:, 256:384] = (p == f + 1)
    cb[:, 384:512] = (p == 127) & (f == 0)
    cb[:, 512:640] = (p == 0) & (f == 127)
    return c, cb


def _rope_tables(smax):
    t = np.arange(smax)
    pos = np.stack([t // 64, t % 64], -1).astype(np.float32)
    inv = (10000.0 ** (-np.arange(0, 32, 2, dtype=np.float32) / 32)).astype(np.float32)
    ang = pos[:, :, None] * inv[None, None, :]
    cos, sin = np.cos(ang).astype(np.float32), np.sin(ang).astype(np.float32)
    C = np.zeros((smax, 2, 2, 16), np.float32)
    Sg = np.zeros((smax, 2, 2, 16), np.float32)
    C[:, :, 0, :] = cos
    C[:, :, 1, :] = cos
    Sg[:, :, 0, :] = -sin
    Sg[:, :, 1, :] = sin
    C = np.tile(C.reshape(smax, 1, 64), (1, 6, 1)).reshape(smax, 384)
    Sg = np.tile(Sg.reshape(smax, 1, 64), (1, 6, 1)).reshape(smax, 384)
    return np.ascontiguousarray(C), np.ascontiguousarray(Sg)


def _col_perm():
    o = dict(a_q=0, a_k=256, a_v=384, a_gate=512, d_qkv=768, d_beta=1536, d_alpha=1544, d_gate=1552,
             l_q=1808, l_k=1936, l_v=2064, l_low=2320, l_gate=2352, m_q=2608, m_gate=2864)
    r = lambda a, n: list(range(a, a + n))
    aq = []
    for g in range(2):
        for kv in range(2):
            h = kv * 2 + g
            aq += r(o["a_q"] + h * 64, 64)
    perm = (aq + r(o["a_k"], 128) + r(o["a_v"], 128)
            + r(o["a_gate"], 256) + r(o["m_gate"], 256)
            + r(o["d_gate"], 256) + r(o["l_gate"], 256)
            + r(o["d_qkv"], 512)
            + r(o["d_qkv"] + 512, 256) + r(o["d_beta"], 8) + r(o["d_alpha"], 8)
            + r(o["l_q"], 128) + r(o["l_k"], 128) + r(o["l_v"], 256)
            + r(o["m_q"], 256) + r(o["l_low"], 32))
    assert len(perm) == NCOL and len(set(perm)) == NCOL
    return np.array(perm)


def _prep_shared(inp, depth):
    f = lambda a: np.ascontiguousarray(np.asarray(a, dtype=np.float32))
    perm = _col_perm()
    w_in = f(np.asarray(inp["w_in"])[:depth][:, :, perm])
    pvec = np.zeros((depth, NPV), np.float32)
    wup = np.zeros((depth, 32, 256), np.float32)
    for l in range(depth):
        pvec[l, P_GQK:P_GQK + 256] = np.tile(np.asarray(inp["att_q_norm_g"])[l], 4)
        pvec[l, P_GQK + 256:P_GQK + 384] = np.tile(np.asarray(inp["att_k_norm_g"])[l], 2)
        pvec[l, P_ALOG:P_ALOG + 8] = np.asarray(inp["gdn_a_log"])[l].reshape(-1)
        pvec[l, P_DTB:P_DTB + 8] = np.asarray(inp["gdn_dt_bias"])[l].reshape(-1)
        pvec[l, P_GOG:P_GOG + 256] = np.tile(np.asarray(inp["gdn_out_norm_g"])[l], 4)
        pvec[l, P_GLB:P_GLB + 256] = np.asarray(inp["gla_b_gate"])[l].reshape(-1)
        pvec[l, P_GLG:P_GLG + 256] = np.tile(np.asarray(inp["gla_out_norm_g"])[l], 4)
        for d in range(2):
            wup[l, d * 16:(d + 1) * 16, d * 128:(d + 1) * 128] = np.asarray(inp["gla_w_gate_up"])[l, d]
    return dict(w_in=w_in, w_out=f(np.asarray(inp["w_out"])[:depth]), w_mem=f(np.asarray(inp["w_mem_kv"])[:depth]),
                norm_g=f(np.asarray(inp["norm_g"])[:depth]), mem_g=f(np.asarray(inp["mem_norm_g"])[:depth]),
                pvec=pvec, wup=wup, fng=f(inp["final_norm_g"]), cst=_consts()[0], cstb=_consts()[1],
                convw=f(np.asarray(inp["gdn_conv_w"])[:depth].reshape(depth, -1)))


def kernel(**inputs):
    depth = 4
    xp = np.asarray(inputs["x_prompt"], np.float32)
    xs = np.asarray(inputs["x_sample"], np.float32)
    mp = np.asarray(inputs["mem_prompt"], np.float32)
    ms = np.asarray(inputs["mem_sample"], np.float32)
    Sp, Ss = xp.shape[1], xs.shape[1]
    seqs = [Sp, Ss, Ss]
    shared = _prep_shared(inputs, depth)
    C, Sg = _rope_tables(max(seqs))
    shared["ropec"], shared["ropes"] = C, Sg
    nc = _build(seqs, depth)
    in_maps = []
    for c in range(8):
        m = dict(shared)
        m["x"] = np.ascontiguousarray(np.concatenate([xp[c], xs[2 * c], xs[2 * c + 1]], 0))
        m["mem"] = np.ascontiguousarray(np.concatenate([mp[c], ms[2 * c], ms[2 * c + 1]], 0))
        in_maps.append(m)
    res = run_bass_kernel_spmd(nc, in_maps, core_ids=list(range(8)))
    yp = np.stack([res.results[c]["y"][:Sp] for c in range(8)], 0)
    ysm = np.stack([res.results[c // 2]["y"][Sp + (c % 2) * Ss:Sp + (c % 2 + 1) * Ss] for c in range(16)], 0)
    return (yp.astype(np.float32), ysm.astype(np.float32))
```

```python
from contextlib import ExitStack
import numpy as np
import concourse.bass as bass
import concourse.mybir as mybir
from concourse.bass_utils import run_bass_kernel_spmd

F32 = mybir.dt.float32
BF16 = mybir.dt.bfloat16
AF = mybir.ActivationFunctionType
ALU = mybir.AluOpType
AX = mybir.AxisListType

D = 1024
NCOL = 3120
EPS = 1e-6
NB = 2560
NF = 904
B_GA, B_GM, B_GD, B_GL, B_MQ, B_GQ, B_GK, B_GV, B_LV, B_Q = 0, 256, 512, 768, 1024, 1280, 1536, 1792, 2048, 2304
F_BG, F_LQ, F_LK, F_NGK, F_OB, F_OC = 0, 8, 136, 264, 392, 648
P_GQK, P_ALOG, P_DTB, P_GOG, P_GLG, P_GLB = 0, 384, 392, 400, 656, 912
NPV = 1168
C_ID, C_LE, C_LT, C_GE, C_GT, C_BD, C_SEL0, C_SEL1 = [128 * k for k in range(8)]
C_RM4 = 128 * 8
C_BD4 = C_RM4 + 4
C_ONE = C_BD4 + 256
NCST = C_ONE + 1


class _Op:
    __slots__ = ("eng", "idx", "fn", "waits", "signal", "kind", "dma_m", "tok")

    def __init__(self, eng, idx, fn, kind):
        self.eng = eng
        self.idx = idx
        self.fn = fn
        self.kind = kind
        self.waits = []
        self.signal = False
        self.dma_m = None
        self.tok = None


class Sched:
    ENGS = ("pe", "act", "dve", "pool", "sp")

    def __init__(self, nc, R=8, epoch=16000):
        self.nc = nc
        self.R = R
        self.epoch = epoch
        self.streams = {e: [] for e in self.ENGS}
        self.ndma = {e: 0 for e in self.ENGS}
        self.dma_ops = {e: [] for e in self.ENGS}
        self.last_w = {}
        self.readers = {}
        self.waited = {e: {p: -1 for p in self.ENGS} for e in self.ENGS}
        self.waited_dma = {e: {} for e in self.ENGS}

    def capture(self):
        self._cap = []

    def end_capture(self):
        lst = self._cap
        self._cap = None
        return lst

    def merge(self, lists):
        pos = [0] * len(lists)
        while True:
            best, bf = -1, 2.0
            for j, l in enumerate(lists):
                if pos[j] < len(l):
                    f = pos[j] / len(l)
                    if f < bf:
                        best, bf = j, f
            if best < 0:
                break
            self._add(*lists[best][pos[best]])
            pos[best] += 1

    def _add(self, eng, fn, reads, writes, kind):
        if getattr(self, "_cap", None) is not None:
            self._cap.append((eng, fn, list(reads), list(writes), kind))
            return None
        self.total = getattr(self, "total", 0) + 1
        if getattr(self, "desc", None) is not None:
            import sys as _sys
            fr = _sys._getframe(3)
            self.desc.append((self.total, eng, kind, fr.f_lineno, fr.f_back.f_lineno))
        if self.total > getattr(self, "limit", 1 << 60) or self.total in getattr(self, "skip", ()):
            return None
        excl = getattr(self, "excl", ())
        if excl and eng != "pe":
            extra = [k for k in reads if k in excl and k not in writes]
            if extra:
                writes = list(writes) + extra
        st = self.streams[eng]
        op = _Op(eng, len(st), fn, kind)
        st.append(op)
        deps = []
        for k in reads:
            w = self.last_w.get(k)
            if w is not None:
                deps.append((w, True))
        for k in writes:
            w = self.last_w.get(k)
            if w is not None:
                deps.append((w, False))
            for r in self.readers.get(k, {}).values():
                deps.append((r, False))
        for (p, raw) in deps:
            if p is op:
                continue
            if p.kind == "c":
                if p.eng == eng:
                    if eng == "pe":
                        continue
                if self.waited[eng][p.eng] >= p.idx:
                    continue
                self.waited[eng][p.eng] = p.idx
                p.signal = True
                op.waits.append(p)
            else:
                key = (p.eng, p.dma_m % self.R)
                if self.waited_dma[eng].get(key, -1) >= p.dma_m:
                    continue
                self.waited_dma[eng][key] = p.dma_m
                op.waits.append(p)
        if kind == "d":
            m = self.ndma[eng]
            self.ndma[eng] = m + 1
            op.dma_m = m
            self.dma_ops[eng].append(op)
            if m >= self.R:
                prev = self.dma_ops[eng][m - self.R]
                key = (eng, m % self.R)
                if self.waited_dma[eng].get(key, -1) < prev.dma_m:
                    self.waited_dma[eng][key] = prev.dma_m
                    op.waits.append(prev)
        for k in writes:
            self.last_w[k] = op
            self.readers[k] = {}
        for k in reads:
            self.readers.setdefault(k, {})[eng] = op
        return op

    def op(self, eng, fn, reads=(), writes=()):
        return self._add(eng, fn, reads, writes, "c")

    def dma(self, eng, fn, reads=(), writes=()):
        return self._add(eng, fn, reads, writes, "d")

    def emit(self, stack):
        nc = self.nc
        for e in self.ENGS:
            n = self.ndma[e]
            if n:
                last = self.dma_ops[e][max(0, n - self.R):]
                f = _Op(e, len(self.streams[e]), None, "f")
                f.waits = list(last)
                self.streams[e].append(f)
        csem = {}
        for e in self.ENGS:
            cnt = 0
            for op in self.streams[e]:
                if op.kind == "c" and op.signal:
                    ep = cnt // self.epoch
                    if (e, ep) not in csem:
                        csem[(e, ep)] = stack.enter_context(nc.semaphore(f"c_{e}_{ep}"))
                    op.tok = (csem[(e, ep)], cnt % self.epoch + 1)
                    cnt += 1
        dsem = {}
        for e in self.ENGS:
            for op in self.dma_ops[e]:
                s = op.dma_m % self.R
                if (e, s) not in dsem:
                    dsem[(e, s)] = stack.enter_context(nc.semaphore(f"d_{e}_{s}"))
                op.tok = (dsem[(e, s)], 16 * (op.dma_m // self.R + 1))
        block = stack.enter_context(nc.Block())

        def replay(e, eng):
            for op in self.streams[e]:
                for p in op.waits:
                    eng.wait_ge(p.tok[0], p.tok[1])
                if op.fn is None:
                    continue
                ins = op.fn(eng)
                if op.kind == "d":
                    ins.then_inc(op.tok[0], 16)
                elif op.signal:
                    ins.then_inc(op.tok[0], 1)

        if self.streams["pe"]:
            @block.tensor
            def _(eng):
                replay("pe", eng)
        if self.streams["act"]:
            @block.scalar
            def _(eng):
                replay("act", eng)
        if self.streams["dve"]:
            @block.vector
            def _(eng):
                replay("dve", eng)
        if self.streams["pool"]:
            @block.gpsimd
            def _(eng):
                replay("pool", eng)
        if self.streams["sp"]:
            @block.sync
            def _(eng):
                replay("sp", eng)


class V:
    __slots__ = ("ap", "key")

    def __init__(self, ap, key):
        self.ap = ap
        self.key = key

    def __getitem__(self, idx):
        return V(self.ap[idx], self.key)

    def r(self, pat, **kw):
        return V(self.ap.rearrange(pat, **kw), self.key)

    def bc(self, axis, shape):
        return V(self.ap.unsqueeze(axis).to_broadcast(list(shape)), self.key)


class TT:
    def __init__(self, t, key):
        self.t = t
        self.key = key

    def __getitem__(self, idx):
        return V(self.t[idx], self.key)

    def k(self, sub):
        return TT(self.t, (self.key, sub))


def _build(seqs, depth, final=True):
    nc = bass.Bass("TRN2", target_bir_lowering=False)
    NT = sum(seqs)
    NS = len(seqs)
    SMAX = max(seqs)
    roff = [sum(seqs[:i]) for i in range(NS)]

    def din(name, shape, dt=F32):
        return nc.dram_tensor(name, list(shape), dt, kind="ExternalInput").ap()

    x_d = TT(din("x", [NT, D]), "x")
    mem_d = TT(din("mem", [NS * 256, D]), "mem")
    win_d = din("w_in", [depth, D, NCOL])
    wout_d = din("w_out", [depth, D, D])
    wmem_d = din("w_mem", [depth, D, 512])
    ng_d = din("norm_g", [depth, D])
    mg_d = din("mem_g", [depth, D])
    pv_d = din("pvec", [depth, NPV])
    wup_d = din("wup", [depth, 32, 256])
    fng_d = din("fng", [D])
    cst_d = din("cst", [128, NCST])
    cstb_d = din("cstb", [128, 640])
    cw_d = din("convw", [depth, 2304])
    rc_d = TT(din("ropec", [SMAX, 384]), "ropec")
    rs_d = TT(din("ropes", [SMAX, 384]), "ropes")
    y_d = TT(nc.dram_tensor("y", [NT, D], F32, kind="ExternalOutput").ap(), "y")
    hbuf = TT(nc.dram_tensor("hbuf", [NT, D], F32, kind="Internal").ap(), "hbuf")
    stb = TT(nc.dram_tensor("stashb", [NT, NB], BF16, kind="Internal").ap(), "stb")
    stf = TT(nc.dram_tensor("stashf", [NT, NF], F32, kind="Internal").ap(), "stf")

    with ExitStack() as st:
        def sb(name, shape, dt=F32):
            return TT(st.enter_context(nc.sbuf_tensor("s_" + name, list(shape), dt)), name)

        def ps(name, shape, dt=F32):
            return TT(st.enter_context(nc.psum_tensor("ps_" + name, list(shape), dt)), name)

        S = Sched(nc)
        import os as _os
        if _os.environ.get("KLIMIT"):
            S.limit = int(_os.environ["KLIMIT"])
        if _os.environ.get("KDESC"):
            S.desc = []
        if _os.environ.get("KSKIP"):
            S.skip = set(int(v) for v in _os.environ["KSKIP"].split(","))

        def keys(*vs):
            return [v.key for v in vs if isinstance(v, V)]

        def apof(v):
            return v.ap if isinstance(v, V) else v

        def act(out, in_, func, scale=1.0, bias=0.0, accum=None):
            kw = dict(out=out.ap, in_=in_.ap, func=func, scale=apof(scale), bias=apof(bias))
            w = [out.key]
            if accum is not None:
                kw["accum_out"] = accum.ap
                w.append(accum.key)
            S.op("act", lambda e, kw=kw: e.activation(**kw), reads=keys(in_, scale, bias), writes=w)

        def ts(out, in0, s1, s2=None, op0=ALU.mult, op1=None, eng="dve"):
            kw = dict(out=out.ap, in0=in0.ap, scalar1=apof(s1), scalar2=apof(s2), op0=op0)
            if op1 is not None:
                kw["op1"] = op1
            S.op(eng, lambda e, kw=kw: e.tensor_scalar(**kw), reads=keys(in0, s1, s2), writes=[out.key])

        def tt(out, in0, in1, op=ALU.mult, eng="dve"):
            S.op(eng, lambda e: e.tensor_tensor(out=out.ap, in0=in0.ap, in1=in1.ap, op=op),
                 reads=keys(in0, in1), writes=[out.key])

        def stt(out, in0, scalar, in1, op0, op1, eng="dve"):
            S.op(eng, lambda e: e.scalar_tensor_tensor(out=out.ap, in0=in0.ap, scalar=apof(scalar), in1=in1.ap,
                                                       op0=op0, op1=op1),
                 reads=keys(in0, scalar, in1), writes=[out.key])

        def cp(out, in_, eng="dve"):
            if eng == "act":
                S.op("act", lambda e: e.copy(out=out.ap, in_=in_.ap), reads=[in_.key], writes=[out.key])
            else:
                S.op(eng, lambda e: e.tensor_copy(out=out.ap, in_=in_.ap), reads=[in_.key], writes=[out.key])

        def red(out, in_, op=ALU.add):
            S.op("dve", lambda e: e.tensor_reduce(out=out.ap, in_=in_.ap, axis=AX.X, op=op),
                 reads=[in_.key], writes=[out.key])

        def recip(out, in_):
            S.op("dve", lambda e: e.reciprocal(out=out.ap, in_=in_.ap), reads=[in_.key], writes=[out.key])

        def memset(out, val, eng="pool"):
            S.op(eng, lambda e: e.memset(out.ap, val), writes=[out.key])

        def mm(out, lhsT, rhs, start=True, stop=True):
            S.op("pe", lambda e: e.matmul(out.ap, lhsT=lhsT.ap, rhs=rhs.ap, start=start, stop=stop),
                 reads=keys(lhsT, rhs), writes=[out.key])

        def tr(out, in_, ident):
            S.op("pe", lambda e: e.transpose(out=out.ap, in_=in_.ap, identity=ident.ap),
                 reads=keys(in_, ident), writes=[out.key])

        def dma(eng, out, in_, slow=False):
            if slow:
                S.dma(eng, lambda e: e.dma_start(out=out.ap, in_=in_.ap, allow_slow_non_contiguous=True),
                      reads=[in_.key], writes=[out.key])
            else:
                S.dma(eng, lambda e: e.dma_start(out=out.ap, in_=in_.ap), reads=[in_.key], writes=[out.key])

        def rsqrt_(out, in_, scale, eps):
            act(out, in_, AF.Ln, scale=scale, bias=eps)
            act(out, out, AF.Exp, scale=-0.5)

        cst = sb("cst", [128, NCST])
        cstb = sb("cstb", [128, 5 * 128], BF16)
        win = sb("win", [128, 8, NCOL], BF16)
        wout = sb("wout", [128, 8, D], BF16)
        wmem = sb("wmem", [128, 8, 512], BF16)
        ngt = sb("ngt", [128, 8])
        mgt = sb("mgt", [128, 8])
        pv = sb("pv", [128, NPV])
        wup = sb("wup", [32, 256])
        negA = sb("negA", [128, 8])
        KT = sb("KT", [128, SMAX], BF16)
        qm = sb("qm", [128, 4, 128], BF16)
        Vaug = sb("Vaug", [128, SMAX // 128, 2, 65], BF16)
        kmT = sb("kmT", [128, 2, 2, 256], BF16)
        Vmaug = sb("Vmaug", [128, 2, 4, 65], BF16)
        hin = sb("hin", [128, D])
        ssq = sb("ssq", [128, 1])
        rstd = sb("rstd", [128, 1])
        xn = sb("xn", [128, D], BF16)
        xnT = sb("xnT", [128, D], BF16)
        rc = sb("rc", [128, 384])
        rs = sb("rs", [128, 384])
        ss8 = sb("ss8", [128, 8])
        rn8 = sb("rn8", [128, 8])
        qkrot = sb("qkrot", [128, 384], BF16)
        we = sb("we", [128, 768])
        xw = [sb(f"xw{k}", [128, 3, 768], BF16) for k in range(3)]
        lg2 = sb("lg2", [128, 16])
        bgf = [sb(f"bgf{k}", [128, 8]) for k in range(2)]
        ngkf = [sb(f"ngkf{k}", [128, 128]) for k in range(2)]
        low = sb("low", [128, 32])
        lowT = sb("lowT", [32, 128])
        gl = sb("gl", [128, 256])
        Tb = [sb(f"Tb{k}", [128, NB], BF16) for k in range(2)]
        Tf = [sb(f"Tf{k}", [128, NF]) for k in range(2)]
        wqB = sb("wqB", [128, 384])
        ss8b = sb("ss8b", [128, 8])
        rn8b = sb("rn8b", [128, 8])
        convw = sb("convw", [128, 2304], BF16)
        lgr = [sb(f"lgr{k}", [128, 16]) for k in range(2)]
        cv = sb("cv", [128, 768])
        gDN = sb("gDN", [128, 4, 128])
        gDA = sb("gDA", [128, 4, 128])
        gsm = sb("gsm", [128, 24])
        gcd = sb("gcd", [128, 2, 4])
        gN1 = sb("gN", [128, 4, 128], BF16)
        gG = sb("gG", [128, 4, 128])
        gA1 = sb("gA", [128, 4, 128], BF16)
        gP1 = sb("gP", [128, 4, 128], BF16)
        gN = [gN1, gN1]
        gA = [gA1, gA1]
        gP = [gP1, gP1]
        gkT = sb("gkT", [128, 2, 128], BF16)
        gkTm = sb("gkTm", [128, 2, 2, 128], BF16)
        gkdm = sb("gkdm", [128, 2, 256], BF16)
        gsm2 = sb("gsm2", [128, 8])
        lkdm = sb("lkdm", [128, 2, 128], BF16)
        gqT = sb("gqT", [128, 2, 128], BF16)
        gvk = sb("gvk", [128, 4, 128], BF16)
        guw = sb("guw", [128, 4, 128])
        gwb = sb("gwb", [128, 256], BF16)
        gwT = sb("gwT", [128, 2, 128], BF16)
        gqd = sb("gqd", [128, 256], BF16)
        gqdT = sb("gqdT", [128, 2, 128], BF16)
        gkd = sb("gkd", [128, 256], BF16)
        gqkm = sb("gqkm", [128, 4, 128], BF16)
        gvn = sb("gvn", [128, 256], BF16)
        gS = [[sb(f"gS{d}{p}", [128, 128]) for p in range(2)] for d in range(2)]
        gSb = [[sb(f"gSb{d}{p}", [128, 128], BF16) for p in range(2)] for d in range(2)]
        obcs = [sb(f"obc{k}", [128, 512]) for k in range(2)]
        ymbs = [sb(f"ymb{k}", [128, 256], BF16) for k in range(2)]
        ltmp = sb("ltmp", [128, 256])
        lex = sb("lex", [128, 3, 128])
        lcd = sb("lcd", [128, 2])
        lqe = sb("lqe", [128, 128], BF16)
        lke = sb("lke", [128, 128], BF16)
        lkd = sb("lkd", [128, 128], BF16)
        lqeT = sb("lqeT", [128, 128], BF16)
        lkeT = sb("lkeT", [128, 128], BF16)
        lqblk = sb("lqblk", [128, 4, 128], BF16)
        latt = sb("latt", [128, 4, 128], BF16)
        lS = [sb(f"lS{d}", [128, 256]) for d in range(2)]
        lSb = [sb(f"lSb{d}", [128, 256], BF16) for d in range(2)]
        os2 = sb("os2", [128, 512])
        osum = TT(os2.t[:, 0:256], "os2")
        sq = we
        ycat = xn
        ycatT = xnT
        junk = xn
        xc = cv
        wst = hin
        hout = hin
        pT = sb("pT", [128, 1024], BF16)
        wA = TT(pT.t[:, :].bitcast(F32), "pT")
        mqT = sb("mqT", [128, 2, 128], BF16)
        rs4 = sb("rs4", [128, 4])
        ya = osum
        kmtok = TT(qkrot.t[:, 0:256], "qkrot")

        ptr = ps("ptr", [128, 1024], BF16)
        pa = ps("pa", [128, 512])
        pb = ps("pb", [128, 512])
        p2 = ps("p2", [128, 512])
        p3 = ps("p3", [128, 512])
        sc = ps("sc", [128, 1024])
        p6 = ps("p6", [128, 512])
        sc0 = TT(sc.t[:, 0:512], "sc0")
        sc1 = TT(sc.t[:, 512:1024], "sc1")
        S.excl = {"ptr", "pa", "pb", "p2", "p3", "sc0", "sc1", "p6"}
        p6b = TT(p6.t[:, :].bitcast(BF16), "p6")
        pbb = TT(pb.t[:, :].bitcast(BF16), "pb")
        rs4a = sb("rs4a", [128, 4])
        yaa = sb("yaa", [128, 256])

        ident = cst[:, C_ID:C_ID + 128]
        identb = cstb[:, 0:128]
        SHPb, SHNb, HPb, HNb = [cstb[:, 128 * k:128 * (k + 1)] for k in range(1, 5)]
        mLE, mLT, mGE, mGT, mBD = [cst[:, c:c + 128] for c in (C_LE, C_LT, C_GE, C_GT, C_BD)]
        SEL = [cst[:, C_SEL0:C_SEL0 + 128], cst[:, C_SEL1:C_SEL1 + 128]]
        RM4 = cst[:, C_RM4:C_RM4 + 4]
        BD4 = cst[:, C_BD4:C_BD4 + 256]
        ONE = cst[:, C_ONE:C_ONE + 1]

        dma("sp", cst[:, :], V(cst_d, "cst_d"))
        dma("sp", hin[:, 0:640], V(cstb_d, "cstb_d"))
        cp(cstb[:, :], hin[:, 0:640], eng="pool")
        memset(Vaug[:, :, :, :], 1.0)
        memset(Vmaug[:, :, :, :], 1.0)
        memset(gvn[:, :], 0.0)

        def rms_to_xnT(src, pt=None):
            pt = ptr if pt is None else pt
            act(junk[:, :], src, AF.Square, accum=ssq[:, :])
            rsqrt_(rstd[:, :], ssq[:, :], 1.0 / D, EPS)
            ts(xn[:, :], src, rstd[:, :], None, ALU.mult)
            for c in range(8):
                tr(pt[:, c * 128:(c + 1) * 128], xn[:, c * 128:(c + 1) * 128], identb)
            cp(xnT[:, :], pt[:, :], eng="act")

        def sigmoid_from(dst_e, src, n):
            act(dst_e, src, AF.Exp, scale=-1.0)
            ts(dst_e, dst_e, 1.0, None, ALU.add)
            recip(dst_e, dst_e)

        def gdn_core(d, qn, kn, vv, beta, g, o_out, first):
            M1, GS, NSm, AI = (mLE, mGT, mGT, mLE) if d == 0 else (mGE, mLT, mLT, mGE)
            if first:
                for p in range(2):
                    memset(gS[d][p][:, :], 0.0)
                    memset(gSb[d][p][:, :], 0.0)
            for p in range(2):
                tr(ptr[:, p * 128:(p + 1) * 128], kn[:, p * 128:(p + 1) * 128], identb)
                tr(ptr[:, 256 + p * 128:256 + (p + 1) * 128], qn[:, p * 128:(p + 1) * 128], identb)
            cp(gkT[:, :, :], ptr[:, 0:256].r("p (a b) -> p a b", a=2), eng="act")
            for q in range(2):
                ts(gkTm[:, :, q, :], ptr[:, 0:256].r("p (a b) -> p a b", a=2), mBD[:, 64 * q:64 * q + 1], None, ALU.mult)
            cp(gqT[:, :, :], ptr[:, 256:512].r("p (a b) -> p a b", a=2), eng="act")
            tt(gG[:, :, :], GS.bc(1, [128, 4, 128]), g.bc(2, [128, 4, 128]), ALU.mult)
            mm(p2[:, :], M1, gG[:, :, :].r("p a b -> p (a b)"))
            for h in range(4):
                mm(p3[:, h * 128:(h + 1) * 128], gG[:, h, :], M1)
            act(gDN[:, :, :].r("p a b -> p (a b)"), p2[:, :], AF.Exp)
            act(gDA[:, :, :].r("p a b -> p (a b)"), p3[:, :], AF.Exp)
            tt(gDN[:, :, :], gDN[:, :, :], NSm.bc(1, [128, 4, 128]), ALU.mult)
            tt(gDA[:, :, :], gDA[:, :, :], AI.bc(1, [128, 4, 128]), ALU.mult, eng="pool")
            mm(pa[:, 0:4], M1, g)
            mm(pa[:, 4:8], mBD, g)
            mm(pa[:, 8:12], SEL[0], g)
            mm(pa[:, 12:16], SEL[1], g)
            cp(gsm[:, 0:4], pa[:, 0:4])
            act(gsm[:, 4:8], pa[:, 0:4], AF.Exp)
            tt(gsm[:, 8:12], pa[:, 4:8], gsm[:, 0:4], ALU.subtract)
            act(gsm[:, 8:12], gsm[:, 8:12], AF.Exp)
            act(gcd[:, :, :].r("p a b -> p (a b)"), pa[:, 8:16], AF.Exp)
            ts(gsm[:, 12:16], beta, -1.0, None, ALU.mult)
            tt(gsm[:, 16:20], beta, gsm[:, 4:8], ALU.mult)
            for h in range(4):
                pr, hb = h // 2, (h % 2) * 64
                mm(p2[:, h * 128:(h + 1) * 128], gkT[:, pr, :], gkTm[:, pr, h % 2, :])
            for h in range(4):
                stt(gN[0][:, h, :], p2[:, h * 128:(h + 1) * 128], gsm[:, 12 + h:13 + h], gDN[:, h, :],
                    ALU.mult, ALU.mult)
            for h in range(4):
                tr(ptr[:, h * 128:(h + 1) * 128], gN[0][:, h, :], identb)
            cp(gA[0][:, :, :].r("p a b -> p (a b)"), ptr[:, 0:512], eng="act")
            tt(gP[0][:, :, :], gA[0][:, :, :], ident.bc(1, [128, 4, 128]), ALU.add)
            cur = 0
            pc = 0
            for lvl in range(5):
                nxt = 1 - cur
                last = lvl == 4
                for h in range(4):
                    mm(p2[:, h * 128:(h + 1) * 128], gA[cur][:, h, :], gN[cur][:, h, :])
                if not last:
                    for h in range(4):
                        mm(p3[:, h * 128:(h + 1) * 128], gN[cur][:, h, :], gA[cur][:, h, :])
                cp(gN[nxt][:, :, :].r("p a b -> p (a b)"), p2[:, :], eng="dve")
                if not last:
                    cp(gA[nxt][:, :, :].r("p a b -> p (a b)"), p3[:, :], eng="act")
                for h in range(4):
                    mm(pa[:, h * 128:(h + 1) * 128], gN[nxt][:, h, :], gP[pc][:, h, :])
                tt(gP[1 - pc][:, :, :].r("p a b -> p (a b)"), pa[:, :], gP[pc][:, :, :].r("p a b -> p (a b)"),
                   ALU.add)
                pc = 1 - pc
                cur = nxt
            Pm = gP[pc]
            tt(gvk[:, :, 0:64], vv.r("p (a b) -> p a b", a=4), beta.bc(2, [128, 4, 64]), ALU.mult, eng="pool")
            tt(gvk[:, :, 64:128], kn.r("p (a b) -> p a b", a=4), gsm[:, 16:20].bc(2, [128, 4, 64]), ALU.mult,
               eng="pool")
            for h in range(4):
                mm(p3[:, h * 128:(h + 1) * 128], Pm[:, h, :], gvk[:, h, :])
            cp(guw[:, :, :].r("p a b -> p (a b)"), p3[:, :], eng="act")
            cp(gwb[:, :].r("p (a b) -> p a b", a=4), guw[:, :, 64:128], eng="dve")
            for p in range(2):
                tr(ptr[:, 512 + p * 128:512 + (p + 1) * 128], gwb[:, p * 128:(p + 1) * 128], identb)
            cp(gwT[:, :, :], ptr[:, 512:768].r("p (a b) -> p a b", a=2), eng="act")
            tt(gqd[:, :].r("p (a b) -> p a b", a=4), qn.r("p (a b) -> p a b", a=4),
               gsm[:, 4:8].bc(2, [128, 4, 64]), ALU.mult, eng="pool")
            for p in range(2):
                tr(ptr[:, p * 128:(p + 1) * 128], gqd[:, p * 128:(p + 1) * 128], identb)
            cp(gqdT[:, :, :], ptr[:, 0:256].r("p (a b) -> p a b", a=2), eng="act")
            for c in range(2):
                ts(gsm2[:, 4 * c:4 * c + 4], gsm[:, 8:12], mBD[:, 64 * c:64 * c + 1], None, ALU.mult)
                tt(gkdm[:, c, :].r("p (a b) -> p a b", a=4), kn.r("p (a b) -> p a b", a=4),
                   gsm2[:, 4 * c:4 * c + 4].bc(2, [128, 4, 64]), ALU.mult, eng="pool")
            for h in range(4):
                pr, hb = h // 2, (h % 2) * 64
                mm(p2[:, h * 128:(h + 1) * 128], gkTm[:, pr, h % 2, :], gqT[:, pr, :])
            tt(gqkm[:, :, :].r("p a b -> p (a b)"), p2[:, :], gDA[:, :, :].r("p a b -> p (a b)"), ALU.mult)
            for c in ((0, 1) if d == 0 else (1, 0)):
                r0, r1 = 64 * c, 64 * c + 64
                for p in range(2):
                    mm(p3[:, p * 128:(p + 1) * 128], gwT[:, p, :], gSb[d][p][:, :])
                tt(gvn[r0:r1, :].r("p (a b) -> p a b", a=4), guw[r0:r1, :, 0:64],
                   p3[r0:r1, 0:256].r("p (a b) -> p a b", a=4), ALU.subtract)
                for h in range(4):
                    pr, hb = h // 2, (h % 2) * 64
                    mm(pa[:, h * 64:(h + 1) * 64], gqdT[:, pr, :], gSb[d][pr][:, hb:hb + 64], start=True, stop=False)
                    mm(pa[:, h * 64:(h + 1) * 64], gqkm[:, h, :], gvn[:, h * 64:(h + 1) * 64],
                       start=False, stop=True)
                cp(o_out[r0:r1, :], pa[r0:r1, 0:256], eng="act")
                for p in range(2):
                    mm(p2[:, p * 128:(p + 1) * 128], gkdm[:, c, p * 128:(p + 1) * 128],
                       gvn[:, p * 128:(p + 1) * 128])
                for p in range(2):
                    for q in range(2):
                        h = 2 * p + q
                        b0, b1 = 64 * q, 64 * q + 64
                        stt(gS[d][p][b0:b1, b0:b1], gS[d][p][b0:b1, b0:b1], gcd[b0:b1, c, h:h + 1],
                            p2[b0:b1, p * 128 + b0:p * 128 + b1], ALU.mult, ALU.add)
                    cp(gSb[d][p][:, :], gS[d][p][:, :], eng="dve")

        def gla_core(d, lq, lk, lv, ngk, o_out, first):
            M1, GS, AI = (mLE, mGT, mLE) if d == 0 else (mGE, mLT, mGE)
            if first:
                memset(lS[d][:, :], 0.0)
                memset(lSb[d][:, :], 0.0)
            mm(sc1[:, 0:128], M1, ngk)
            mm(sc1[:, 128:256], GS, ngk)
            for c in range(2):
                mm(sc1[:, 256 + c:257 + c], ngk, mBD[:, 64 * c:64 * c + 1])
            act(lex[:, 0, :], sc1[:, 0:128], AF.Exp, scale=-1.0 / 16)
            act(lex[:, 1, :], sc1[:, 0:128], AF.Exp, scale=1.0 / 16)
            act(lex[:, 2, :], sc1[:, 128:256], AF.Exp, scale=-1.0 / 16)
            act(lcd[:, :], sc1[:, 256:258], AF.Exp, scale=-1.0 / 16)
            tt(lqe[:, :], lq, lex[:, 0, :], ALU.mult)
            tt(lke[:, :], lk, lex[:, 1, :], ALU.mult, eng="pool")
            tt(lkd[:, :], lk, lex[:, 2, :], ALU.mult, eng="pool")
            for c in range(2):
                ts(lkdm[:, c, :], lkd[:, :], mBD[:, 64 * c:64 * c + 1], None, ALU.mult)
            tr(ptr[:, 768:896], lqe[:, :], identb)
            tr(ptr[:, 896:1024], lke[:, :], identb)
            cp(lqeT[:, :], ptr[:, 768:896], eng="act")
            cp(lkeT[:, :], ptr[:, 896:1024], eng="act")
            tt(lqblk[:, :, :], ptr[:, 768:896].bc(1, [128, 4, 128]), RM4.bc(2, [128, 4, 128]), ALU.mult)
            mm(pb[:, :], lkeT[:, :], lqblk[:, :, :].r("p a b -> p (a b)"))
            tt(latt[:, :, :], pb[:, :].r("p (a b) -> p a b", a=4), AI.bc(1, [128, 4, 128]), ALU.mult)
            for c in ((0, 1) if d == 0 else (1, 0)):
                r0, r1 = 64 * c, 64 * c + 64
                for h in range(4):
                    mm(sc1[:, h * 64:(h + 1) * 64], lqeT[:, :], lSb[d][:, h * 64:(h + 1) * 64], start=True, stop=False)
                    mm(sc1[:, h * 64:(h + 1) * 64], latt[:, h, :], lv[:, h * 64:(h + 1) * 64],
                       start=False, stop=True)
                cp(o_out[r0:r1, :], sc1[r0:r1, 0:256], eng="act")
                mm(sc1[:, 256:512], lkdm[:, c, :], lv)
                tt(ltmp[:, :], sc1[:, 256:512], BD4, ALU.mult)
                stt(lS[d][:, :], lS[d][:, :], lcd[:, c:c + 1], ltmp[:, :], ALU.mult, ALU.add)
                cp(lSb[d][:, :], lS[d][:, :], eng="dve")

        for l in range(depth):
            last_layer = (l == depth - 1)
            dma("sp", ngt[:, :], V(ng_d[l].rearrange("(c p) -> p c", p=128), "ng_d"), slow=True)
            dma("sp", mgt[:, :], V(mg_d[l].rearrange("(c p) -> p c", p=128), "mg_d"), slow=True)
            dma("sp", pv[:, :], V(pv_d[l].partition_broadcast(128), "pv_d"))
            dma("sp", wup[:, :], V(wup_d[l], "wup_d"))
            for c in range(8):
                for (j0, j1) in ((0, 1024), (1024, 2048), (2048, 3072), (3072, 3120)):
                    dma("sp", wst[:, 0:j1 - j0], V(win_d[l, c * 128:(c + 1) * 128, j0:j1], "win_d"))
                    ts(win[:, c, j0:j1], wst[:, 0:j1 - j0], ngt[:, c:c + 1], None, ALU.mult)
                dma("sp", wst[:, 0:1024], V(wout_d[l, c * 128:(c + 1) * 128, :], "wout_d"))
                cp(wout[:, c, :], wst[:, 0:1024], eng="pool")
                dma("sp", wst[:, 0:512], V(wmem_d[l, c * 128:(c + 1) * 128, :], "wmem_d"))
                ts(wmem[:, c, :], wst[:, 0:512], mgt[:, c:c + 1], None, ALU.mult)
            for j in range(3):
                dma("sp", wst[:, 0:768], V(cw_d[l, j * 768:(j + 1) * 768].partition_broadcast(128), "cw_d"))
                cp(convw[:, j * 768:(j + 1) * 768], wst[:, 0:768], eng="pool")
            act(negA[:, :], pv[:, P_ALOG:P_ALOG + 8], AF.Exp)
            ts(negA[:, :], negA[:, :], -1.0, None, ALU.mult)

            for s in range(NS):
                Sq = seqs[s]
                n = Sq // 128
                R0 = roff[s]
                src_h = x_d if l == 0 else hbuf

                for mt in range(2):
                    dma("sp", hin[:, :], mem_d[s * 256 + mt * 128:s * 256 + (mt + 1) * 128, :])
                    rms_to_xnT(hin[:, :])
                    for c in range(8):
                        mm(pa[:, :], xnT[:, c * 128:(c + 1) * 128], wmem[:, c, :], start=(c == 0), stop=(c == 7))
                    cp(kmtok[:, :], pa[:, 0:256], eng="act")
                    cp(Vmaug[:, mt, :, 0:64], pa[:, 256:512].r("p (a b) -> p a b", a=4), eng="dve")
                    for p in range(2):
                        tr(ptr[:, p * 128:(p + 1) * 128], kmtok[:, p * 128:(p + 1) * 128], identb)
                    for q in range(2):
                        ts(kmT[:, :, q, mt * 128:(mt + 1) * 128], ptr[:, 0:256].r("p (a b) -> p a b", a=2),
                           mBD[:, 64 * q:64 * q + 1], None, ALU.mult)

                def P_stage(i, phase):
                    par = i % 2
                    rows = slice(R0 + i * 128, R0 + (i + 1) * 128)
                    tb, tf = Tb[par], Tf[par]
                    if phase == 0:
                        dma("sp", hin[:, :], src_h.k(R0 // 128 + i)[rows, :])
                        dma("sp", rc[:, :], rc_d[i * 128:(i + 1) * 128, :])
                        dma("sp", rs[:, :], rs_d[i * 128:(i + 1) * 128, :])
                        rms_to_xnT(hin[:, :], p6b)
                    groups = [(0, 512), (512, 1024), (1024, 1536), (1536, 2048), (2048, 2320), (2320, 2832),
                              (2832, 3120)]
                    for gi, (c0, c1) in enumerate(groups):
                        if (gi in (3, 4)) != (phase == 0):
                            continue
                        pp = p6 if gi % 2 == 0 else sc0
                        w = c1 - c0
                        for c in range(8):
                            mm(pp[:, 0:w], xnT[:, c * 128:(c + 1) * 128], win[:, c, c0:c1], start=(c == 0),
                               stop=(c == 7))
                        if gi == 0:
                            act(wA[:, 0:384], pp[:, 0:384], AF.Square)
                            red(ss8b[:, 0:6], wA[:, 0:384].r("p (a b) -> p a b", a=6))
                            rsqrt_(rn8b[:, 0:6], ss8b[:, 0:6], 1.0 / 64, EPS)
                            tt(wqB[:, :].r("p (a b) -> p a b", a=6), pp[:, 0:384].r("p (a b) -> p a b", a=6),
                               rn8b[:, 0:6].bc(2, [128, 6, 64]), ALU.mult)
                            cp(Vaug[:, i, :, 0:64], pp[:, 384:512].r("p (a b) -> p a b", a=2), eng="dve")
                            tt(wqB[:, :], wqB[:, :], pv[:, P_GQK:P_GQK + 384], ALU.mult, eng="pool")
                            v5 = lambda t: t[:, 0:384].r("p (a b c) -> p a b c", a=12, b=2)
                            tt(v5(wA)[:, :, 0, :], v5(wqB)[:, :, 1, :], v5(rs)[:, :, 0, :], ALU.mult, eng="pool")
                            tt(v5(wA)[:, :, 1, :], v5(wqB)[:, :, 0, :], v5(rs)[:, :, 1, :], ALU.mult, eng="pool")
                            tt(wqB[:, :], wqB[:, :], rc[:, :], ALU.mult, eng="pool")
                            tt(tb[:, B_Q:B_Q + 256], wqB[:, 0:256], wA[:, 0:256], ALU.add)
                            tt(qkrot[:, 0:128], wqB[:, 256:384], wA[:, 256:384], ALU.add)
                            tr(p6b[:, 0:128], qkrot[:, 0:128], identb)
                            cp(KT[:, i * 128:(i + 1) * 128], p6b[:, 0:128], eng="act")
                        elif gi in (1, 2):
                            sigmoid_from(wA[:, 0:512], pp[:, 0:512], 512)
                            off = B_GA if gi == 1 else B_GD
                            tt(tb[:, off:off + 512], pp[:, 0:512], wA[:, 0:512], ALU.mult)
                        elif gi == 3:
                            for k3 in range(3):
                                tt(xw[i % 3][:, k3, 0:512], pp[:, 0:512], convw[:, 768 * k3:768 * k3 + 512], ALU.mult)
                        elif gi == 4:
                            cp(lgr[par][:, :], pp[:, 256:272], eng="dve")
                            for k3 in range(3):
                                tt(xw[i % 3][:, k3, 512:768], pp[:, 0:256], convw[:, 768 * k3 + 512:768 * (k3 + 1)],
                                   ALU.mult)
                        elif gi == 5:
                            act(tf[:, F_LQ:F_LQ + 128], pp[:, 0:128], AF.Copy, scale=32.0 ** -0.5)
                            cp(tf[:, F_LK:F_LK + 128], pp[:, 128:256], eng="act")
                            cp(tb[:, B_LV:B_LV + 256], pp[:, 256:512], eng="act")
                        else:
                            cp(tb[:, B_MQ:B_MQ + 256], pp[:, 0:256], eng="act")
                            cp(low[:, :], pp[:, 256:288], eng="dve")
                            tr(sc0[0:32, 0:128], low[:, :], ident)
                            cp(lowT[:, :], sc0[0:32, 0:128], eng="act")
                            mm(sc0[:, 128:384], lowT[:, :], wup[:, :])
                            tt(gl[:, :], sc0[:, 128:384], pv[:, P_GLB:P_GLB + 256], ALU.add)
                            act(gl[:, :], gl[:, :], AF.Exp, scale=-1.0)
                            act(ngkf[par][:, :], gl[:, 0:128], AF.Ln, bias=1.0)
                            act(tf[:, F_NGK:F_NGK + 128], gl[:, 128:256], AF.Ln, bias=1.0)

                    if phase == 1:
                        lg = lgr[par]
                        sigmoid_from(lg2[:, 0:8], lg[:, 0:8], 8)
                        tt(lg2[:, 8:16], lg[:, 8:16], pv[:, P_DTB:P_DTB + 8], ALU.add)
                        act(lg2[:, 8:16], lg2[:, 8:16], AF.Exp)
                        act(lg2[:, 8:16], lg2[:, 8:16], AF.Ln, bias=1.0)
                        tt(lg2[:, 8:16], lg2[:, 8:16], negA[:, :], ALU.mult)
                        cp(bgf[par][:, 0:4], lg2[:, 0:4], eng="pool")
                        cp(bgf[par][:, 4:8], lg2[:, 8:12], eng="pool")
                        cp(tf[:, F_BG:F_BG + 4], lg2[:, 4:8], eng="pool")
                        cp(tf[:, F_BG + 4:F_BG + 8], lg2[:, 12:16], eng="pool")

                def post(i, extra=None):
                    par = i % 2
                    rows = slice(R0 + i * 128, R0 + (i + 1) * 128)
                    tb, tf = Tb[par], Tf[par]
                    S.capture()
                    for (c0, c1, pp) in ((0, 512, p2), (512, 768, p3)):
                        w = c1 - c0
                        ops = [(SHPb, xw[i % 3][:, 0, c0:c1]), (identb, xw[i % 3][:, 1, c0:c1]),
                               (SHNb, xw[i % 3][:, 2, c0:c1])]
                        if i > 0:
                            ops.append((HPb, xw[(i - 1) % 3][:, 0, c0:c1]))
                        if i < n - 1:
                            ops.append((HNb, xw[(i + 1) % 3][:, 2, c0:c1]))
                        for k3, (lh, rh) in enumerate(ops):
                            mm(pp[:, 0:w], lh, rh, start=(k3 == 0), stop=(k3 == len(ops) - 1))
                        sigmoid_from(we[:, c0:c1], pp[:, 0:w], w)
                        tt(cv[:, c0:c1], pp[:, 0:w], we[:, c0:c1], ALU.mult)
                    act(sq[:, 0:512], cv[:, 0:512], AF.Square)
                    red(ss8[:, :], sq[:, 0:512].r("p (a b) -> p a b", a=8))
                    rsqrt_(rn8[:, :], ss8[:, :], 1.0, EPS)
                    ts(rn8[:, 0:4], rn8[:, 0:4], 0.125, None, ALU.mult)
                    tt(tb[:, B_GQ:B_GQ + 512].r("p (a b) -> p a b", a=8), cv[:, 0:512].r("p (a b) -> p a b", a=8),
                       rn8[:, :].bc(2, [128, 8, 64]), ALU.mult)
                    cp(tb[:, B_GV:B_GV + 256], cv[:, 512:768], eng="act")
                    gdn_core(0, tb[:, B_GQ:B_GQ + 256], tb[:, B_GK:B_GK + 256], tb[:, B_GV:B_GV + 256],
                             bgf[par][:, 0:4], bgf[par][:, 4:8], tf[:, F_OB:F_OB + 256], first=(i == 0))
                    LG1 = S.end_capture()
                    S.capture()
                    gla_core(0, tf[:, F_LQ:F_LQ + 128], tf[:, F_LK:F_LK + 128], tb[:, B_LV:B_LV + 256],
                             ngkf[par][:, :], tf[:, F_OC:F_OC + 256], first=(i == 0))
                    LL1 = S.end_capture()
                    S.merge([LG1, LL1] + ([extra] if extra else []))
                    dma("pool", stb.k(R0 // 128 + i)[rows, :], tb[:, :])
                    dma("pool", stf.k(R0 // 128 + i)[rows, :], tf[:, :])

                P_stage(0, 0)
                for i in range(n + 1):
                    LP = None
                    if i < n:
                        S.capture()
                        P_stage(i, 1)
                        if i + 1 < n:
                            P_stage(i + 1, 0)
                        LP = S.end_capture()
                    if i >= 1:
                        post(i - 1, LP)
                    elif LP:
                        S.merge([LP])

                if last_layer and final:
                    dma("sp", cv[:, 0:768], V(fng_d[0:768].partition_broadcast(128), "fng_d"))
                    dma("sp", we[:, 512:768], V(fng_d[768:1024].partition_broadcast(128), "fng_d"))
                def tail(t):
                    rows_t = slice(R0 + t * 128, R0 + (t + 1) * 128)
                    dma("sp", hin[:, :], src_h.k(R0 // 128 + t)[rows_t, :])
                    tt(os2[:, :], obcs[t % 2][:, :], Tf[t % 2][:, F_OB:F_OB + 512], ALU.add)
                    act(sq[:, 0:512], os2[:, :], AF.Square)
                    red(ss8[:, 0:8], sq[:, 0:512].r("p (a b) -> p a b", a=8))
                    rsqrt_(rn8[:, 0:8], ss8[:, 0:8], 1.0 / 64, EPS)
                    tt(os2[:, :].r("p (a b) -> p a b", a=8), os2[:, :].r("p (a b) -> p a b", a=8),
                       rn8[:, 0:8].bc(2, [128, 8, 64]), ALU.mult)
                    tt(os2[:, :], os2[:, :], pv[:, P_GOG:P_GOG + 512], ALU.mult, eng="pool")
                    tt(ycat[:, 256:768], os2[:, :], Tb[t % 2][:, B_GD:B_GD + 512], ALU.mult)
                    for c in range(8):
                        src_c = ycat[:, c * 128:(c + 1) * 128] if c < 6 else ymbs[t % 2][:, (c - 6) * 128:(c - 5) * 128]
                        tr(pbb[:, c * 128:(c + 1) * 128], src_c, identb)
                    cp(ycatT[:, :], pbb[:, :], eng="act")
                    for half, pp in ((0, sc1), (1, pb)):
                        for c in range(8):
                            mm(pp[:, :], ycatT[:, c * 128:(c + 1) * 128], wout[:, c, half * 512:(half + 1) * 512],
                               start=(c == 0), stop=(c == 7))
                        tt(hout[:, half * 512:(half + 1) * 512], pp[:, :], hin[:, half * 512:(half + 1) * 512], ALU.add)
                    if last_layer and final:
                        act(junk[:, :], hout[:, :], AF.Square, accum=ssq[:, :])
                        rsqrt_(rstd[:, :], ssq[:, :], 1.0 / D, EPS)
                        ts(hout[:, :], hout[:, :], rstd[:, :], None, ALU.mult)
                        tt(hout[:, 0:768], hout[:, 0:768], cv[:, 0:768], ALU.mult, eng="pool")
                        tt(hout[:, 768:1024], hout[:, 768:1024], we[:, 512:768], ALU.mult, eng="pool")
                        dma("pool", y_d.k(R0 // 128 + t)[rows_t, :], hout[:, :])
                    elif last_layer:
                        dma("pool", y_d.k(R0 // 128 + t)[rows_t, :], hout[:, :])
                    else:
                        dma("pool", hbuf.k(R0 // 128 + t)[rows_t, :], hout[:, :])
                nkt = n
                prev = None
                for i in range(n - 1, -1, -1):
                    rows = slice(R0 + i * 128, R0 + (i + 1) * 128)
                    Tb2, Tf2 = Tb[i % 2], Tf[i % 2]
                    dma("sp", Tb2[:, :], stb.k(R0 // 128 + i)[rows, :])
                    dma("sp", Tf2[:, :], stf.k(R0 // 128 + i)[rows, :])
                    first = (i == n - 1)
                    S.capture()
                    gdn_core(1, Tb2[:, B_GQ:B_GQ + 256], Tb2[:, B_GK:B_GK + 256], Tb2[:, B_GV:B_GV + 256],
                             Tf2[:, F_BG:F_BG + 4], Tf2[:, F_BG + 4:F_BG + 8], obcs[i % 2][:, 0:256], first=first)
                    LG = S.end_capture()
                    S.capture()
                    if prev is not None:
                        tail(prev)
                    gla_core(1, Tf2[:, F_LQ:F_LQ + 128], Tf2[:, F_LK:F_LK + 128], Tb2[:, B_LV:B_LV + 256],
                             Tf2[:, F_NGK:F_NGK + 128], obcs[i % 2][:, 256:512], first=first)
                    for p in range(2):
                        tr(ptr[:, 768 + p * 128:768 + (p + 1) * 128], Tb2[:, B_MQ + p * 128:B_MQ + (p + 1) * 128], identb)
                    cp(mqT[:, :, :], ptr[:, 768:1024].r("p (a b) -> p a b", a=2), eng="act")
                    for hp in range(2):
                        for hq in range(2):
                            h = 2 * hp + hq
                            pr, _hb = h // 2, (h % 2) * 64
                            for kt in range(2):
                                mm(sc1[:, (hq * 2 + kt) * 128:(hq * 2 + kt + 1) * 128],
                                   kmT[:, pr, h % 2, kt * 128:(kt + 1) * 128], mqT[:, pr, :])
                        act(pT[:, 512:1024], sc1[:, :], AF.Exp, scale=0.125)
                        for hq in range(2):
                            h = 2 * hp + hq
                            for kt in range(2):
                                mm(pb[:, h * 65:(h + 1) * 65], pT[:, 512 + (hq * 2 + kt) * 128:512 + (hq * 2 + kt + 1) * 128],
                                   Vmaug[:, kt, h, 0:65], start=(kt == 0), stop=(kt == 1))
                    recip(rs4[:, :], pb[:, 0:260].r("p (a b) -> p a b", a=4)[:, :, 64])
                    tt(ymbs[i % 2][:, :].r("p (a b) -> p a b", a=4), pb[:, 0:260].r("p (a b) -> p a b", a=4)[:, :, 0:64],
                       rs4[:, :].bc(2, [128, 4, 64]), ALU.mult)
                    tt(ymbs[i % 2][:, :], ymbs[i % 2][:, :], Tb2[:, B_GM:B_GM + 256], ALU.mult, eng="pool")
                    LL = S.end_capture()
                    S.capture()
                    for g2 in range(2):
                        tr(p6b[:, 768 + g2 * 128:768 + (g2 + 1) * 128], Tb2[:, B_Q + g2 * 128:B_Q + (g2 + 1) * 128], identb)
                    for h in range(4):
                        ts(qm[:, h, :], p6b[:, 768 + (h % 2) * 128:768 + (h % 2 + 1) * 128],
                           mBD[:, 64 * (h // 2):64 * (h // 2) + 1], None, ALU.mult)
                    for h in range(4):
                        kv, g2 = h // 2, h % 2
                        kb = 64 * kv
                        for k0 in range(0, nkt, 4):
                            kn_ = min(4, nkt - k0)
                            for kk in range(kn_):
                                kt = k0 + kk
                                mm(sc0[:, kk * 128:(kk + 1) * 128], KT[:, kt * 128:(kt + 1) * 128], qm[:, h, :])
                            act(pT[:, 0:kn_ * 128], sc0[:, 0:kn_ * 128], AF.Exp, scale=0.125)
                            for kk in range(kn_):
                                kt = k0 + kk
                                mm(p6[:, h * 65:(h + 1) * 65], pT[:, kk * 128:(kk + 1) * 128], Vaug[:, kt, kv, 0:65],
                                   start=(kt == 0), stop=(kt == nkt - 1))
                    recip(rs4a[:, :], p6[:, 0:260].r("p (a b) -> p a b", a=4)[:, :, 64])
                    tt(yaa[:, :].r("p (a b) -> p a b", a=4), p6[:, 0:260].r("p (a b) -> p a b", a=4)[:, :, 0:64],
                       rs4a[:, :].bc(2, [128, 4, 64]), ALU.mult)
                    tt(ycat[:, 0:256], yaa[:, :], Tb2[:, B_GA:B_GA + 256], ALU.mult, eng="pool")
                    LA = S.end_capture()
                    S.merge([LG, LL, LA])
                    prev = i
                tail(prev)
        if _os.environ.get("KDESC"):
            lo, hi = [int(v) for v in _os.environ["KDESC"].split(",")]
            for dd in S.desc:
                if lo <= dd[0] <= hi:
                    print("OP", dd)
        if _os.environ.get("KLIMIT"):
            print("TOTAL OPS", S.total)
        S.emit(st)
    return nc


def _consts():
    c = np.zeros((128, NCST), np.float32)
    p = np.arange(128)[:, None]
    f = np.arange(128)[None, :]
    bd = (p // 64 == f // 64)
    c[:, C_ID:C_ID + 128] = (p == f)
    c[:, C_LE:C_LE + 128] = bd & (p <= f)
    c[:, C_LT:C_LT + 128] = bd & (p < f)
    c[:, C_GE:C_GE + 128] = bd & (p >= f)
    c[:, C_GT:C_GT + 128] = bd & (p > f)
    c[:, C_BD:C_BD + 128] = bd
    c[:, C_SEL0:C_SEL0 + 128] = (p < 64) & (f >= 0)
    c[:, C_SEL1:C_SEL1 + 128] = (p >= 64) & (f >= 0)
    c[:, C_RM4:C_RM4 + 4] = (p // 32 == np.arange(4)[None, :])
    c[:, C_BD4:C_BD4 + 256] = (p // 32 == np.arange(256)[None, :] // 64)
    c[:, C_ONE] = 1.0
    cb = np.zeros((128, 640), np.float32)
    cb[:, 0:128] = (p == f)
    cb[:, 128:256] = (p == f - 1)
    cb[:, 256:384] = (p == f + 1)
    cb[:, 384:512] = (p == 127) & (f == 0)
    cb[:, 512:640] = (p == 0) & (f == 127)
    return c, cb


def _rope_tables(smax):
    t = np.arange(smax)
    pos = np.stack([t // 64, t % 64], -1).astype(np.float32)
    inv = (10000.0 ** (-np.arange(0, 32, 2, dtype=np.float32) / 32)).astype(np.float32)
    ang = pos[:, :, None] * inv[None, None, :]
    cos, sin = np.cos(ang).astype(np.float32), np.sin(ang).astype(np.float32)
    C = np.zeros((smax, 2, 2, 16), np.float32)
    Sg = np.zeros((smax, 2, 2, 16), np.float32)
    C[:, :, 0, :] = cos
    C[:, :, 1, :] = cos
    Sg[:, :, 0, :] = -sin
    Sg[:, :, 1, :] = sin
    C = np.tile(C.reshape(smax, 1, 64), (1, 6, 1)).reshape(smax, 384)
    Sg = np.tile(Sg.reshape(smax, 1, 64), (1, 6, 1)).reshape(smax, 384)
    return np.ascontiguousarray(C), np.ascontiguousarray(Sg)


def _col_perm():
    o = dict(a_q=0, a_k=256, a_v=384, a_gate=512, d_qkv=768, d_beta=1536, d_alpha=1544, d_gate=1552,
             l_q=1808, l_k=1936, l_v=2064, l_low=2320, l_gate=2352, m_q=2608, m_gate=2864)
    r = lambda a, n: list(range(a, a + n))
    aq = []
    for g in range(2):
        for kv in range(2):
            h = kv * 2 + g
            aq += r(o["a_q"] + h * 64, 64)
    perm = (aq + r(o["a_k"], 128) + r(o["a_v"], 128)
            + r(o["a_gate"], 256) + r(o["m_gate"], 256)
            + r(o["d_gate"], 256) + r(o["l_gate"], 256)
            + r(o["d_qkv"], 512)
            + r(o["d_qkv"] + 512, 256) + r(o["d_beta"], 8) + r(o["d_alpha"], 8)
            + r(o["l_q"], 128) + r(o["l_k"], 128) + r(o["l_v"], 256)
            + r(o["m_q"], 256) + r(o["l_low"], 32))
    assert len(perm) == NCOL and len(set(perm)) == NCOL
    return np.array(perm)


def _prep_shared(inp, depth):
    f = lambda a: np.ascontiguousarray(np.asarray(a, dtype=np.float32))
    perm = _col_perm()
    w_in = f(np.asarray(inp["w_in"])[:depth][:, :, perm])
    pvec = np.zeros((depth, NPV), np.float32)
    wup = np.zeros((depth, 32, 256), np.float32)
    for l in range(depth):
        pvec[l, P_GQK:P_GQK + 256] = np.tile(np.asarray(inp["att_q_norm_g"])[l], 4)
        pvec[l, P_GQK + 256:P_GQK + 384] = np.tile(np.asarray(inp["att_k_norm_g"])[l], 2)
        pvec[l, P_ALOG:P_ALOG + 8] = np.asarray(inp["gdn_a_log"])[l].reshape(-1)
        pvec[l, P_DTB:P_DTB + 8] = np.asarray(inp["gdn_dt_bias"])[l].reshape(-1)
        pvec[l, P_GOG:P_GOG + 256] = np.tile(np.asarray(inp["gdn_out_norm_g"])[l], 4)
        pvec[l, P_GLB:P_GLB + 256] = np.asarray(inp["gla_b_gate"])[l].reshape(-1)
        pvec[l, P_GLG:P_GLG + 256] = np.tile(np.asarray(inp["gla_out_norm_g"])[l], 4)
        for d in range(2):
            wup[l, d * 16:(d + 1) * 16, d * 128:(d + 1) * 128] = np.asarray(inp["gla_w_gate_up"])[l, d]
    return dict(w_in=w_in, w_out=f(np.asarray(inp["w_out"])[:depth]), w_mem=f(np.asarray(inp["w_mem_kv"])[:depth]),
                norm_g=f(np.asarray(inp["norm_g"])[:depth]), mem_g=f(np.asarray(inp["mem_norm_g"])[:depth]),
                pvec=pvec, wup=wup, fng=f(inp["final_norm_g"]), cst=_consts()[0], cstb=_consts()[1],
                convw=f(np.asarray(inp["gdn_conv_w"])[:depth].reshape(depth, -1)))


def kernel(**inputs):
    depth = 4
    xp = np.asarray(inputs["x_prompt"], np.float32)
    xs = np.asarray(inputs["x_sample"], np.float32)
    mp = np.asarray(inputs["mem_prompt"], np.float32)
    ms = np.asarray(inputs["mem_sample"], np.float32)
    Sp, Ss = xp.shape[1], xs.shape[1]
    seqs = [Sp, Ss, Ss]
    shared = _prep_shared(inputs, depth)
    C, Sg = _rope_tables(max(seqs))
    shared["ropec"], shared["ropes"] = C, Sg
    nc = _build(seqs, depth)
    in_maps = []
    for c in range(8):
        m = dict(shared)
        m["x"] = np.ascontiguousarray(np.concatenate([xp[c], xs[2 * c], xs[2 * c + 1]], 0))
        m["mem"] = np.ascontiguousarray(np.concatenate([mp[c], ms[2 * c], ms[2 * c + 1]], 0))
        in_maps.append(m)
    res = run_bass_kernel_spmd(nc, in_maps, core_ids=list(range(8)))
    yp = np.stack([res.results[c]["y"][:Sp] for c in range(8)], 0)
    ysm = np.stack([res.results[c // 2]["y"][Sp + (c % 2) * Ss:Sp + (c % 2 + 1) * Ss] for c in range(16)], 0)
    return (yp.astype(np.float32), ysm.astype(np.float32))
```

```python
from contextlib import ExitStack
import numpy as np
import concourse.bass as bass
import concourse.mybir as mybir
from concourse.bass_utils import run_bass_kernel_spmd

F32 = mybir.dt.float32
BF16 = mybir.dt.bfloat16
AF = mybir.ActivationFunctionType
ALU = mybir.AluOpType
AX = mybir.AxisListType

D = 1024
NCOL = 3120
EPS = 1e-6
NB = 2560
NF = 904
B_GA, B_GM, B_GD, B_GL, B_MQ, B_GQ, B_GK, B_GV, B_LV, B_Q = 0, 256, 512, 768, 1024, 1280, 1536, 1792, 2048, 2304
F_BG, F_LQ, F_LK, F_NGK, F_OB, F_OC = 0, 8, 136, 264, 392, 648
P_GQK, P_ALOG, P_DTB, P_GOG, P_GLG, P_GLB = 0, 384, 392, 400, 656, 912
NPV = 1168
C_ID, C_LE, C_LT, C_GE, C_GT, C_BD, C_SEL0, C_SEL1 = [128 * k for k in range(8)]
C_RM4 = 128 * 8
C_BD4 = C_RM4 + 4
C_ONE = C_BD4 + 256
NCST = C_ONE + 1


class _Op:
    __slots__ = ("eng", "idx", "fn", "waits", "signal", "kind", "dma_m", "tok")

    def __init__(self, eng, idx, fn, kind):
        self.eng = eng
        self.idx = idx
        self.fn = fn
        self.kind = kind
        self.waits = []
        self.signal = False
        self.dma_m = None
        self.tok = None


class Sched:
    ENGS = ("pe", "act", "dve", "pool", "sp")

    def __init__(self, nc, R=8, epoch=16000):
        self.nc = nc
        self.R = R
        self.epoch = epoch
        self.streams = {e: [] for e in self.ENGS}
        self.ndma = {e: 0 for e in self.ENGS}
        self.dma_ops = {e: [] for e in self.ENGS}
        self.last_w = {}
        self.readers = {}
        self.waited = {e: {p: -1 for p in self.ENGS} for e in self.ENGS}
        self.waited_dma = {e: {} for e in self.ENGS}

    def capture(self):
        self._cap = []

    def end_capture(self):
        lst = self._cap
        self._cap = None
        return lst

    def merge(self, lists):
        pos = [0] * len(lists)
        while True:
            best, bf = -1, 2.0
            for j, l in enumerate(lists):
                if pos[j] < len(l):
                    f = pos[j] / len(l)
                    if f < bf:
                        best, bf = j, f
            if best < 0:
                break
            self._add(*lists[best][pos[best]])
            pos[best] += 1

    def _add(self, eng, fn, reads, writes, kind):
        if getattr(self, "_cap", None) is not None:
            self._cap.append((eng, fn, list(reads), list(writes), kind))
            return None
        self.total = getattr(self, "total", 0) + 1
        if getattr(self, "desc", None) is not None:
            import sys as _sys
            fr = _sys._getframe(3)
            self.desc.append((self.total, eng, kind, fr.f_lineno, fr.f_back.f_lineno))
        if self.total > getattr(self, "limit", 1 << 60) or self.total in getattr(self, "skip", ()):
            return None
        excl = getattr(self, "excl", ())
        if excl and eng != "pe":
            extra = [k for k in reads if k in excl and k not in writes]
            if extra:
                writes = list(writes) + extra
        st = self.streams[eng]
        op = _Op(eng, len(st), fn, kind)
        st.append(op)
        deps = []
        for k in reads:
            w = self.last_w.get(k)
            if w is not None:
                deps.append((w, True))
        for k in writes:
            w = self.last_w.get(k)
            if w is not None:
                deps.append((w, False))
            for r in self.readers.get(k, {}).values():
                deps.append((r, False))
        for (p, raw) in deps:
            if p is op:
                continue
            if p.kind == "c":
                if p.eng == eng:
                    if eng == "pe":
                        continue
                if self.waited[eng][p.eng] >= p.idx:
                    continue
                self.waited[eng][p.eng] = p.idx
                p.signal = True
                op.waits.append(p)
            else:
                key = (p.eng, p.dma_m % self.R)
                if self.waited_dma[eng].get(key, -1) >= p.dma_m:
                    continue
                self.waited_dma[eng][key] = p.dma_m
                op.waits.append(p)
        if kind == "d":
            m = self.ndma[eng]
            self.ndma[eng] = m + 1
            op.dma_m = m
            self.dma_ops[eng].append(op)
            if m >= self.R:
                prev = self.dma_ops[eng][m - self.R]
                key = (eng, m % self.R)
                if self.waited_dma[eng].get(key, -1) < prev.dma_m:
                    self.waited_dma[eng][key] = prev.dma_m
                    op.waits.append(prev)
        for k in writes:
            self.last_w[k] = op
            self.readers[k] = {}
        for k in reads:
            self.readers.setdefault(k, {})[eng] = op
        return op

    def op(self, eng, fn, reads=(), writes=()):
        return self._add(eng, fn, reads, writes, "c")

    def dma(self, eng, fn, reads=(), writes=()):
        return self._add(eng, fn, reads, writes, "d")

    def emit(self, stack):
        nc = self.nc
        for e in self.ENGS:
            n = self.ndma[e]
            if n:
                last = self.dma_ops[e][max(0, n - self.R):]
                f = _Op(e, len(self.streams[e]), None, "f")
                f.waits = list(last)
                self.streams[e].append(f)
        csem = {}
        for e in self.ENGS:
            cnt = 0
            for op in self.streams[e]:
                if op.kind == "c" and op.signal:
                    ep = cnt // self.epoch
                    if (e, ep) not in csem:
                        csem[(e, ep)] = stack.enter_context(nc.semaphore(f"c_{e}_{ep}"))
                    op.tok = (csem[(e, ep)], cnt % self.epoch + 1)
                    cnt += 1
        dsem = {}
        for e in self.ENGS:
            for op in self.dma_ops[e]:
                s = op.dma_m % self.R
                if (e, s) not in dsem:
                    dsem[(e, s)] = stack.enter_context(nc.semaphore(f"d_{e}_{s}"))
                op.tok = (dsem[(e, s)], 16 * (op.dma_m // self.R + 1))
        block = stack.enter_context(nc.Block())

        def replay(e, eng):
            for op in self.streams[e]:
                for p in op.waits:
                    eng.wait_ge(p.tok[0], p.tok[1])
                if op.fn is None:
                    continue
                ins = op.fn(eng)
                if op.kind == "d":
                    ins.then_inc(op.tok[0], 16)
                elif op.signal:
                    ins.then_inc(op.tok[0], 1)

        if self.streams["pe"]:
            @block.tensor
            def _(eng):
                replay("pe", eng)
        if self.streams["act"]:
            @block.scalar
            def _(eng):
                replay("act", eng)
        if self.streams["dve"]:
            @block.vector
            def _(eng):
                replay("dve", eng)
        if self.streams["pool"]:
            @block.gpsimd
            def _(eng):
                replay("pool", eng)
        if self.streams["sp"]:
            @block.sync
            def _(eng):
                replay("sp", eng)


class V:
    __slots__ = ("ap", "key")

    def __init__(self, ap, key):
        self.ap = ap
        self.key = key

    def __getitem__(self, idx):
        return V(self.ap[idx], self.key)

    def r(self, pat, **kw):
        return V(self.ap.rearrange(pat, **kw), self.key)

    def bc(self, axis, shape):
        return V(self.ap.unsqueeze(axis).to_broadcast(list(shape)), self.key)


class TT:
    def __init__(self, t, key):
        self.t = t
        self.key = key

    def __getitem__(self, idx):
        return V(self.t[idx], self.key)

    def k(self, sub):
        return TT(self.t, (self.key, sub))


def _build(seqs, depth, final=True):
    nc = bass.Bass("TRN2", target_bir_lowering=False)
    NT = sum(seqs)
    NS = len(seqs)
    SMAX = max(seqs)
    roff = [sum(seqs[:i]) for i in range(NS)]

    def din(name, shape, dt=F32):
        return nc.dram_tensor(name, list(shape), dt, kind="ExternalInput").ap()

    x_d = TT(din("x", [NT, D]), "x")
    mem_d = TT(din("mem", [NS * 256, D]), "mem")
    win_d = din("w_in", [depth, D, NCOL])
    wout_d = din("w_out", [depth, D, D])
    wmem_d = din("w_mem", [depth, D, 512])
    ng_d = din("norm_g", [depth, D])
    mg_d = din("mem_g", [depth, D])
    pv_d = din("pvec", [depth, NPV])
    wup_d = din("wup", [depth, 32, 256])
    fng_d = din("fng", [D])
    cst_d = din("cst", [128, NCST])
    cstb_d = din("cstb", [128, 640])
    cw_d = din("convw", [depth, 2304])
    rc_d = TT(din("ropec", [SMAX, 384]), "ropec")
    rs_d = TT(din("ropes", [SMAX, 384]), "ropes")
    y_d = TT(nc.dram_tensor("y", [NT, D], F32, kind="ExternalOutput").ap(), "y")
    hbuf = TT(nc.dram_tensor("hbuf", [NT, D], F32, kind="Internal").ap(), "hbuf")
    stb = TT(nc.dram_tensor("stashb", [NT, NB], BF16, kind="Internal").ap(), "stb")
    stf = TT(nc.dram_tensor("stashf", [NT, NF], F32, kind="Internal").ap(), "stf")

    with ExitStack() as st:
        def sb(name, shape, dt=F32):
            return TT(st.enter_context(nc.sbuf_tensor("s_" + name, list(shape), dt)), name)

        def ps(name, shape, dt=F32):
            return TT(st.enter_context(nc.psum_tensor("ps_" + name, list(shape), dt)), name)

        S = Sched(nc)
        import os as _os
        if _os.environ.get("KLIMIT"):
            S.limit = int(_os.environ["KLIMIT"])
        if _os.environ.get("KDESC"):
            S.desc = []
        if _os.environ.get("KSKIP"):
            S.skip = set(int(v) for v in _os.environ["KSKIP"].split(","))

        def keys(*vs):
            return [v.key for v in vs if isinstance(v, V)]

        def apof(v):
            return v.ap if isinstance(v, V) else v

        def act(out, in_, func, scale=1.0, bias=0.0, accum=None):
            kw = dict(out=out.ap, in_=in_.ap, func=func, scale=apof(scale), bias=apof(bias))
            w = [out.key]
            if accum is not None:
                kw["accum_out"] = accum.ap
                w.append(accum.key)
            S.op("act", lambda e, kw=kw: e.activation(**kw), reads=keys(in_, scale, bias), writes=w)

        def ts(out, in0, s1, s2=None, op0=ALU.mult, op1=None, eng="dve"):
            kw = dict(out=out.ap, in0=in0.ap, scalar1=apof(s1), scalar2=apof(s2), op0=op0)
            if op1 is not None:
                kw["op1"] = op1
            S.op(eng, lambda e, kw=kw: e.tensor_scalar(**kw), reads=keys(in0, s1, s2), writes=[out.key])

        def tt(out, in0, in1, op=ALU.mult, eng="dve"):
            S.op(eng, lambda e: e.tensor_tensor(out=out.ap, in0=in0.ap, in1=in1.ap, op=op),
                 reads=keys(in0, in1), writes=[out.key])

        def stt(out, in0, scalar, in1, op0, op1, eng="dve"):
            S.op(eng, lambda e: e.scalar_tensor_tensor(out=out.ap, in0=in0.ap, scalar=apof(scalar), in1=in1.ap,
                                                       op0=op0, op1=op1),
                 reads=keys(in0, scalar, in1), writes=[out.key])

        def cp(out, in_, eng="dve"):
            if eng == "act":
                S.op("act", lambda e: e.copy(out=out.ap, in_=in_.ap), reads=[in_.key], writes=[out.key])
            else:
                S.op(eng, lambda e: e.tensor_copy(out=out.ap, in_=in_.ap), reads=[in_.key], writes=[out.key])

        def red(out, in_, op=ALU.add):
            S.op("dve", lambda e: e.tensor_reduce(out=out.ap, in_=in_.ap, axis=AX.X, op=op),
                 reads=[in_.key], writes=[out.key])

        def recip(out, in_):
            S.op("dve", lambda e: e.reciprocal(out=out.ap, in_=in_.ap), reads=[in_.key], writes=[out.key])

        def memset(out, val, eng="pool"):
            S.op(eng, lambda e: e.memset(out.ap, val), writes=[out.key])

        def mm(out, lhsT, rhs, start=True, stop=True, skip=False):
            S.op("pe", lambda e: e.matmul(out.ap, lhsT=lhsT.ap, rhs=rhs.ap, start=start, stop=stop,
                                          skip_group_check=skip),
                 reads=keys(lhsT, rhs), writes=[out.key])

        def tr(out, in_, ident):
            S.op("pe", lambda e: e.transpose(out=out.ap, in_=in_.ap, identity=ident.ap),
                 reads=keys(in_, ident), writes=[out.key])

        def dma(eng, out, in_, slow=False):
            if slow:
                S.dma(eng, lambda e: e.dma_start(out=out.ap, in_=in_.ap, allow_slow_non_contiguous=True),
                      reads=[in_.key], writes=[out.key])
            else:
                S.dma(eng, lambda e: e.dma_start(out=out.ap, in_=in_.ap), reads=[in_.key], writes=[out.key])

        def rsqrt_(out, in_, scale, eps):
            act(out, in_, AF.Ln, scale=scale, bias=eps)
            act(out, out, AF.Exp, scale=-0.5)

        cst = sb("cst", [128, NCST])
        cstb = sb("cstb", [128, 5 * 128], BF16)
        win = sb("win", [128, 8, NCOL], BF16)
        wout = sb("wout", [128, 8, D], BF16)
        wmem = sb("wmem", [128, 8, 512], BF16)
        ngt = sb("ngt", [128, 8])
        mgt = sb("mgt", [128, 8])
        pv = sb("pv", [128, NPV])
        wup = sb("wup", [32, 256])
        negA = sb("negA", [128, 8])
        KT = sb("KT", [128, SMAX], BF16)
        qm = sb("qm", [128, 4, 128], BF16)
        Vaug = sb("Vaug", [128, SMAX // 128, 2, 65], BF16)
        kmT = sb("kmT", [128, 2, 2, 256], BF16)
        Vmaug = sb("Vmaug", [128, 2, 4, 65], BF16)
        hin = sb("hin", [128, D])
        ssq = sb("ssq", [128, 1])
        rstd = sb("rstd", [128, 1])
        xn = sb("xn", [128, D], BF16)
        xnT = sb("xnT", [128, D], BF16)
        rc = sb("rc", [128, 384])
        rs = sb("rs", [128, 384])
        ss8 = sb("ss8", [128, 8])
        rn8 = sb("rn8", [128, 8])
        qkrot = sb("qkrot", [128, 384], BF16)
        we = sb("we", [128, 768])
        xw = [sb(f"xw{k}", [128, 3, 768], BF16) for k in range(3)]
        lg2 = sb("lg2", [128, 16])
        bgf = [sb(f"bgf{k}", [128, 8]) for k in range(2)]
        ngkf = [sb(f"ngkf{k}", [128, 128]) for k in range(2)]
        low = sb("low", [128, 32])
        lowT = sb("lowT", [32, 128])
        gl = sb("gl", [128, 256])
        Tb = [sb(f"Tb{k}", [128, NB], BF16) for k in range(2)]
        Tf = [sb(f"Tf{k}", [128, NF]) for k in range(2)]
        wqB = sb("wqB", [128, 384])
        ss8b = sb("ss8b", [128, 8])
        rn8b = sb("rn8b", [128, 8])
        convw = sb("convw", [128, 2304], BF16)
        lgr = [sb(f"lgr{k}", [128, 16]) for k in range(2)]
        cv = sb("cv", [128, 768])
        gDN = sb("gDN", [128, 4, 128])
        gDA = sb("gDA", [128, 4, 128])
        gsm = sb("gsm", [128, 24])
        gcd = sb("gcd", [128, 2, 4])
        gN1 = sb("gN", [128, 4, 128], BF16)
        gG = sb("gG", [128, 4, 128])
        gA1 = sb("gA", [128, 4, 128], BF16)
        gP1 = sb("gP", [128, 4, 128], BF16)
        gN = [gN1, gN1]
        gA = [gA1, gA1]
        gP = [gP1, gP1]
        gkT = sb("gkT", [128, 2, 128], BF16)
        gkTm = sb("gkTm", [128, 2, 2, 128], BF16)
        gkdm = sb("gkdm", [128, 2, 256], BF16)
        gsm2 = sb("gsm2", [128, 8])
        lkdm = sb("lkdm", [128, 2, 128], BF16)
        gqT = sb("gqT", [128, 2, 128], BF16)
        gvk = sb("gvk", [128, 4, 128], BF16)
        guw = sb("guw", [128, 4, 128])
        gwb = sb("gwb", [128, 256], BF16)
        gwT = sb("gwT", [128, 2, 128], BF16)
        gqd = sb("gqd", [128, 256], BF16)
        gqdT = sb("gqdT", [128, 2, 128], BF16)
        gkd = sb("gkd", [128, 256], BF16)
        gqkm = sb("gqkm", [128, 4, 128], BF16)
        gvn = sb("gvn", [128, 256], BF16)
        gS = [[sb(f"gS{d}{p}", [128, 128]) for p in range(2)] for d in range(2)]
        gSb = [[sb(f"gSb{d}{p}", [128, 128], BF16) for p in range(2)] for d in range(2)]
        obcs = [sb(f"obc{k}", [128, 512]) for k in range(2)]
        ymbs = [sb(f"ymb{k}", [128, 256], BF16) for k in range(2)]
        ltmp = sb("ltmp", [128, 256])
        lex = sb("lex", [128, 3, 128])
        lcd = sb("lcd", [128, 2])
        lqe = sb("lqe", [128, 128], BF16)
        lke = sb("lke", [128, 128], BF16)
        lkd = sb("lkd", [128, 128], BF16)
        lqeT = sb("lqeT", [128, 128], BF16)
        lkeT = sb("lkeT", [128, 128], BF16)
        lqblk = sb("lqblk", [128, 4, 128], BF16)
        latt = sb("latt", [128, 4, 128], BF16)
        lS = [sb(f"lS{d}", [128, 256]) for d in range(2)]
        lSb = [sb(f"lSb{d}", [128, 256], BF16) for d in range(2)]
        os2 = sb("os2", [128, 512])
        osum = TT(os2.t[:, 0:256], "os2")
        sq = we
        ycat = xn
        ycatT = xnT
        junk = xn
        xc = cv
        wst = hin
        hout = hin
        pT = sb("pT", [128, 1024], BF16)
        wA = TT(pT.t[:, :].bitcast(F32), "pT")
        mqT = sb("mqT", [128, 2, 128], BF16)
        rs4 = sb("rs4", [128, 4])
        ya = osum
        kmtok = TT(qkrot.t[:, 0:256], "qkrot")

        ptr = ps("ptr", [128, 1024], BF16)
        pa = ps("pa", [128, 512])
        pb = ps("pb", [128, 512])
        p2 = ps("p2", [128, 512])
        p3 = ps("p3", [128, 512])
        sc = ps("sc", [128, 1024])
        p6 = ps("p6", [128, 512])
        sc0 = TT(sc.t[:, 0:512], "sc0")
        sc1 = TT(sc.t[:, 512:1024], "sc1")
        S.excl = {"ptr", "pa", "pb", "p2", "p3", "sc0", "sc1", "p6"}
        p6b = TT(p6.t[:, :].bitcast(BF16), "p6")
        pbb = TT(pb.t[:, :].bitcast(BF16), "pb")
        rs4a = sb("rs4a", [128, 4])
        yaa = sb("yaa", [128, 256])

        ident = cst[:, C_ID:C_ID + 128]
        identb = cstb[:, 0:128]
        SHPb, SHNb, HPb, HNb = [cstb[:, 128 * k:128 * (k + 1)] for k in range(1, 5)]
        mLE, mLT, mGE, mGT, mBD = [cst[:, c:c + 128] for c in (C_LE, C_LT, C_GE, C_GT, C_BD)]
        SEL = [cst[:, C_SEL0:C_SEL0 + 128], cst[:, C_SEL1:C_SEL1 + 128]]
        RM4 = cst[:, C_RM4:C_RM4 + 4]
        BD4 = cst[:, C_BD4:C_BD4 + 256]
        ONE = cst[:, C_ONE:C_ONE + 1]

        dma("sp", cst[:, :], V(cst_d, "cst_d"))
        dma("sp", hin[:, 0:640], V(cstb_d, "cstb_d"))
        cp(cstb[:, :], hin[:, 0:640], eng="pool")
        memset(Vaug[:, :, :, :], 1.0)
        memset(Vmaug[:, :, :, :], 1.0)
        memset(gvn[:, :], 0.0)

        def rms_to_xnT(src, pt=None):
            pt = ptr if pt is None else pt
            act(junk[:, :], src, AF.Square, accum=ssq[:, :])
            rsqrt_(rstd[:, :], ssq[:, :], 1.0 / D, EPS)
            ts(xn[:, :], src, rstd[:, :], None, ALU.mult)
            for c in range(8):
                tr(pt[:, c * 128:(c + 1) * 128], xn[:, c * 128:(c + 1) * 128], identb)
            cp(xnT[:, :], pt[:, :], eng="act")

        def sigmoid_from(dst_e, src, n):
            act(dst_e, src, AF.Exp, scale=-1.0)
            ts(dst_e, dst_e, 1.0, None, ALU.add)
            recip(dst_e, dst_e)

        def gdn_core(d, qn, kn, vv, beta, g, o_out, first):
            M1, GS, NSm, AI = (mLE, mGT, mGT, mLE) if d == 0 else (mGE, mLT, mLT, mGE)
            if first:
                for p in range(2):
                    memset(gS[d][p][:, :], 0.0)
                    memset(gSb[d][p][:, :], 0.0)
            for p in range(2):
                tr(ptr[:, p * 128:(p + 1) * 128], kn[:, p * 128:(p + 1) * 128], identb)
                tr(ptr[:, 256 + p * 128:256 + (p + 1) * 128], qn[:, p * 128:(p + 1) * 128], identb)
            cp(gkT[:, :, :], ptr[:, 0:256].r("p (a b) -> p a b", a=2), eng="act")
            for q in range(2):
                ts(gkTm[:, :, q, :], ptr[:, 0:256].r("p (a b) -> p a b", a=2), mBD[:, 64 * q:64 * q + 1], None, ALU.mult)
            cp(gqT[:, :, :], ptr[:, 256:512].r("p (a b) -> p a b", a=2), eng="act")
            tt(gG[:, :, :], GS.bc(1, [128, 4, 128]), g.bc(2, [128, 4, 128]), ALU.mult)
            mm(p2[:, :], M1, gG[:, :, :].r("p a b -> p (a b)"))
            for h in range(4):
                mm(p3[:, h * 128:(h + 1) * 128], gG[:, h, :], M1)
            act(gDN[:, :, :].r("p a b -> p (a b)"), p2[:, :], AF.Exp)
            act(gDA[:, :, :].r("p a b -> p (a b)"), p3[:, :], AF.Exp)
            tt(gDN[:, :, :], gDN[:, :, :], NSm.bc(1, [128, 4, 128]), ALU.mult)
            tt(gDA[:, :, :], gDA[:, :, :], AI.bc(1, [128, 4, 128]), ALU.mult, eng="pool")
            mm(pa[:, 0:4], M1, g)
            mm(pa[:, 4:8], mBD, g)
            mm(pa[:, 8:12], SEL[0], g)
            mm(pa[:, 12:16], SEL[1], g)
            cp(gsm[:, 0:4], pa[:, 0:4])
            act(gsm[:, 4:8], pa[:, 0:4], AF.Exp)
            tt(gsm[:, 8:12], pa[:, 4:8], gsm[:, 0:4], ALU.subtract)
            act(gsm[:, 8:12], gsm[:, 8:12], AF.Exp)
            act(gcd[:, :, :].r("p a b -> p (a b)"), pa[:, 8:16], AF.Exp)
            ts(gsm[:, 12:16], beta, -1.0, None, ALU.mult)
            tt(gsm[:, 16:20], beta, gsm[:, 4:8], ALU.mult)
            for h in range(4):
                pr, hb = h // 2, (h % 2) * 64
                mm(p2[:, h * 128:(h + 1) * 128], gkT[:, pr, :], gkTm[:, pr, h % 2, :])
            for h in range(4):
                stt(gN[0][:, h, :], p2[:, h * 128:(h + 1) * 128], gsm[:, 12 + h:13 + h], gDN[:, h, :],
                    ALU.mult, ALU.mult)
            for h in range(4):
                tr(ptr[:, h * 128:(h + 1) * 128], gN[0][:, h, :], identb)
            cp(gA[0][:, :, :].r("p a b -> p (a b)"), ptr[:, 0:512], eng="act")
            tt(gP[0][:, :, :], gA[0][:, :, :], ident.bc(1, [128, 4, 128]), ALU.add)
            cur = 0
            pc = 0
            for lvl in range(5):
                nxt = 1 - cur
                last = lvl == 4
                for h in range(4):
                    mm(p2[:, h * 128:(h + 1) * 128], gA[cur][:, h, :], gN[cur][:, h, :])
                if not last:
                    for h in range(4):
                        mm(p3[:, h * 128:(h + 1) * 128], gN[cur][:, h, :], gA[cur][:, h, :])
                cp(gN[nxt][:, :, :].r("p a b -> p (a b)"), p2[:, :], eng="dve")
                if not last:
                    cp(gA[nxt][:, :, :].r("p a b -> p (a b)"), p3[:, :], eng="act")
                for h in range(4):
                    mm(pa[:, h * 128:(h + 1) * 128], gN[nxt][:, h, :], gP[pc][:, h, :])
                tt(gP[1 - pc][:, :, :].r("p a b -> p (a b)"), pa[:, :], gP[pc][:, :, :].r("p a b -> p (a b)"),
                   ALU.add)
                pc = 1 - pc
                cur = nxt
            Pm = gP[pc]
            tt(gvk[:, :, 0:64], vv.r("p (a b) -> p a b", a=4), beta.bc(2, [128, 4, 64]), ALU.mult, eng="pool")
            tt(gvk[:, :, 64:128], kn.r("p (a b) -> p a b", a=4), gsm[:, 16:20].bc(2, [128, 4, 64]), ALU.mult,
               eng="pool")
            for h in range(4):
                mm(p3[:, h * 128:(h + 1) * 128], Pm[:, h, :], gvk[:, h, :])
            cp(guw[:, :, :].r("p a b -> p (a b)"), p3[:, :], eng="act")
            cp(gwb[:, :].r("p (a b) -> p a b", a=4), guw[:, :, 64:128], eng="dve")
            for p in range(2):
                tr(ptr[:, 512 + p * 128:512 + (p + 1) * 128], gwb[:, p * 128:(p + 1) * 128], identb)
            cp(gwT[:, :, :], ptr[:, 512:768].r("p (a b) -> p a b", a=2), eng="act")
            tt(gqd[:, :].r("p (a b) -> p a b", a=4), qn.r("p (a b) -> p a b", a=4),
               gsm[:, 4:8].bc(2, [128, 4, 64]), ALU.mult, eng="pool")
            for p in range(2):
                tr(ptr[:, p * 128:(p + 1) * 128], gqd[:, p * 128:(p + 1) * 128], identb)
            cp(gqdT[:, :, :], ptr[:, 0:256].r("p (a b) -> p a b", a=2), eng="act")
            for c in range(2):
                ts(gsm2[:, 4 * c:4 * c + 4], gsm[:, 8:12], mBD[:, 64 * c:64 * c + 1], None, ALU.mult)
                tt(gkdm[:, c, :].r("p (a b) -> p a b", a=4), kn.r("p (a b) -> p a b", a=4),
                   gsm2[:, 4 * c:4 * c + 4].bc(2, [128, 4, 64]), ALU.mult, eng="pool")
            for h in range(4):
                pr, hb = h // 2, (h % 2) * 64
                mm(p2[:, h * 128:(h + 1) * 128], gkTm[:, pr, h % 2, :], gqT[:, pr, :])
            tt(gqkm[:, :, :].r("p a b -> p (a b)"), p2[:, :], gDA[:, :, :].r("p a b -> p (a b)"), ALU.mult)
            for c in ((0, 1) if d == 0 else (1, 0)):
                r0, r1 = 64 * c, 64 * c + 64
                for p in range(2):
                    mm(p3[:, p * 128:(p + 1) * 128], gwT[:, p, :], gSb[d][p][:, :])
                tt(gvn[r0:r1, :].r("p (a b) -> p a b", a=4), guw[r0:r1, :, 0:64],
                   p3[r0:r1, 0:256].r("p (a b) -> p a b", a=4), ALU.subtract)
                for h in range(4):
                    pr, hb = h // 2, (h % 2) * 64
                    mm(pa[:, h * 64:(h + 1) * 64], gqdT[:, pr, :], gSb[d][pr][:, hb:hb + 64], start=True, stop=False)
                    mm(pa[:, h * 64:(h + 1) * 64], gqkm[:, h, :], gvn[:, h * 64:(h + 1) * 64],
                       start=False, stop=True)
                cp(o_out[r0:r1, :], pa[r0:r1, 0:256], eng="act")
                for p in range(2):
                    mm(p2[:, p * 128:(p + 1) * 128], gkdm[:, c, p * 128:(p + 1) * 128],
                       gvn[:, p * 128:(p + 1) * 128])
                for p in range(2):
                    for q in range(2):
                        h = 2 * p + q
                        b0, b1 = 64 * q, 64 * q + 64
                        stt(gS[d][p][b0:b1, b0:b1], gS[d][p][b0:b1, b0:b1], gcd[b0:b1, c, h:h + 1],
                            p2[b0:b1, p * 128 + b0:p * 128 + b1], ALU.mult, ALU.add)
                    cp(gSb[d][p][:, :], gS[d][p][:, :], eng="dve")

        def gla_core(d, lq, lk, lv, ngk, o_out, first):
            M1, GS, AI = (mLE, mGT, mLE) if d == 0 else (mGE, mLT, mGE)
            if first:
                memset(lS[d][:, :], 0.0)
                memset(lSb[d][:, :], 0.0)
            mm(sc1[:, 0:128], M1, ngk)
            mm(sc1[:, 128:256], GS, ngk)
            for c in range(2):
                mm(sc1[:, 256 + c:257 + c], ngk, mBD[:, 64 * c:64 * c + 1])
            act(lex[:, 0, :], sc1[:, 0:128], AF.Exp, scale=-1.0 / 16)
            act(lex[:, 1, :], sc1[:, 0:128], AF.Exp, scale=1.0 / 16)
            act(lex[:, 2, :], sc1[:, 128:256], AF.Exp, scale=-1.0 / 16)
            act(lcd[:, :], sc1[:, 256:258], AF.Exp, scale=-1.0 / 16)
            tt(lqe[:, :], lq, lex[:, 0, :], ALU.mult)
            tt(lke[:, :], lk, lex[:, 1, :], ALU.mult, eng="pool")
            tt(lkd[:, :], lk, lex[:, 2, :], ALU.mult, eng="pool")
            for c in range(2):
                ts(lkdm[:, c, :], lkd[:, :], mBD[:, 64 * c:64 * c + 1], None, ALU.mult)
            tr(ptr[:, 768:896], lqe[:, :], identb)
            tr(ptr[:, 896:1024], lke[:, :], identb)
            cp(lqeT[:, :], ptr[:, 768:896], eng="act")
            cp(lkeT[:, :], ptr[:, 896:1024], eng="act")
            tt(lqblk[:, :, :], ptr[:, 768:896].bc(1, [128, 4, 128]), RM4.bc(2, [128, 4, 128]), ALU.mult)
            mm(pb[:, :], lkeT[:, :], lqblk[:, :, :].r("p a b -> p (a b)"))
            tt(latt[:, :, :], pb[:, :].r("p (a b) -> p a b", a=4), AI.bc(1, [128, 4, 128]), ALU.mult)
            for c in ((0, 1) if d == 0 else (1, 0)):
                r0, r1 = 64 * c, 64 * c + 64
                for h in range(4):
                    mm(sc1[:, h * 64:(h + 1) * 64], lqeT[:, :], lSb[d][:, h * 64:(h + 1) * 64], start=True, stop=False)
                    mm(sc1[:, h * 64:(h + 1) * 64], latt[:, h, :], lv[:, h * 64:(h + 1) * 64],
                       start=False, stop=True)
                cp(o_out[r0:r1, :], sc1[r0:r1, 0:256], eng="act")
                mm(sc1[:, 256:512], lkdm[:, c, :], lv)
                tt(ltmp[:, :], sc1[:, 256:512], BD4, ALU.mult)
                stt(lS[d][:, :], lS[d][:, :], lcd[:, c:c + 1], ltmp[:, :], ALU.mult, ALU.add)
                cp(lSb[d][:, :], lS[d][:, :], eng="dve")

        for l in range(depth):
            last_layer = (l == depth - 1)
            dma("sp", ngt[:, :], V(ng_d[l].rearrange("(c p) -> p c", p=128), "ng_d"), slow=True)
            dma("sp", mgt[:, :], V(mg_d[l].rearrange("(c p) -> p c", p=128), "mg_d"), slow=True)
            dma("sp", pv[:, :], V(pv_d[l].partition_broadcast(128), "pv_d"))
            dma("sp", wup[:, :], V(wup_d[l], "wup_d"))
            for c in range(8):
                for (j0, j1) in ((0, 1024), (1024, 2048), (2048, 3072), (3072, 3120)):
                    dma("sp", wst[:, 0:j1 - j0], V(win_d[l, c * 128:(c + 1) * 128, j0:j1], "win_d"))
                    ts(win[:, c, j0:j1], wst[:, 0:j1 - j0], ngt[:, c:c + 1], None, ALU.mult)
                dma("sp", wst[:, 0:1024], V(wout_d[l, c * 128:(c + 1) * 128, :], "wout_d"))
                cp(wout[:, c, :], wst[:, 0:1024], eng="pool")
                dma("sp", wst[:, 0:512], V(wmem_d[l, c * 128:(c + 1) * 128, :], "wmem_d"))
                ts(wmem[:, c, :], wst[:, 0:512], mgt[:, c:c + 1], None, ALU.mult)
            for j in range(3):
                dma("sp", wst[:, 0:768], V(cw_d[l, j * 768:(j + 1) * 768].partition_broadcast(128), "cw_d"))
                cp(convw[:, j * 768:(j + 1) * 768], wst[:, 0:768], eng="pool")
            act(negA[:, :], pv[:, P_ALOG:P_ALOG + 8], AF.Exp)
            ts(negA[:, :], negA[:, :], -1.0, None, ALU.mult)

            for s in range(NS):
                Sq = seqs[s]
                n = Sq // 128
                R0 = roff[s]
                src_h = x_d if l == 0 else hbuf

                for mt in range(2):
                    dma("sp", hin[:, :], mem_d[s * 256 + mt * 128:s * 256 + (mt + 1) * 128, :])
                    rms_to_xnT(hin[:, :])
                    for c in range(8):
                        mm(pa[:, :], xnT[:, c * 128:(c + 1) * 128], wmem[:, c, :], start=(c == 0), stop=(c == 7))
                    cp(kmtok[:, :], pa[:, 0:256], eng="act")
                    cp(Vmaug[:, mt, :, 0:64], pa[:, 256:512].r("p (a b) -> p a b", a=4), eng="dve")
                    for p in range(2):
                        tr(ptr[:, p * 128:(p + 1) * 128], kmtok[:, p * 128:(p + 1) * 128], identb)
                    for q in range(2):
                        ts(kmT[:, :, q, mt * 128:(mt + 1) * 128], ptr[:, 0:256].r("p (a b) -> p a b", a=2),
                           mBD[:, 64 * q:64 * q + 1], None, ALU.mult)

                def P_stage(i, phase):
                    par = i % 2
                    rows = slice(R0 + i * 128, R0 + (i + 1) * 128)
                    tb, tf = Tb[par], Tf[par]
                    if phase == 0:
                        dma("sp", hin[:, :], src_h.k(R0 // 128 + i)[rows, :])
                        dma("sp", rc[:, :], rc_d[i * 128:(i + 1) * 128, :])
                        dma("sp", rs[:, :], rs_d[i * 128:(i + 1) * 128, :])
                        rms_to_xnT(hin[:, :], p6b)
                    groups = [(0, 512), (512, 1024), (1024, 1536), (1536, 2048), (2048, 2320), (2320, 2832),
                              (2832, 3120)]
                    for gi, (c0, c1) in enumerate(groups):
                        if (gi in (3, 4)) != (phase == 0):
                            continue
                        pp = p6 if gi % 2 == 0 else sc0
                        w = c1 - c0
                        for c in range(8):
                            mm(pp[:, 0:w], xnT[:, c * 128:(c + 1) * 128], win[:, c, c0:c1], start=(c == 0),
                               stop=(c == 7))
                        if gi == 0:
                            act(wA[:, 0:384], pp[:, 0:384], AF.Square)
                            red(ss8b[:, 0:6], wA[:, 0:384].r("p (a b) -> p a b", a=6))
                            rsqrt_(rn8b[:, 0:6], ss8b[:, 0:6], 1.0 / 64, EPS)
                            tt(wqB[:, :].r("p (a b) -> p a b", a=6), pp[:, 0:384].r("p (a b) -> p a b", a=6),
                               rn8b[:, 0:6].bc(2, [128, 6, 64]), ALU.mult)
                            cp(Vaug[:, i, :, 0:64], pp[:, 384:512].r("p (a b) -> p a b", a=2), eng="dve")
                            tt(wqB[:, :], wqB[:, :], pv[:, P_GQK:P_GQK + 384], ALU.mult, eng="pool")
                            v5 = lambda t: t[:, 0:384].r("p (a b c) -> p a b c", a=12, b=2)
                            tt(v5(wA)[:, :, 0, :], v5(wqB)[:, :, 1, :], v5(rs)[:, :, 0, :], ALU.mult, eng="pool")
                            tt(v5(wA)[:, :, 1, :], v5(wqB)[:, :, 0, :], v5(rs)[:, :, 1, :], ALU.mult, eng="pool")
                            tt(wqB[:, :], wqB[:, :], rc[:, :], ALU.mult, eng="pool")
                            tt(tb[:, B_Q:B_Q + 256], wqB[:, 0:256], wA[:, 0:256], ALU.add)
                            tt(qkrot[:, 0:128], wqB[:, 256:384], wA[:, 256:384], ALU.add)
                            tr(p6b[:, 0:128], qkrot[:, 0:128], identb)
                            cp(KT[:, i * 128:(i + 1) * 128], p6b[:, 0:128], eng="act")
                        elif gi in (1, 2):
                            sigmoid_from(wA[:, 0:512], pp[:, 0:512], 512)
                            off = B_GA if gi == 1 else B_GD
                            tt(tb[:, off:off + 512], pp[:, 0:512], wA[:, 0:512], ALU.mult)
                        elif gi == 3:
                            for k3 in range(3):
                                tt(xw[i % 3][:, k3, 0:512], pp[:, 0:512], convw[:, 768 * k3:768 * k3 + 512], ALU.mult)
                        elif gi == 4:
                            cp(lgr[par][:, :], pp[:, 256:272], eng="dve")
                            for k3 in range(3):
                                tt(xw[i % 3][:, k3, 512:768], pp[:, 0:256], convw[:, 768 * k3 + 512:768 * (k3 + 1)],
                                   ALU.mult)
                        elif gi == 5:
                            act(tf[:, F_LQ:F_LQ + 128], pp[:, 0:128], AF.Copy, scale=32.0 ** -0.5)
                            cp(tf[:, F_LK:F_LK + 128], pp[:, 128:256], eng="act")
                            cp(tb[:, B_LV:B_LV + 256], pp[:, 256:512], eng="act")
                        else:
                            cp(tb[:, B_MQ:B_MQ + 256], pp[:, 0:256], eng="act")
                            cp(low[:, :], pp[:, 256:288], eng="dve")
                            tr(sc0[0:32, 0:128], low[:, :], ident)
                            cp(lowT[:, :], sc0[0:32, 0:128], eng="act")
                            mm(sc0[:, 128:384], lowT[:, :], wup[:, :])
                            tt(gl[:, :], sc0[:, 128:384], pv[:, P_GLB:P_GLB + 256], ALU.add)
                            act(gl[:, :], gl[:, :], AF.Exp, scale=-1.0)
                            act(ngkf[par][:, :], gl[:, 0:128], AF.Ln, bias=1.0)
                            act(tf[:, F_NGK:F_NGK + 128], gl[:, 128:256], AF.Ln, bias=1.0)

                    if phase == 1:
                        lg = lgr[par]
                        sigmoid_from(lg2[:, 0:8], lg[:, 0:8], 8)
                        tt(lg2[:, 8:16], lg[:, 8:16], pv[:, P_DTB:P_DTB + 8], ALU.add)
                        act(lg2[:, 8:16], lg2[:, 8:16], AF.Exp)
                        act(lg2[:, 8:16], lg2[:, 8:16], AF.Ln, bias=1.0)
                        tt(lg2[:, 8:16], lg2[:, 8:16], negA[:, :], ALU.mult)
                        cp(bgf[par][:, 0:4], lg2[:, 0:4], eng="pool")
                        cp(bgf[par][:, 4:8], lg2[:, 8:12], eng="pool")
                        cp(tf[:, F_BG:F_BG + 4], lg2[:, 4:8], eng="pool")
                        cp(tf[:, F_BG + 4:F_BG + 8], lg2[:, 12:16], eng="pool")

                def post(i, extra=None):
                    par = i % 2
                    rows = slice(R0 + i * 128, R0 + (i + 1) * 128)
                    tb, tf = Tb[par], Tf[par]
                    S.capture()
                    for (c0, c1, pp) in ((0, 512, p2), (512, 768, p3)):
                        w = c1 - c0
                        ops = [(SHPb, xw[i % 3][:, 0, c0:c1]), (identb, xw[i % 3][:, 1, c0:c1]),
                               (SHNb, xw[i % 3][:, 2, c0:c1])]
                        if i > 0:
                            ops.append((HPb, xw[(i - 1) % 3][:, 0, c0:c1]))
                        if i < n - 1:
                            ops.append((HNb, xw[(i + 1) % 3][:, 2, c0:c1]))
                        for k3, (lh, rh) in enumerate(ops):
                            mm(pp[:, 0:w], lh, rh, start=(k3 == 0), stop=(k3 == len(ops) - 1))
                        sigmoid_from(we[:, c0:c1], pp[:, 0:w], w)
                        tt(cv[:, c0:c1], pp[:, 0:w], we[:, c0:c1], ALU.mult)
                    act(sq[:, 0:512], cv[:, 0:512], AF.Square)
                    red(ss8[:, :], sq[:, 0:512].r("p (a b) -> p a b", a=8))
                    rsqrt_(rn8[:, :], ss8[:, :], 1.0, EPS)
                    ts(rn8[:, 0:4], rn8[:, 0:4], 0.125, None, ALU.mult)
                    tt(tb[:, B_GQ:B_GQ + 512].r("p (a b) -> p a b", a=8), cv[:, 0:512].r("p (a b) -> p a b", a=8),
                       rn8[:, :].bc(2, [128, 8, 64]), ALU.mult)
                    cp(tb[:, B_GV:B_GV + 256], cv[:, 512:768], eng="act")
                    gdn_core(0, tb[:, B_GQ:B_GQ + 256], tb[:, B_GK:B_GK + 256], tb[:, B_GV:B_GV + 256],
                             bgf[par][:, 0:4], bgf[par][:, 4:8], tf[:, F_OB:F_OB + 256], first=(i == 0))
                    LG1 = S.end_capture()
                    S.capture()
                    gla_core(0, tf[:, F_LQ:F_LQ + 128], tf[:, F_LK:F_LK + 128], tb[:, B_LV:B_LV + 256],
                             ngkf[par][:, :], tf[:, F_OC:F_OC + 256], first=(i == 0))
                    LL1 = S.end_capture()
                    S.merge([LG1, LL1] + ([extra] if extra else []))
                    dma("pool", stb.k(R0 // 128 + i)[rows, :], tb[:, :])
                    dma("pool", stf.k(R0 // 128 + i)[rows, :], tf[:, :])

                P_stage(0, 0)
                for i in range(n + 1):
                    LP = None
                    if i < n:
                        S.capture()
                        P_stage(i, 1)
                        if i + 1 < n:
                            P_stage(i + 1, 0)
                        LP = S.end_capture()
                    if i >= 1:
                        post(i - 1, LP)
                    elif LP:
                        S.merge([LP])

                if last_layer and final:
                    dma("sp", cv[:, 0:768], V(fng_d[0:768].partition_broadcast(128), "fng_d"))
                    dma("sp", we[:, 512:768], V(fng_d[768:1024].partition_broadcast(128), "fng_d"))
                def tail(t):
                    rows_t = slice(R0 + t * 128, R0 + (t + 1) * 128)
                    dma("sp", hin[:, :], src_h.k(R0 // 128 + t)[rows_t, :])
                    tt(os2[:, :], obcs[t % 2][:, :], Tf[t % 2][:, F_OB:F_OB + 512], ALU.add)
                    act(sq[:, 0:512], os2[:, :], AF.Square)
                    red(ss8[:, 0:8], sq[:, 0:512].r("p (a b) -> p a b", a=8))
                    rsqrt_(rn8[:, 0:8], ss8[:, 0:8], 1.0 / 64, EPS)
                    tt(os2[:, :].r("p (a b) -> p a b", a=8), os2[:, :].r("p (a b) -> p a b", a=8),
                       rn8[:, 0:8].bc(2, [128, 8, 64]), ALU.mult)
                    tt(os2[:, :], os2[:, :], pv[:, P_GOG:P_GOG + 512], ALU.mult, eng="pool")
                    tt(ycat[:, 256:768], os2[:, :], Tb[t % 2][:, B_GD:B_GD + 512], ALU.mult)
                    for c in range(8):
                        src_c = ycat[:, c * 128:(c + 1) * 128] if c < 6 else ymbs[t % 2][:, (c - 6) * 128:(c - 5) * 128]
                        tr(pbb[:, c * 128:(c + 1) * 128], src_c, identb)
                    cp(ycatT[:, :], pbb[:, :], eng="act")
                    for half, pp in ((0, sc1), (1, pb)):
                        for c in range(8):
                            mm(pp[:, :], ycatT[:, c * 128:(c + 1) * 128], wout[:, c, half * 512:(half + 1) * 512],
                               start=(c == 0), stop=(c == 7))
                        tt(hout[:, half * 512:(half + 1) * 512], pp[:, :], hin[:, half * 512:(half + 1) * 512], ALU.add)
                    if last_layer and final:
                        act(junk[:, :], hout[:, :], AF.Square, accum=ssq[:, :])
                        rsqrt_(rstd[:, :], ssq[:, :], 1.0 / D, EPS)
                        ts(hout[:, :], hout[:, :], rstd[:, :], None, ALU.mult)
                        tt(hout[:, 0:768], hout[:, 0:768], cv[:, 0:768], ALU.mult, eng="pool")
                        tt(hout[:, 768:1024], hout[:, 768:1024], we[:, 512:768], ALU.mult, eng="pool")
                        dma("pool", y_d.k(R0 // 128 + t)[rows_t, :], hout[:, :])
                    elif last_layer:
                        dma("pool", y_d.k(R0 // 128 + t)[rows_t, :], hout[:, :])
                    else:
                        dma("pool", hbuf.k(R0 // 128 + t)[rows_t, :], hout[:, :])
                nkt = n
                prev = None
                for i in range(n - 1, -1, -1):
                    rows = slice(R0 + i * 128, R0 + (i + 1) * 128)
                    Tb2, Tf2 = Tb[i % 2], Tf[i % 2]
                    dma("sp", Tb2[:, :], stb.k(R0 // 128 + i)[rows, :])
                    dma("sp", Tf2[:, :], stf.k(R0 // 128 + i)[rows, :])
                    first = (i == n - 1)
                    S.capture()
                    gdn_core(1, Tb2[:, B_GQ:B_GQ + 256], Tb2[:, B_GK:B_GK + 256], Tb2[:, B_GV:B_GV + 256],
                             Tf2[:, F_BG:F_BG + 4], Tf2[:, F_BG + 4:F_BG + 8], obcs[i % 2][:, 0:256], first=first)
                    LG = S.end_capture()
                    S.capture()
                    if prev is not None:
                        tail(prev)
                    gla_core(1, Tf2[:, F_LQ:F_LQ + 128], Tf2[:, F_LK:F_LK + 128], Tb2[:, B_LV:B_LV + 256],
                             Tf2[:, F_NGK:F_NGK + 128], obcs[i % 2][:, 256:512], first=first)
                    for p in range(2):
                        tr(ptr[:, 768 + p * 128:768 + (p + 1) * 128], Tb2[:, B_MQ + p * 128:B_MQ + (p + 1) * 128], identb)
                    cp(mqT[:, :, :], ptr[:, 768:1024].r("p (a b) -> p a b", a=2), eng="act")
                    for hp in range(2):
                        for hq in range(2):
                            h = 2 * hp + hq
                            pr, _hb = h // 2, (h % 2) * 64
                            for kt in range(2):
                                mm(sc1[:, (hq * 2 + kt) * 128:(hq * 2 + kt + 1) * 128],
                                   kmT[:, pr, h % 2, kt * 128:(kt + 1) * 128], mqT[:, pr, :])
                        act(pT[:, 512:1024], sc1[:, :], AF.Exp, scale=0.125)
                        for hq in range(2):
                            h = 2 * hp + hq
                            for kt in range(2):
                                mm(pb[:, h * 65:(h + 1) * 65], pT[:, 512 + (hq * 2 + kt) * 128:512 + (hq * 2 + kt + 1) * 128],
                                   Vmaug[:, kt, h, 0:65], start=(kt == 0), stop=(kt == 1))
                    recip(rs4[:, :], pb[:, 0:260].r("p (a b) -> p a b", a=4)[:, :, 64])
                    tt(ymbs[i % 2][:, :].r("p (a b) -> p a b", a=4), pb[:, 0:260].r("p (a b) -> p a b", a=4)[:, :, 0:64],
                       rs4[:, :].bc(2, [128, 4, 64]), ALU.mult)
                    tt(ymbs[i % 2][:, :], ymbs[i % 2][:, :], Tb2[:, B_GM:B_GM + 256], ALU.mult, eng="pool")
                    LL = S.end_capture()
                    S.capture()
                    for g2 in range(2):
                        tr(p6b[:, 768 + g2 * 128:768 + (g2 + 1) * 128], Tb2[:, B_Q + g2 * 128:B_Q + (g2 + 1) * 128], identb)
                    for h in range(4):
                        ts(qm[:, h, :], p6b[:, 768 + (h % 2) * 128:768 + (h % 2 + 1) * 128],
                           mBD[:, 64 * (h // 2):64 * (h // 2) + 1], None, ALU.mult)
                    for kt in range(nkt):
                        mm(sc0[:, :], KT[:, kt * 128:(kt + 1) * 128], qm[:, :, :].r("p a b -> p (a b)"))
                        act(pT[:, 0:512], sc0[:, :], AF.Exp, scale=0.125)
                        for h in range(4):
                            mm(p6[:, h * 65:(h + 1) * 65], pT[:, h * 128:(h + 1) * 128], Vaug[:, kt, h // 2, 0:65],
                               start=(kt == 0 and h == 0), stop=(kt == nkt - 1), skip=True)
                    recip(rs4a[:, :], p6[:, 0:260].r("p (a b) -> p a b", a=4)[:, :, 64])
                    tt(yaa[:, :].r("p (a b) -> p a b", a=4), p6[:, 0:260].r("p (a b) -> p a b", a=4)[:, :, 0:64],
                       rs4a[:, :].bc(2, [128, 4, 64]), ALU.mult)
                    tt(ycat[:, 0:256], yaa[:, :], Tb2[:, B_GA:B_GA + 256], ALU.mult, eng="pool")
                    LA = S.end_capture()
                    S.merge([LG, LL, LA])
                    prev = i
                tail(prev)
        if _os.environ.get("KDESC"):
            lo, hi = [int(v) for v in _os.environ["KDESC"].split(",")]
            for dd in S.desc:
                if lo <= dd[0] <= hi:
                    print("OP", dd)
        if _os.environ.get("KLIMIT"):
            print("TOTAL OPS", S.total)
        S.emit(st)
    return nc


def _consts():
    c = np.zeros((128, NCST), np.float32)
    p = np.arange(128)[:, None]
    f = np.arange(128)[None, :]
    bd = (p // 64 == f // 64)
    c[:, C_ID:C_ID + 128] = (p == f)
    c[:, C_LE:C_LE + 128] = bd & (p <= f)
    c[:, C_LT:C_LT + 128] = bd & (p < f)
    c[:, C_GE:C_GE + 128] = bd & (p >= f)
    c[:, C_GT:C_GT + 128] = bd & (p > f)
    c[:, C_BD:C_BD + 128] = bd
    c[:, C_SEL0:C_SEL0 + 128] = (p < 64) & (f >= 0)
    c[:, C_SEL1:C_SEL1 + 128] = (p >= 64) & (f >= 0)
    c[:, C_RM4:C_RM4 + 4] = (p // 32 == np.arange(4)[None, :])
    c[:, C_BD4:C_BD4 + 256] = (p // 32 == np.arange(256)[None, :] // 64)
    c[:, C_ONE] = 1.0
    cb = np.zeros((128, 640), np.float32)
    cb[:, 0:128] = (p == f)
    cb[:, 128:256] = (p == f - 1)
    cb[:, 256:384] = (p == f + 1)
    cb[:, 384:512] = (p == 127) & (f == 0)
    cb[:, 512:640] = (p == 0) & (f == 127)
    return c, cb


def _rope_tables(smax):
    t = np.arange(smax)
    pos = np.stack([t // 64, t % 64], -1).astype(np.float32)
    inv = (10000.0 ** (-np.arange(0, 32, 2, dtype=np.float32) / 32)).astype(np.float32)
    ang = pos[:, :, None] * inv[None, None, :]
    cos, sin = np.cos(ang).astype(np.float32), np.sin(ang).astype(np.float32)
    C = np.zeros((smax, 2, 2, 16), np.float32)
    Sg = np.zeros((smax, 2, 2, 16), np.float32)
    C[:, :, 0, :] = cos
    C[:, :, 1, :] = cos
    Sg[:, :, 0, :] = -sin
    Sg[:, :, 1, :] = sin
    C = np.tile(C.reshape(smax, 1, 64), (1, 6, 1)).reshape(smax, 384)
    Sg = np.tile(Sg.reshape(smax, 1, 64), (1, 6, 1)).reshape(smax, 384)
    return np.ascontiguousarray(C), np.ascontiguousarray(Sg)


def _col_perm():
    o = dict(a_q=0, a_k=256, a_v=384, a_gate=512, d_qkv=768, d_beta=1536, d_alpha=1544, d_gate=1552,
             l_q=1808, l_k=1936, l_v=2064, l_low=2320, l_gate=2352, m_q=2608, m_gate=2864)
    r = lambda a, n: list(range(a, a + n))
    aq = []
    for g in range(2):
        for kv in range(2):
            h = kv * 2 + g
            aq += r(o["a_q"] + h * 64, 64)
    perm = (aq + r(o["a_k"], 128) + r(o["a_v"], 128)
            + r(o["a_gate"], 256) + r(o["m_gate"], 256)
            + r(o["d_gate"], 256) + r(o["l_gate"], 256)
            + r(o["d_qkv"], 512)
            + r(o["d_qkv"] + 512, 256) + r(o["d_beta"], 8) + r(o["d_alpha"], 8)
            + r(o["l_q"], 128) + r(o["l_k"], 128) + r(o["l_v"], 256)
            + r(o["m_q"], 256) + r(o["l_low"], 32))
    assert len(perm) == NCOL and len(set(perm)) == NCOL
    return np.array(perm)


def _prep_shared(inp, depth):
    f = lambda a: np.ascontiguousarray(np.asarray(a, dtype=np.float32))
    perm = _col_perm()
    w_in = f(np.asarray(inp["w_in"])[:depth][:, :, perm])
    pvec = np.zeros((depth, NPV), np.float32)
    wup = np.zeros((depth, 32, 256), np.float32)
    for l in range(depth):
        pvec[l, P_GQK:P_GQK + 256] = np.tile(np.asarray(inp["att_q_norm_g"])[l], 4)
        pvec[l, P_GQK + 256:P_GQK + 384] = np.tile(np.asarray(inp["att_k_norm_g"])[l], 2)
        pvec[l, P_ALOG:P_ALOG + 8] = np.asarray(inp["gdn_a_log"])[l].reshape(-1)
        pvec[l, P_DTB:P_DTB + 8] = np.asarray(inp["gdn_dt_bias"])[l].reshape(-1)
        pvec[l, P_GOG:P_GOG + 256] = np.tile(np.asarray(inp["gdn_out_norm_g"])[l], 4)
        pvec[l, P_GLB:P_GLB + 256] = np.asarray(inp["gla_b_gate"])[l].reshape(-1)
        pvec[l, P_GLG:P_GLG + 256] = np.tile(np.asarray(inp["gla_out_norm_g"])[l], 4)
        for d in range(2):
            wup[l, d * 16:(d + 1) * 16, d * 128:(d + 1) * 128] = np.asarray(inp["gla_w_gate_up"])[l, d]
    return dict(w_in=w_in, w_out=f(np.asarray(inp["w_out"])[:depth]), w_mem=f(np.asarray(inp["w_mem_kv"])[:depth]),
                norm_g=f(np.asarray(inp["norm_g"])[:depth]), mem_g=f(np.asarray(inp["mem_norm_g"])[:depth]),
                pvec=pvec, wup=wup, fng=f(inp["final_norm_g"]), cst=_consts()[0], cstb=_consts()[1],
                convw=f(np.asarray(inp["gdn_conv_w"])[:depth].reshape(depth, -1)))


def kernel(**inputs):
    depth = 4
    xp = np.asarray(inputs["x_prompt"], np.float32)
    xs = np.asarray(inputs["x_sample"], np.float32)
    mp = np.asarray(inputs["mem_prompt"], np.float32)
    ms = np.asarray(inputs["mem_sample"], np.float32)
    Sp, Ss = xp.shape[1], xs.shape[1]
    seqs = [Sp, Ss, Ss]
    shared = _prep_shared(inputs, depth)
    C, Sg = _rope_tables(max(seqs))
    shared["ropec"], shared["ropes"] = C, Sg
    nc = _build(seqs, depth)
    in_maps = []
    for c in range(8):
        m = dict(shared)
        m["x"] = np.ascontiguousarray(np.concatenate([xp[c], xs[2 * c], xs[2 * c + 1]], 0))
        m["mem"] = np.ascontiguousarray(np.concatenate([mp[c], ms[2 * c], ms[2 * c + 1]], 0))
        in_maps.append(m)
    res = run_bass_kernel_spmd(nc, in_maps, core_ids=list(range(8)))
    yp = np.stack([res.results[c]["y"][:Sp] for c in range(8)], 0)
    ysm = np.stack([res.results[c // 2]["y"][Sp + (c % 2) * Ss:Sp + (c % 2 + 1) * Ss] for c in range(16)], 0)
    return (yp.astype(np.float32), ysm.astype(np.float32))
```

```python
from contextlib import ExitStack
import numpy as np
import concourse.bass as bass
import concourse.mybir as mybir
from concourse.bass_utils import run_bass_kernel_spmd

F32 = mybir.dt.float32
BF16 = mybir.dt.bfloat16
AF = mybir.ActivationFunctionType
ALU = mybir.AluOpType
AX = mybir.AxisListType

D = 1024
NCOL = 3120
EPS = 1e-6
NB = 2560
NF = 904
B_GA, B_GM, B_GD, B_GL, B_MQ, B_GQ, B_GK, B_GV, B_LV, B_Q = 0, 256, 512, 768, 1024, 1280, 1536, 1792, 2048, 2304
F_BG, F_LQ, F_LK, F_NGK, F_OB, F_OC = 0, 8, 136, 264, 392, 648
P_GQK, P_ALOG, P_DTB, P_GOG, P_GLG, P_GLB = 0, 384, 392, 400, 656, 912
NPV = 1168
C_ID, C_LE, C_LT, C_GE, C_GT, C_BD, C_SEL0, C_SEL1 = [128 * k for k in range(8)]
C_RM4 = 128 * 8
C_BD4 = C_RM4 + 4
C_ONE = C_BD4 + 256
NCST = C_ONE + 1


class _Op:
    __slots__ = ("eng", "idx", "fn", "waits", "signal", "kind", "dma_m", "tok")

    def __init__(self, eng, idx, fn, kind):
        self.eng = eng
        self.idx = idx
        self.fn = fn
        self.kind = kind
        self.waits = []
        self.signal = False
        self.dma_m = None
        self.tok = None


class Sched:
    ENGS = ("pe", "act", "dve", "pool", "sp")

    def __init__(self, nc, R=8, epoch=16000):
        self.nc = nc
        self.R = R
        self.epoch = epoch
        self.streams = {e: [] for e in self.ENGS}
        self.ndma = {e: 0 for e in self.ENGS}
        self.dma_ops = {e: [] for e in self.ENGS}
        self.last_w = {}
        self.readers = {}
        self.waited = {e: {p: -1 for p in self.ENGS} for e in self.ENGS}
        self.waited_dma = {e: {} for e in self.ENGS}

    def capture(self):
        self._cap = []

    def end_capture(self):
        lst = self._cap
        self._cap = None
        return lst

    def merge(self, lists):
        pos = [0] * len(lists)
        while True:
            best, bf = -1, 2.0
            for j, l in enumerate(lists):
                if pos[j] < len(l):
                    f = pos[j] / len(l)
                    if f < bf:
                        best, bf = j, f
            if best < 0:
                break
            self._add(*lists[best][pos[best]])
            pos[best] += 1

    def _add(self, eng, fn, reads, writes, kind):
        if getattr(self, "_cap", None) is not None:
            self._cap.append((eng, fn, list(reads), list(writes), kind))
            return None
        self.total = getattr(self, "total", 0) + 1
        if getattr(self, "desc", None) is not None:
            import sys as _sys
            fr = _sys._getframe(3)
            self.desc.append((self.total, eng, kind, fr.f_lineno, fr.f_back.f_lineno))
        if self.total > getattr(self, "limit", 1 << 60) or self.total in getattr(self, "skip", ()):
            return None
        excl = getattr(self, "excl", ())
        if excl and eng != "pe":
            extra = [k for k in reads if k in excl and k not in writes]
            if extra:
                writes = list(writes) + extra
        st = self.streams[eng]
        op = _Op(eng, len(st), fn, kind)
        st.append(op)
        deps = []
        for k in reads:
            w = self.last_w.get(k)
            if w is not None:
                deps.append((w, True))
        for k in writes:
            w = self.last_w.get(k)
            if w is not None:
                deps.append((w, False))
            for r in self.readers.get(k, {}).values():
                deps.append((r, False))
        for (p, raw) in deps:
            if p is op:
                continue
            if p.kind == "c":
                if p.eng == eng:
                    if eng == "pe":
                        continue
                if self.waited[eng][p.eng] >= p.idx:
                    continue
                self.waited[eng][p.eng] = p.idx
                p.signal = True
                op.waits.append(p)
            else:
                key = (p.eng, p.dma_m % self.R)
                if self.waited_dma[eng].get(key, -1) >= p.dma_m:
                    continue
                self.waited_dma[eng][key] = p.dma_m
                op.waits.append(p)
        if kind == "d":
            m = self.ndma[eng]
            self.ndma[eng] = m + 1
            op.dma_m = m
            self.dma_ops[eng].append(op)
            if m >= self.R:
                prev = self.dma_ops[eng][m - self.R]
                key = (eng, m % self.R)
                if self.waited_dma[eng].get(key, -1) < prev.dma_m:
                    self.waited_dma[eng][key] = prev.dma_m
                    op.waits.append(prev)
        for k in writes:
            self.last_w[k] = op
            self.readers[k] = {}
        for k in reads:
            self.readers.setdefault(k, {})[eng] = op
        return op

    def op(self, eng, fn, reads=(), writes=()):
        return self._add(eng, fn, reads, writes, "c")

    def dma(self, eng, fn, reads=(), writes=()):
        return self._add(eng, fn, reads, writes, "d")

    def emit(self, stack):
        nc = self.nc
        for e in self.ENGS:
            n = self.ndma[e]
            if n:
                last = self.dma_ops[e][max(0, n - self.R):]
                f = _Op(e, len(self.streams[e]), None, "f")
                f.waits = list(last)
                self.streams[e].append(f)
        csem = {}
        for e in self.ENGS:
            cnt = 0
            for op in self.streams[e]:
                if op.kind == "c" and op.signal:
                    ep = cnt // self.epoch
                    if (e, ep) not in csem:
                        csem[(e, ep)] = stack.enter_context(nc.semaphore(f"c_{e}_{ep}"))
                    op.tok = (csem[(e, ep)], cnt % self.epoch + 1)
                    cnt += 1
        dsem = {}
        for e in self.ENGS:
            for op in self.dma_ops[e]:
                s = op.dma_m % self.R
                if (e, s) not in dsem:
                    dsem[(e, s)] = stack.enter_context(nc.semaphore(f"d_{e}_{s}"))
                op.tok = (dsem[(e, s)], 16 * (op.dma_m // self.R + 1))
        block = stack.enter_context(nc.Block())

        def replay(e, eng):
            for op in self.streams[e]:
                for p in op.waits:
                    eng.wait_ge(p.tok[0], p.tok[1])
                if op.fn is None:
                    continue
                ins = op.fn(eng)
                if op.kind == "d":
                    ins.then_inc(op.tok[0], 16)
                elif op.signal:
                    ins.then_inc(op.tok[0], 1)

        if self.streams["pe"]:
            @block.tensor
            def _(eng):
                replay("pe", eng)
        if self.streams["act"]:
            @block.scalar
            def _(eng):
                replay("act", eng)
        if self.streams["dve"]:
            @block.vector
            def _(eng):
                replay("dve", eng)
        if self.streams["pool"]:
            @block.gpsimd
            def _(eng):
                replay("pool", eng)
        if self.streams["sp"]:
            @block.sync
            def _(eng):
                replay("sp", eng)


class V:
    __slots__ = ("ap", "key")

    def __init__(self, ap, key):
        self.ap = ap
        self.key = key

    def __getitem__(self, idx):
        return V(self.ap[idx], self.key)

    def r(self, pat, **kw):
        return V(self.ap.rearrange(pat, **kw), self.key)

    def bc(self, axis, shape):
        return V(self.ap.unsqueeze(axis).to_broadcast(list(shape)), self.key)


class TT:
    def __init__(self, t, key):
        self.t = t
        self.key = key

    def __getitem__(self, idx):
        return V(self.t[idx], self.key)

    def k(self, sub):
        return TT(self.t, (self.key, sub))


def _build(seqs, depth, final=True):
    nc = bass.Bass("TRN2", target_bir_lowering=False)
    NT = sum(seqs)
    NS = len(seqs)
    SMAX = max(seqs)
    roff = [sum(seqs[:i]) for i in range(NS)]

    def din(name, shape, dt=F32):
        return nc.dram_tensor(name, list(shape), dt, kind="ExternalInput").ap()

    x_d = TT(din("x", [NT, D]), "x")
    mem_d = TT(din("mem", [NS * 256, D]), "mem")
    win_d = din("w_in", [depth, D, NCOL])
    wout_d = din("w_out", [depth, D, D])
    wmem_d = din("w_mem", [depth, D, 512])
    ng_d = din("norm_g", [depth, D])
    mg_d = din("mem_g", [depth, D])
    pv_d = din("pvec", [depth, NPV])
    wup_d = din("wup", [depth, 32, 256])
    fng_d = din("fng", [D])
    cst_d = din("cst", [128, NCST])
    cstb_d = din("cstb", [128, 640])
    cw_d = din("convw", [depth, 2304])
    rc_d = TT(din("ropec", [SMAX, 384]), "ropec")
    rs_d = TT(din("ropes", [SMAX, 384]), "ropes")
    y_d = TT(nc.dram_tensor("y", [NT, D], F32, kind="ExternalOutput").ap(), "y")
    hbuf = TT(nc.dram_tensor("hbuf", [NT, D], F32, kind="Internal").ap(), "hbuf")
    stb = TT(nc.dram_tensor("stashb", [NT, NB], BF16, kind="Internal").ap(), "stb")
    stf = TT(nc.dram_tensor("stashf", [NT, NF], F32, kind="Internal").ap(), "stf")

    with ExitStack() as st:
        def sb(name, shape, dt=F32):
            return TT(st.enter_context(nc.sbuf_tensor("s_" + name, list(shape), dt)), name)

        def ps(name, shape, dt=F32):
            return TT(st.enter_context(nc.psum_tensor("ps_" + name, list(shape), dt)), name)

        S = Sched(nc)
        import os as _os
        if _os.environ.get("KLIMIT"):
            S.limit = int(_os.environ["KLIMIT"])
        if _os.environ.get("KDESC"):
            S.desc = []
        if _os.environ.get("KSKIP"):
            S.skip = set(int(v) for v in _os.environ["KSKIP"].split(","))

        def keys(*vs):
            return [v.key for v in vs if isinstance(v, V)]

        def apof(v):
            return v.ap if isinstance(v, V) else v

        def act(out, in_, func, scale=1.0, bias=0.0, accum=None):
            kw = dict(out=out.ap, in_=in_.ap, func=func, scale=apof(scale), bias=apof(bias))
            w = [out.key]
            if accum is not None:
                kw["accum_out"] = accum.ap
                w.append(accum.key)
            S.op("act", lambda e, kw=kw: e.activation(**kw), reads=keys(in_, scale, bias), writes=w)

        def ts(out, in0, s1, s2=None, op0=ALU.mult, op1=None, eng="dve"):
            kw = dict(out=out.ap, in0=in0.ap, scalar1=apof(s1), scalar2=apof(s2), op0=op0)
            if op1 is not None:
                kw["op1"] = op1
            S.op(eng, lambda e, kw=kw: e.tensor_scalar(**kw), reads=keys(in0, s1, s2), writes=[out.key])

        def tt(out, in0, in1, op=ALU.mult, eng="dve"):
            S.op(eng, lambda e: e.tensor_tensor(out=out.ap, in0=in0.ap, in1=in1.ap, op=op),
                 reads=keys(in0, in1), writes=[out.key])

        def stt(out, in0, scalar, in1, op0, op1, eng="dve"):
            S.op(eng, lambda e: e.scalar_tensor_tensor(out=out.ap, in0=in0.ap, scalar=apof(scalar), in1=in1.ap,
                                                       op0=op0, op1=op1),
                 reads=keys(in0, scalar, in1), writes=[out.key])

        def cp(out, in_, eng="dve"):
            if eng == "act":
                S.op("act", lambda e: e.copy(out=out.ap, in_=in_.ap), reads=[in_.key], writes=[out.key])
            else:
                S.op(eng, lambda e: e.tensor_copy(out=out.ap, in_=in_.ap), reads=[in_.key], writes=[out.key])

        def red(out, in_, op=ALU.add):
            S.op("dve", lambda e: e.tensor_reduce(out=out.ap, in_=in_.ap, axis=AX.X, op=op),
                 reads=[in_.key], writes=[out.key])

        def recip(out, in_):
            S.op("dve", lambda e: e.reciprocal(out=out.ap, in_=in_.ap), reads=[in_.key], writes=[out.key])

        def memset(out, val, eng="pool"):
            S.op(eng, lambda e: e.memset(out.ap, val), writes=[out.key])

        def mm(out, lhsT, rhs, start=True, stop=True, skip=False):
            S.op("pe", lambda e: e.matmul(out.ap, lhsT=lhsT.ap, rhs=rhs.ap, start=start, stop=stop,
                                          skip_group_check=skip),
                 reads=keys(lhsT, rhs), writes=[out.key])

        def tr(out, in_, ident):
            S.op("pe", lambda e: e.transpose(out=out.ap, in_=in_.ap, identity=ident.ap),
                 reads=keys(in_, ident), writes=[out.key])

        def dma(eng, out, in_, slow=False):
            if slow:
                S.dma(eng, lambda e: e.dma_start(out=out.ap, in_=in_.ap, allow_slow_non_contiguous=True),
                      reads=[in_.key], writes=[out.key])
            else:
                S.dma(eng, lambda e: e.dma_start(out=out.ap, in_=in_.ap), reads=[in_.key], writes=[out.key])

        def rsqrt_(out, in_, scale, eps):
            act(out, in_, AF.Ln, scale=scale, bias=eps)
            act(out, out, AF.Exp, scale=-0.5)

        cst = sb("cst", [128, NCST])
        cstb = sb("cstb", [128, 5 * 128], BF16)
        win = sb("win", [128, 8, NCOL], BF16)
        wout = sb("wout", [128, 8, D], BF16)
        wmem = sb("wmem", [128, 8, 512], BF16)
        ngt = sb("ngt", [128, 8])
        mgt = sb("mgt", [128, 8])
        pv = sb("pv", [128, NPV])
        wup = sb("wup", [32, 256])
        negA = sb("negA", [128, 8])
        KT = sb("KT", [128, SMAX], BF16)
        qm = sb("qm", [128, 4, 128], BF16)
        Vaug = sb("Vaug", [128, SMAX // 128, 2, 65], BF16)
        kmT = sb("kmT", [128, 2, 2, 256], BF16)
        Vmaug = sb("Vmaug", [128, 2, 4, 65], BF16)
        hin = sb("hin", [128, D])
        ssq = sb("ssq", [128, 1])
        rstd = sb("rstd", [128, 1])
        xn = sb("xn", [128, D], BF16)
        xnT = sb("xnT", [128, D], BF16)
        rc = sb("rc", [128, 384])
        rs = sb("rs", [128, 384])
        ss8 = sb("ss8", [128, 8])
        rn8 = sb("rn8", [128, 8])
        qkrot = sb("qkrot", [128, 384], BF16)
        we = sb("we", [128, 768])
        xw = [sb(f"xw{k}", [128, 3, 768], BF16) for k in range(3)]
        lg2 = sb("lg2", [128, 16])
        bgf = [sb(f"bgf{k}", [128, 8]) for k in range(2)]
        ngkf = [sb(f"ngkf{k}", [128, 128]) for k in range(2)]
        low = sb("low", [128, 32])
        lowT = sb("lowT", [32, 128])
        gl = sb("gl", [128, 256])
        Tb = [sb(f"Tb{k}", [128, NB], BF16) for k in range(2)]
        Tf = [sb(f"Tf{k}", [128, NF]) for k in range(2)]
        wqB = sb("wqB", [128, 384])
        ss8b = sb("ss8b", [128, 8])
        rn8b = sb("rn8b", [128, 8])
        convw = sb("convw", [128, 2304], BF16)
        lgr = [sb(f"lgr{k}", [128, 16]) for k in range(2)]
        cv = sb("cv", [128, 768])
        gDN = sb("gDN", [128, 4, 128])
        gDA = sb("gDA", [128, 4, 128])
        gsm = sb("gsm", [128, 24])
        gcd = sb("gcd", [128, 2, 4])
        gN1 = sb("gN", [128, 4, 128], BF16)
        gG = sb("gG", [128, 4, 128])
        gA1 = sb("gA", [128, 4, 128], BF16)
        gP1 = sb("gP", [128, 4, 128], BF16)
        gN = [gN1, gN1]
        gA = [gA1, gA1]
        gP = [gP1, gP1]
        gkT = sb("gkT", [128, 2, 128], BF16)
        gkTm = sb("gkTm", [128, 2, 2, 128], BF16)
        gkdm = sb("gkdm", [128, 2, 256], BF16)
        gsm2 = sb("gsm2", [128, 8])
        lkdm = sb("lkdm", [128, 2, 128], BF16)
        gqT = sb("gqT", [128, 2, 128], BF16)
        gvk = sb("gvk", [128, 4, 128], BF16)
        guw = sb("guw", [128, 4, 128])
        gwb = sb("gwb", [128, 256], BF16)
        gwT = sb("gwT", [128, 2, 128], BF16)
        gqd = sb("gqd", [128, 256], BF16)
        gqdT = sb("gqdT", [128, 2, 128], BF16)
        gkd = sb("gkd", [128, 256], BF16)
        gqkm = sb("gqkm", [128, 4, 128], BF16)
        gvn = sb("gvn", [128, 256], BF16)
        gS = [[sb(f"gS{d}{p}", [128, 128]) for p in range(2)] for d in range(2)]
        gSb = [[sb(f"gSb{d}{p}", [128, 128], BF16) for p in range(2)] for d in range(2)]
        obcs = [sb(f"obc{k}", [128, 512]) for k in range(2)]
        ymbs = [sb(f"ymb{k}", [128, 256], BF16) for k in range(2)]
        ltmp = sb("ltmp", [128, 256])
        lex = sb("lex", [128, 3, 128])
        lcd = sb("lcd", [128, 2])
        lqe = sb("lqe", [128, 128], BF16)
        lke = sb("lke", [128, 128], BF16)
        lkd = sb("lkd", [128, 128], BF16)
        lqeT = sb("lqeT", [128, 128], BF16)
        lkeT = sb("lkeT", [128, 128], BF16)
        lqblk = sb("lqblk", [128, 4, 128], BF16)
        latt = sb("latt", [128, 4, 128], BF16)
        lS = [sb(f"lS{d}", [128, 256]) for d in range(2)]
        lSb = [sb(f"lSb{d}", [128, 256], BF16) for d in range(2)]
        os2 = sb("os2", [128, 512])
        osum = TT(os2.t[:, 0:256], "os2")
        sq = we
        ycat = xn
        ycatT = xnT
        junk = xn
        xc = cv
        wst = hin
        hout = hin
        pT = sb("pT", [128, 1024], BF16)
        wA = TT(pT.t[:, :].bitcast(F32), "pT")
        mqT = sb("mqT", [128, 2, 128], BF16)
        rs4 = sb("rs4", [128, 4])
        ya = osum
        kmtok = TT(qkrot.t[:, 0:256], "qkrot")

        ptr = ps("ptr", [128, 1024], BF16)
        pa = ps("pa", [128, 512])
        pb = ps("pb", [128, 512])
        p2 = ps("p2", [128, 512])
        p3 = ps("p3", [128, 512])
        sc = ps("sc", [128, 1024])
        p6 = ps("p6", [128, 512])
        sc0 = TT(sc.t[:, 0:512], "sc0")
        sc1 = TT(sc.t[:, 512:1024], "sc1")
        S.excl = {"ptr", "pa", "pb", "p2", "p3", "sc0", "sc1", "p6"}
        p6b = TT(p6.t[:, :].bitcast(BF16), "p6")
        pbb = TT(pb.t[:, :].bitcast(BF16), "pb")
        rs4a = sb("rs4a", [128, 4])
        yaa = sb("yaa", [128, 256])

        ident = cst[:, C_ID:C_ID + 128]
        identb = cstb[:, 0:128]
        SHPb, SHNb, HPb, HNb = [cstb[:, 128 * k:128 * (k + 1)] for k in range(1, 5)]
        mLE, mLT, mGE, mGT, mBD = [cst[:, c:c + 128] for c in (C_LE, C_LT, C_GE, C_GT, C_BD)]
        SEL = [cst[:, C_SEL0:C_SEL0 + 128], cst[:, C_SEL1:C_SEL1 + 128]]
        RM4 = cst[:, C_RM4:C_RM4 + 4]
        BD4 = cst[:, C_BD4:C_BD4 + 256]
        ONE = cst[:, C_ONE:C_ONE + 1]

        dma("sp", cst[:, :], V(cst_d, "cst_d"))
        dma("sp", hin[:, 0:640], V(cstb_d, "cstb_d"))
        cp(cstb[:, :], hin[:, 0:640], eng="pool")
        memset(Vaug[:, :, :, :], 1.0)
        memset(Vmaug[:, :, :, :], 1.0)
        memset(gvn[:, :], 0.0)

        def rms_to_xnT(src, pt=None):
            pt = ptr if pt is None else pt
            act(junk[:, :], src, AF.Square, accum=ssq[:, :])
            rsqrt_(rstd[:, :], ssq[:, :], 1.0 / D, EPS)
            ts(xn[:, :], src, rstd[:, :], None, ALU.mult)
            for c in range(8):
                tr(pt[:, c * 128:(c + 1) * 128], xn[:, c * 128:(c + 1) * 128], identb)
            cp(xnT[:, :], pt[:, :], eng="act")

        def sigmoid_from(dst_e, src, n):
            act(dst_e, src, AF.Exp, scale=-1.0)
            ts(dst_e, dst_e, 1.0, None, ALU.add)
            recip(dst_e, dst_e)

        def gdn_core(d, qn, kn, vv, beta, g, o_out, first):
            M1, GS, NSm, AI = (mLE, mGT, mGT, mLE) if d == 0 else (mGE, mLT, mLT, mGE)
            if first:
                for p in range(2):
                    memset(gS[d][p][:, :], 0.0)
                    memset(gSb[d][p][:, :], 0.0)
            for p in range(2):
                tr(ptr[:, p * 128:(p + 1) * 128], kn[:, p * 128:(p + 1) * 128], identb)
                tr(ptr[:, 256 + p * 128:256 + (p + 1) * 128], qn[:, p * 128:(p + 1) * 128], identb)
            cp(gkT[:, :, :], ptr[:, 0:256].r("p (a b) -> p a b", a=2), eng="act")
            for q in range(2):
                ts(gkTm[:, :, q, :], ptr[:, 0:256].r("p (a b) -> p a b", a=2), mBD[:, 64 * q:64 * q + 1], None, ALU.mult)
            cp(gqT[:, :, :], ptr[:, 256:512].r("p (a b) -> p a b", a=2), eng="act")
            tt(gG[:, :, :], GS.bc(1, [128, 4, 128]), g.bc(2, [128, 4, 128]), ALU.mult)
            mm(p2[:, :], M1, gG[:, :, :].r("p a b -> p (a b)"))
            for h in range(4):
                mm(p3[:, h * 128:(h + 1) * 128], gG[:, h, :], M1)
            act(gDN[:, :, :].r("p a b -> p (a b)"), p2[:, :], AF.Exp)
            act(gDA[:, :, :].r("p a b -> p (a b)"), p3[:, :], AF.Exp)
            tt(gDN[:, :, :], gDN[:, :, :], NSm.bc(1, [128, 4, 128]), ALU.mult)
            tt(gDA[:, :, :], gDA[:, :, :], AI.bc(1, [128, 4, 128]), ALU.mult, eng="pool")
            mm(pa[:, 0:4], M1, g)
            mm(pa[:, 4:8], mBD, g)
            mm(pa[:, 8:12], SEL[0], g)
            mm(pa[:, 12:16], SEL[1], g)
            cp(gsm[:, 0:4], pa[:, 0:4])
            act(gsm[:, 4:8], pa[:, 0:4], AF.Exp)
            tt(gsm[:, 8:12], pa[:, 4:8], gsm[:, 0:4], ALU.subtract)
            act(gsm[:, 8:12], gsm[:, 8:12], AF.Exp)
            act(gcd[:, :, :].r("p a b -> p (a b)"), pa[:, 8:16], AF.Exp)
            ts(gsm[:, 12:16], beta, -1.0, None, ALU.mult)
            tt(gsm[:, 16:20], beta, gsm[:, 4:8], ALU.mult)
            for h in range(4):
                pr, hb = h // 2, (h % 2) * 64
                mm(p2[:, h * 128:(h + 1) * 128], gkT[:, pr, :], gkTm[:, pr, h % 2, :])
            for h in range(4):
                stt(gN[0][:, h, :], p2[:, h * 128:(h + 1) * 128], gsm[:, 12 + h:13 + h], gDN[:, h, :],
                    ALU.mult, ALU.mult)
            for h in range(4):
                tr(ptr[:, h * 128:(h + 1) * 128], gN[0][:, h, :], identb)
            cp(gA[0][:, :, :].r("p a b -> p (a b)"), ptr[:, 0:512], eng="act")
            tt(gP[0][:, :, :], gA[0][:, :, :], ident.bc(1, [128, 4, 128]), ALU.add)
            cur = 0
            pc = 0
            for lvl in range(5):
                nxt = 1 - cur
                last = lvl == 4
                for h in range(4):
                    mm(p2[:, h * 128:(h + 1) * 128], gA[cur][:, h, :], gN[cur][:, h, :])
                if not last:
                    for h in range(4):
                        mm(p3[:, h * 128:(h + 1) * 128], gN[cur][:, h, :], gA[cur][:, h, :])
                cp(gN[nxt][:, :, :].r("p a b -> p (a b)"), p2[:, :], eng="dve")
                if not last:
                    cp(gA[nxt][:, :, :].r("p a b -> p (a b)"), p3[:, :], eng="act")
                for h in range(4):
                    mm(pa[:, h * 128:(h + 1) * 128], gN[nxt][:, h, :], gP[pc][:, h, :])
                tt(gP[1 - pc][:, :, :].r("p a b -> p (a b)"), pa[:, :], gP[pc][:, :, :].r("p a b -> p (a b)"),
                   ALU.add)
                pc = 1 - pc
                cur = nxt
            Pm = gP[pc]
            tt(gvk[:, :, 0:64], vv.r("p (a b) -> p a b", a=4), beta.bc(2, [128, 4, 64]), ALU.mult, eng="pool")
            tt(gvk[:, :, 64:128], kn.r("p (a b) -> p a b", a=4), gsm[:, 16:20].bc(2, [128, 4, 64]), ALU.mult,
               eng="pool")
            for h in range(4):
                mm(p3[:, h * 128:(h + 1) * 128], Pm[:, h, :], gvk[:, h, :])
            cp(guw[:, :, :].r("p a b -> p (a b)"), p3[:, :], eng="act")
            cp(gwb[:, :].r("p (a b) -> p a b", a=4), guw[:, :, 64:128], eng="dve")
            for p in range(2):
                tr(ptr[:, 512 + p * 128:512 + (p + 1) * 128], gwb[:, p * 128:(p + 1) * 128], identb)
            cp(gwT[:, :, :], ptr[:, 512:768].r("p (a b) -> p a b", a=2), eng="act")
            tt(gqd[:, :].r("p (a b) -> p a b", a=4), qn.r("p (a b) -> p a b", a=4),
               gsm[:, 4:8].bc(2, [128, 4, 64]), ALU.mult, eng="pool")
            for p in range(2):
                tr(ptr[:, p * 128:(p + 1) * 128], gqd[:, p * 128:(p + 1) * 128], identb)
            cp(gqdT[:, :, :], ptr[:, 0:256].r("p (a b) -> p a b", a=2), eng="act")
            for c in range(2):
                ts(gsm2[:, 4 * c:4 * c + 4], gsm[:, 8:12], mBD[:, 64 * c:64 * c + 1], None, ALU.mult)
                tt(gkdm[:, c, :].r("p (a b) -> p a b", a=4), kn.r("p (a b) -> p a b", a=4),
                   gsm2[:, 4 * c:4 * c + 4].bc(2, [128, 4, 64]), ALU.mult, eng="pool")
            for h in range(4):
                pr, hb = h // 2, (h % 2) * 64
                mm(p2[:, h * 128:(h + 1) * 128], gkTm[:, pr, h % 2, :], gqT[:, pr, :])
            tt(gqkm[:, :, :].r("p a b -> p (a b)"), p2[:, :], gDA[:, :, :].r("p a b -> p (a b)"), ALU.mult)
            for c in ((0, 1) if d == 0 else (1, 0)):
                r0, r1 = 64 * c, 64 * c + 64
                for p in range(2):
                    mm(p3[:, p * 128:(p + 1) * 128], gwT[:, p, :], gSb[d][p][:, :])
                tt(gvn[r0:r1, :].r("p (a b) -> p a b", a=4), guw[r0:r1, :, 0:64],
                   p3[r0:r1, 0:256].r("p (a b) -> p a b", a=4), ALU.subtract)
                for h in range(4):
                    pr, hb = h // 2, (h % 2) * 64
                    mm(pa[:, h * 64:(h + 1) * 64], gqdT[:, pr, :], gSb[d][pr][:, hb:hb + 64], start=True, stop=False)
                    mm(pa[:, h * 64:(h + 1) * 64], gqkm[:, h, :], gvn[:, h * 64:(h + 1) * 64],
                       start=False, stop=True)
                cp(o_out[r0:r1, :], pa[r0:r1, 0:256], eng="act")
                for p in range(2):
                    mm(p2[:, p * 128:(p + 1) * 128], gkdm[:, c, p * 128:(p + 1) * 128],
                       gvn[:, p * 128:(p + 1) * 128])
                for p in range(2):
                    for q in range(2):
                        h = 2 * p + q
                        b0, b1 = 64 * q, 64 * q + 64
                        stt(gS[d][p][b0:b1, b0:b1], gS[d][p][b0:b1, b0:b1], gcd[b0:b1, c, h:h + 1],
                            p2[b0:b1, p * 128 + b0:p * 128 + b1], ALU.mult, ALU.add)
                    cp(gSb[d][p][:, :], gS[d][p][:, :], eng="dve")

        def gla_core(d, lq, lk, lv, ngk, o_out, first):
            M1, GS, AI = (mLE, mGT, mLE) if d == 0 else (mGE, mLT, mGE)
            if first:
                memset(lS[d][:, :], 0.0)
                memset(lSb[d][:, :], 0.0)
            mm(sc1[:, 0:128], M1, ngk)
            mm(sc1[:, 128:256], GS, ngk)
            for c in range(2):
                mm(sc1[:, 256 + c:257 + c], ngk, mBD[:, 64 * c:64 * c + 1])
            act(lex[:, 0, :], sc1[:, 0:128], AF.Exp, scale=-1.0 / 16)
            act(lex[:, 1, :], sc1[:, 0:128], AF.Exp, scale=1.0 / 16)
            act(lex[:, 2, :], sc1[:, 128:256], AF.Exp, scale=-1.0 / 16)
            act(lcd[:, :], sc1[:, 256:258], AF.Exp, scale=-1.0 / 16)
            tt(lqe[:, :], lq, lex[:, 0, :], ALU.mult)
            tt(lke[:, :], lk, lex[:, 1, :], ALU.mult, eng="pool")
            tt(lkd[:, :], lk, lex[:, 2, :], ALU.mult, eng="pool")
            for c in range(2):
                ts(lkdm[:, c, :], lkd[:, :], mBD[:, 64 * c:64 * c + 1], None, ALU.mult)
            tr(ptr[:, 768:896], lqe[:, :], identb)
            tr(ptr[:, 896:1024], lke[:, :], identb)
            cp(lqeT[:, :], ptr[:, 768:896], eng="act")
            cp(lkeT[:, :], ptr[:, 896:1024], eng="act")
            tt(lqblk[:, :, :], ptr[:, 768:896].bc(1, [128, 4, 128]), RM4.bc(2, [128, 4, 128]), ALU.mult)
            mm(pb[:, :], lkeT[:, :], lqblk[:, :, :].r("p a b -> p (a b)"))
            tt(latt[:, :, :], pb[:, :].r("p (a b) -> p a b", a=4), AI.bc(1, [128, 4, 128]), ALU.mult)
            for c in ((0, 1) if d == 0 else (1, 0)):
                r0, r1 = 64 * c, 64 * c + 64
                for h in range(4):
                    mm(sc1[:, h * 64:(h + 1) * 64], lqeT[:, :], lSb[d][:, h * 64:(h + 1) * 64], start=True, stop=False)
                    mm(sc1[:, h * 64:(h + 1) * 64], latt[:, h, :], lv[:, h * 64:(h + 1) * 64],
                       start=False, stop=True)
                cp(o_out[r0:r1, :], sc1[r0:r1, 0:256], eng="act")
                mm(sc1[:, 256:512], lkdm[:, c, :], lv)
                tt(ltmp[:, :], sc1[:, 256:512], BD4, ALU.mult)
                stt(lS[d][:, :], lS[d][:, :], lcd[:, c:c + 1], ltmp[:, :], ALU.mult, ALU.add)
                cp(lSb[d][:, :], lS[d][:, :], eng="dve")

        for l in range(depth):
            last_layer = (l == depth - 1)
            dma("sp", ngt[:, :], V(ng_d[l].rearrange("(c p) -> p c", p=128), "ng_d"), slow=True)
            dma("sp", mgt[:, :], V(mg_d[l].rearrange("(c p) -> p c", p=128), "mg_d"), slow=True)
            dma("sp", pv[:, :], V(pv_d[l].partition_broadcast(128), "pv_d"))
            dma("sp", wup[:, :], V(wup_d[l], "wup_d"))
            for c in range(8):
                for (j0, j1) in ((0, 1024), (1024, 2048), (2048, 3072), (3072, 3120)):
                    dma("sp", wst[:, 0:j1 - j0], V(win_d[l, c * 128:(c + 1) * 128, j0:j1], "win_d"))
                    ts(win[:, c, j0:j1], wst[:, 0:j1 - j0], ngt[:, c:c + 1], None, ALU.mult)
                dma("sp", wst[:, 0:1024], V(wout_d[l, c * 128:(c + 1) * 128, :], "wout_d"))
                cp(wout[:, c, :], wst[:, 0:1024], eng="pool")
                dma("sp", wst[:, 0:512], V(wmem_d[l, c * 128:(c + 1) * 128, :], "wmem_d"))
                ts(wmem[:, c, :], wst[:, 0:512], mgt[:, c:c + 1], None, ALU.mult)
            for j in range(3):
                dma("sp", wst[:, 0:768], V(cw_d[l, j * 768:(j + 1) * 768].partition_broadcast(128), "cw_d"))
                cp(convw[:, j * 768:(j + 1) * 768], wst[:, 0:768], eng="pool")
            act(negA[:, :], pv[:, P_ALOG:P_ALOG + 8], AF.Exp)
            ts(negA[:, :], negA[:, :], -1.0, None, ALU.mult)

            for s in range(NS):
                Sq = seqs[s]
                n = Sq // 128
                R0 = roff[s]
                src_h = x_d if l == 0 else hbuf

                for mt in range(2):
                    dma("sp", hin[:, :], mem_d[s * 256 + mt * 128:s * 256 + (mt + 1) * 128, :])
                    rms_to_xnT(hin[:, :])
                    for c in range(8):
                        mm(pa[:, :], xnT[:, c * 128:(c + 1) * 128], wmem[:, c, :], start=(c == 0), stop=(c == 7))
                    cp(kmtok[:, :], pa[:, 0:256], eng="act")
                    cp(Vmaug[:, mt, :, 0:64], pa[:, 256:512].r("p (a b) -> p a b", a=4), eng="dve")
                    for p in range(2):
                        tr(ptr[:, p * 128:(p + 1) * 128], kmtok[:, p * 128:(p + 1) * 128], identb)
                    for q in range(2):
                        ts(kmT[:, :, q, mt * 128:(mt + 1) * 128], ptr[:, 0:256].r("p (a b) -> p a b", a=2),
                           mBD[:, 64 * q:64 * q + 1], None, ALU.mult)

                def P_stage(i, phase):
                    par = i % 2
                    rows = slice(R0 + i * 128, R0 + (i + 1) * 128)
                    tb, tf = Tb[par], Tf[par]
                    if phase == 0:
                        dma("sp", hin[:, :], src_h.k(R0 // 128 + i)[rows, :])
                        dma("sp", rc[:, :], rc_d[i * 128:(i + 1) * 128, :])
                        dma("sp", rs[:, :], rs_d[i * 128:(i + 1) * 128, :])
                        rms_to_xnT(hin[:, :], p6b)
                    groups = [(0, 512), (512, 1024), (1024, 1536), (1536, 2048), (2048, 2320), (2320, 2832),
                              (2832, 3120)]
                    for gi, (c0, c1) in enumerate(groups):
                        if (gi in (3, 4)) != (phase == 0):
                            continue
                        pp = p6 if gi % 2 == 0 else sc0
                        w = c1 - c0
                        for c in range(8):
                            mm(pp[:, 0:w], xnT[:, c * 128:(c + 1) * 128], win[:, c, c0:c1], start=(c == 0),
                               stop=(c == 7))
                        if gi == 0:
                            act(wA[:, 0:384], pp[:, 0:384], AF.Square)
                            red(ss8b[:, 0:6], wA[:, 0:384].r("p (a b) -> p a b", a=6))
                            rsqrt_(rn8b[:, 0:6], ss8b[:, 0:6], 1.0 / 64, EPS)
                            tt(wqB[:, :].r("p (a b) -> p a b", a=6), pp[:, 0:384].r("p (a b) -> p a b", a=6),
                               rn8b[:, 0:6].bc(2, [128, 6, 64]), ALU.mult)
                            cp(Vaug[:, i, :, 0:64], pp[:, 384:512].r("p (a b) -> p a b", a=2), eng="dve")
                            tt(wqB[:, :], wqB[:, :], pv[:, P_GQK:P_GQK + 384], ALU.mult, eng="pool")
                            v5 = lambda t: t[:, 0:384].r("p (a b c) -> p a b c", a=12, b=2)
                            tt(v5(wA)[:, :, 0, :], v5(wqB)[:, :, 1, :], v5(rs)[:, :, 0, :], ALU.mult, eng="pool")
                            tt(v5(wA)[:, :, 1, :], v5(wqB)[:, :, 0, :], v5(rs)[:, :, 1, :], ALU.mult, eng="pool")
                            tt(wqB[:, :], wqB[:, :], rc[:, :], ALU.mult, eng="pool")
                            tt(tb[:, B_Q:B_Q + 256], wqB[:, 0:256], wA[:, 0:256], ALU.add)
                            tt(qkrot[:, 0:128], wqB[:, 256:384], wA[:, 256:384], ALU.add)
                            tr(p6b[:, 0:128], qkrot[:, 0:128], identb)
                            cp(KT[:, i * 128:(i + 1) * 128], p6b[:, 0:128], eng="act")
                        elif gi in (1, 2):
                            sigmoid_from(wA[:, 0:512], pp[:, 0:512], 512)
                            off = B_GA if gi == 1 else B_GD
                            tt(tb[:, off:off + 512], pp[:, 0:512], wA[:, 0:512], ALU.mult)
                        elif gi == 3:
                            for k3 in range(3):
                                tt(xw[i % 3][:, k3, 0:512], pp[:, 0:512], convw[:, 768 * k3:768 * k3 + 512], ALU.mult)
                        elif gi == 4:
                            cp(lgr[par][:, :], pp[:, 256:272], eng="dve")
                            for k3 in range(3):
                                tt(xw[i % 3][:, k3, 512:768], pp[:, 0:256], convw[:, 768 * k3 + 512:768 * (k3 + 1)],
                                   ALU.mult)
                        elif gi == 5:
                            act(tf[:, F_LQ:F_LQ + 128], pp[:, 0:128], AF.Copy, scale=32.0 ** -0.5)
                            cp(tf[:, F_LK:F_LK + 128], pp[:, 128:256], eng="act")
                            cp(tb[:, B_LV:B_LV + 256], pp[:, 256:512], eng="act")
                        else:
                            cp(tb[:, B_MQ:B_MQ + 256], pp[:, 0:256], eng="act")
                            cp(low[:, :], pp[:, 256:288], eng="dve")
                            tr(sc0[0:32, 0:128], low[:, :], ident)
                            cp(lowT[:, :], sc0[0:32, 0:128], eng="act")
                            mm(sc0[:, 128:384], lowT[:, :], wup[:, :])
                            tt(gl[:, :], sc0[:, 128:384], pv[:, P_GLB:P_GLB + 256], ALU.add)
                            act(gl[:, :], gl[:, :], AF.Exp, scale=-1.0)
                            act(ngkf[par][:, :], gl[:, 0:128], AF.Ln, bias=1.0)
                            act(tf[:, F_NGK:F_NGK + 128], gl[:, 128:256], AF.Ln, bias=1.0)

                    if phase == 1:
                        lg = lgr[par]
                        sigmoid_from(lg2[:, 0:8], lg[:, 0:8], 8)
                        tt(lg2[:, 8:16], lg[:, 8:16], pv[:, P_DTB:P_DTB + 8], ALU.add)
                        act(lg2[:, 8:16], lg2[:, 8:16], AF.Exp)
                        act(lg2[:, 8:16], lg2[:, 8:16], AF.Ln, bias=1.0)
                        tt(lg2[:, 8:16], lg2[:, 8:16], negA[:, :], ALU.mult)
                        cp(bgf[par][:, 0:4], lg2[:, 0:4], eng="pool")
                        cp(bgf[par][:, 4:8], lg2[:, 8:12], eng="pool")
                        cp(tf[:, F_BG:F_BG + 4], lg2[:, 4:8], eng="pool")
                        cp(tf[:, F_BG + 4:F_BG + 8], lg2[:, 12:16], eng="pool")

                def conv_part(i):
                    par = i % 2
                    tb = Tb[par]
                    for (c0, c1, pp) in ((0, 512, sc0), (512, 768, p6)):
                        w = c1 - c0
                        ops = [(SHPb, xw[i % 3][:, 0, c0:c1]), (identb, xw[i % 3][:, 1, c0:c1]),
                               (SHNb, xw[i % 3][:, 2, c0:c1])]
                        if i > 0:
                            ops.append((HPb, xw[(i - 1) % 3][:, 0, c0:c1]))
                        if i < n - 1:
                            ops.append((HNb, xw[(i + 1) % 3][:, 2, c0:c1]))
                        for k3, (lh, rh) in enumerate(ops):
                            mm(pp[:, 0:w], lh, rh, start=(k3 == 0), stop=(k3 == len(ops) - 1))
                        sigmoid_from(we[:, c0:c1], pp[:, 0:w], w)
                        tt(cv[:, c0:c1], pp[:, 0:w], we[:, c0:c1], ALU.mult)
                    act(sq[:, 0:512], cv[:, 0:512], AF.Square)
                    red(ss8[:, :], sq[:, 0:512].r("p (a b) -> p a b", a=8))
                    rsqrt_(rn8[:, :], ss8[:, :], 1.0, EPS)
                    ts(rn8[:, 0:4], rn8[:, 0:4], 0.125, None, ALU.mult)
                    tt(tb[:, B_GQ:B_GQ + 512].r("p (a b) -> p a b", a=8), cv[:, 0:512].r("p (a b) -> p a b", a=8),
                       rn8[:, :].bc(2, [128, 8, 64]), ALU.mult)
                    cp(tb[:, B_GV:B_GV + 256], cv[:, 512:768], eng="act")

                def post(i, extra=None):
                    par = i % 2
                    rows = slice(R0 + i * 128, R0 + (i + 1) * 128)
                    tb, tf = Tb[par], Tf[par]
                    S.capture()
                    gdn_core(0, tb[:, B_GQ:B_GQ + 256], tb[:, B_GK:B_GK + 256], tb[:, B_GV:B_GV + 256],
                             bgf[par][:, 0:4], bgf[par][:, 4:8], tf[:, F_OB:F_OB + 256], first=(i == 0))
                    LG1 = S.end_capture()
                    S.capture()
                    gla_core(0, tf[:, F_LQ:F_LQ + 128], tf[:, F_LK:F_LK + 128], tb[:, B_LV:B_LV + 256],
                             ngkf[par][:, :], tf[:, F_OC:F_OC + 256], first=(i == 0))
                    LL1 = S.end_capture()
                    S.merge([LG1, LL1] + ([extra] if extra else []))
                    dma("pool", stb.k(R0 // 128 + i)[rows, :], tb[:, :])
                    dma("pool", stf.k(R0 // 128 + i)[rows, :], tf[:, :])

                P_stage(0, 0)
                for i in range(n + 1):
                    LP = None
                    if i < n:
                        S.capture()
                        P_stage(i, 1)
                        if i + 1 < n:
                            P_stage(i + 1, 0)
                        conv_part(i)
                        LP = S.end_capture()
                    if i >= 1:
                        post(i - 1, LP)
                    elif LP:
                        S.merge([LP])

                if last_layer and final:
                    dma("sp", cv[:, 0:768], V(fng_d[0:768].partition_broadcast(128), "fng_d"))
                    dma("sp", we[:, 512:768], V(fng_d[768:1024].partition_broadcast(128), "fng_d"))
                def tail(t):
                    rows_t = slice(R0 + t * 128, R0 + (t + 1) * 128)
                    dma("sp", hin[:, :], src_h.k(R0 // 128 + t)[rows_t, :])
                    tt(os2[:, :], obcs[t % 2][:, :], Tf[t % 2][:, F_OB:F_OB + 512], ALU.add)
                    act(sq[:, 0:512], os2[:, :], AF.Square)
                    red(ss8[:, 0:8], sq[:, 0:512].r("p (a b) -> p a b", a=8))
                    rsqrt_(rn8[:, 0:8], ss8[:, 0:8], 1.0 / 64, EPS)
                    tt(os2[:, :].r("p (a b) -> p a b", a=8), os2[:, :].r("p (a b) -> p a b", a=8),
                       rn8[:, 0:8].bc(2, [128, 8, 64]), ALU.mult)
                    tt(os2[:, :], os2[:, :], pv[:, P_GOG:P_GOG + 512], ALU.mult, eng="pool")
                    tt(ycat[:, 256:768], os2[:, :], Tb[t % 2][:, B_GD:B_GD + 512], ALU.mult)
                    for c in range(8):
                        src_c = ycat[:, c * 128:(c + 1) * 128] if c < 6 else ymbs[t % 2][:, (c - 6) * 128:(c - 5) * 128]
                        tr(pbb[:, c * 128:(c + 1) * 128], src_c, identb)
                    cp(ycatT[:, :], pbb[:, :], eng="act")
                    for half, pp in ((0, sc1), (1, pb)):
                        for c in range(8):
                            mm(pp[:, :], ycatT[:, c * 128:(c + 1) * 128], wout[:, c, half * 512:(half + 1) * 512],
                               start=(c == 0), stop=(c == 7))
                        tt(hout[:, half * 512:(half + 1) * 512], pp[:, :], hin[:, half * 512:(half + 1) * 512], ALU.add)
                    if last_layer and final:
                        act(junk[:, :], hout[:, :], AF.Square, accum=ssq[:, :])
                        rsqrt_(rstd[:, :], ssq[:, :], 1.0 / D, EPS)
                        ts(hout[:, :], hout[:, :], rstd[:, :], None, ALU.mult)
                        tt(hout[:, 0:768], hout[:, 0:768], cv[:, 0:768], ALU.mult, eng="pool")
                        tt(hout[:, 768:1024], hout[:, 768:1024], we[:, 512:768], ALU.mult, eng="pool")
                        dma("pool", y_d.k(R0 // 128 + t)[rows_t, :], hout[:, :])
                    elif last_layer:
                        dma("pool", y_d.k(R0 // 128 + t)[rows_t, :], hout[:, :])
                    else:
                        dma("pool", hbuf.k(R0 // 128 + t)[rows_t, :], hout[:, :])
                nkt = n
                prev = None
                for i in range(n - 1, -1, -1):
                    rows = slice(R0 + i * 128, R0 + (i + 1) * 128)
                    Tb2, Tf2 = Tb[i % 2], Tf[i % 2]
                    dma("sp", Tb2[:, :], stb.k(R0 // 128 + i)[rows, :])
                    dma("sp", Tf2[:, :], stf.k(R0 // 128 + i)[rows, :])
                    first = (i == n - 1)
                    S.capture()
                    gdn_core(1, Tb2[:, B_GQ:B_GQ + 256], Tb2[:, B_GK:B_GK + 256], Tb2[:, B_GV:B_GV + 256],
                             Tf2[:, F_BG:F_BG + 4], Tf2[:, F_BG + 4:F_BG + 8], obcs[i % 2][:, 0:256], first=first)
                    LG = S.end_capture()
                    S.capture()
                    if prev is not None:
                        tail(prev)
                    gla_core(1, Tf2[:, F_LQ:F_LQ + 128], Tf2[:, F_LK:F_LK + 128], Tb2[:, B_LV:B_LV + 256],
                             Tf2[:, F_NGK:F_NGK + 128], obcs[i % 2][:, 256:512], first=first)
                    for p in range(2):
                        tr(ptr[:, 768 + p * 128:768 + (p + 1) * 128], Tb2[:, B_MQ + p * 128:B_MQ + (p + 1) * 128], identb)
                    cp(mqT[:, :, :], ptr[:, 768:1024].r("p (a b) -> p a b", a=2), eng="act")
                    for hp in range(2):
                        for hq in range(2):
                            h = 2 * hp + hq
                            pr, _hb = h // 2, (h % 2) * 64
                            for kt in range(2):
                                mm(sc1[:, (hq * 2 + kt) * 128:(hq * 2 + kt + 1) * 128],
                                   kmT[:, pr, h % 2, kt * 128:(kt + 1) * 128], mqT[:, pr, :])
                        act(pT[:, 512:1024], sc1[:, :], AF.Exp, scale=0.125)
                        for hq in range(2):
                            h = 2 * hp + hq
                            for kt in range(2):
                                mm(pb[:, h * 65:(h + 1) * 65], pT[:, 512 + (hq * 2 + kt) * 128:512 + (hq * 2 + kt + 1) * 128],
                                   Vmaug[:, kt, h, 0:65], start=(kt == 0), stop=(kt == 1))
                    recip(rs4[:, :], pb[:, 0:260].r("p (a b) -> p a b", a=4)[:, :, 64])
                    tt(ymbs[i % 2][:, :].r("p (a b) -> p a b", a=4), pb[:, 0:260].r("p (a b) -> p a b", a=4)[:, :, 0:64],
                       rs4[:, :].bc(2, [128, 4, 64]), ALU.mult)
                    tt(ymbs[i % 2][:, :], ymbs[i % 2][:, :], Tb2[:, B_GM:B_GM + 256], ALU.mult, eng="pool")
                    LL = S.end_capture()
                    S.capture()
                    for g2 in range(2):
                        tr(p6b[:, 768 + g2 * 128:768 + (g2 + 1) * 128], Tb2[:, B_Q + g2 * 128:B_Q + (g2 + 1) * 128], identb)
                    for h in range(4):
                        ts(qm[:, h, :], p6b[:, 768 + (h % 2) * 128:768 + (h % 2 + 1) * 128],
                           mBD[:, 64 * (h // 2):64 * (h // 2) + 1], None, ALU.mult)
                    for kt in range(nkt):
                        mm(sc0[:, :], KT[:, kt * 128:(kt + 1) * 128], qm[:, :, :].r("p a b -> p (a b)"))
                        act(pT[:, 0:512], sc0[:, :], AF.Exp, scale=0.125)
                        for h in range(4):
                            mm(p6[:, h * 65:(h + 1) * 65], pT[:, h * 128:(h + 1) * 128], Vaug[:, kt, h // 2, 0:65],
                               start=(kt == 0 and h == 0), stop=(kt == nkt - 1), skip=True)
                    recip(rs4a[:, :], p6[:, 0:260].r("p (a b) -> p a b", a=4)[:, :, 64])
                    tt(yaa[:, :].r("p (a b) -> p a b", a=4), p6[:, 0:260].r("p (a b) -> p a b", a=4)[:, :, 0:64],
                       rs4a[:, :].bc(2, [128, 4, 64]), ALU.mult)
                    tt(ycat[:, 0:256], yaa[:, :], Tb2[:, B_GA:B_GA + 256], ALU.mult, eng="pool")
                    LA = S.end_capture()
                    S.merge([LG, LL, LA])
                    prev = i
                tail(prev)
        if _os.environ.get("KDESC"):
            lo, hi = [int(v) for v in _os.environ["KDESC"].split(",")]
            for dd in S.desc:
                if lo <= dd[0] <= hi:
                    print("OP", dd)
        if _os.environ.get("KLIMIT"):
            print("TOTAL OPS", S.total)
        S.emit(st)
    return nc


def _consts():
    c = np.zeros((128, NCST), np.float32)
    p = np.arange(128)[:, None]
    f = np.arange(128)[None, :]
    bd = (p // 64 == f // 64)
    c[:, C_ID:C_ID + 128] = (p == f)
    c[:, C_LE:C_LE + 128] = bd & (p <= f)
    c[:, C_LT:C_LT + 128] = bd & (p < f)
    c[:, C_GE:C_GE + 128] = bd & (p >= f)
    c[:, C_GT:C_GT + 128] = bd & (p > f)
    c[:, C_BD:C_BD + 128] = bd
    c[:, C_SEL0:C_SEL0 + 128] = (p < 64) & (f >= 0)
    c[:, C_SEL1:C_SEL1 + 128] = (p >= 64) & (f >= 0)
    c[:, C_RM4:C_RM4 + 4] = (p // 32 == np.arange(4)[None, :])
    c[:, C_BD4:C_BD4 + 256] = (p // 32 == np.arange(256)[None, :] // 64)
    c[:, C_ONE] = 1.0
    cb = np.zeros((128, 640), np.float32)
    cb[:, 0:128] = (p == f)
    cb[:, 128:256] = (p == f - 1)
    cb[:, 256:384] = (p == f + 1)
    cb[:, 384:512] = (p == 127) & (f == 0)
    cb[:, 512:640] = (p == 0) & (f == 127)
    return c, cb


def _rope_tables(smax):
    t = np.arange(smax)
    pos = np.stack([t // 64, t % 64], -1).astype(np.float32)
    inv = (10000.0 ** (-np.arange(0, 32, 2, dtype=np.float32) / 32)).astype(np.float32)
    ang = pos[:, :, None] * inv[None, None, :]
    cos, sin = np.cos(ang).astype(np.float32), np.sin(ang).astype(np.float32)
    C = np.zeros((smax, 2, 2, 16), np.float32)
    Sg = np.zeros((smax, 2, 2, 16), np.float32)
    C[:, :, 0, :] = cos
    C[:, :, 1, :] = cos
    Sg[:, :, 0, :] = -sin
    Sg[:, :, 1, :] = sin
    C = np.tile(C.reshape(smax, 1, 64), (1, 6, 1)).reshape(smax, 384)
    Sg = np.tile(Sg.reshape(smax, 1, 64), (1, 6, 1)).reshape(smax, 384)
    return np.ascontiguousarray(C), np.ascontiguousarray(Sg)


def _col_perm():
    o = dict(a_q=0, a_k=256, a_v=384, a_gate=512, d_qkv=768, d_beta=1536, d_alpha=1544, d_gate=1552,
             l_q=1808, l_k=1936, l_v=2064, l_low=2320, l_gate=2352, m_q=2608, m_gate=2864)
    r = lambda a, n: list(range(a, a + n))
    aq = []
    for g in range(2):
        for kv in range(2):
            h = kv * 2 + g
            aq += r(o["a_q"] + h * 64, 64)
    perm = (aq + r(o["a_k"], 128) + r(o["a_v"], 128)
            + r(o["a_gate"], 256) + r(o["m_gate"], 256)
            + r(o["d_gate"], 256) + r(o["l_gate"], 256)
            + r(o["d_qkv"], 512)
            + r(o["d_qkv"] + 512, 256) + r(o["d_beta"], 8) + r(o["d_alpha"], 8)
            + r(o["l_q"], 128) + r(o["l_k"], 128) + r(o["l_v"], 256)
            + r(o["m_q"], 256) + r(o["l_low"], 32))
    assert len(perm) == NCOL and len(set(perm)) == NCOL
    return np.array(perm)


def _prep_shared(inp, depth):
    f = lambda a: np.ascontiguousarray(np.asarray(a, dtype=np.float32))
    perm = _col_perm()
    w_in = f(np.asarray(inp["w_in"])[:depth][:, :, perm])
    pvec = np.zeros((depth, NPV), np.float32)
    wup = np.zeros((depth, 32, 256), np.float32)
    for l in range(depth):
        pvec[l, P_GQK:P_GQK + 256] = np.tile(np.asarray(inp["att_q_norm_g"])[l], 4)
        pvec[l, P_GQK + 256:P_GQK + 384] = np.tile(np.asarray(inp["att_k_norm_g"])[l], 2)
        pvec[l, P_ALOG:P_ALOG + 8] = np.asarray(inp["gdn_a_log"])[l].reshape(-1)
        pvec[l, P_DTB:P_DTB + 8] = np.asarray(inp["gdn_dt_bias"])[l].reshape(-1)
        pvec[l, P_GOG:P_GOG + 256] = np.tile(np.asarray(inp["gdn_out_norm_g"])[l], 4)
        pvec[l, P_GLB:P_GLB + 256] = np.asarray(inp["gla_b_gate"])[l].reshape(-1)
        pvec[l, P_GLG:P_GLG + 256] = np.tile(np.asarray(inp["gla_out_norm_g"])[l], 4)
        for d in range(2):
            wup[l, d * 16:(d + 1) * 16, d * 128:(d + 1) * 128] = np.asarray(inp["gla_w_gate_up"])[l, d]
    return dict(w_in=w_in, w_out=f(np.asarray(inp["w_out"])[:depth]), w_mem=f(np.asarray(inp["w_mem_kv"])[:depth]),
                norm_g=f(np.asarray(inp["norm_g"])[:depth]), mem_g=f(np.asarray(inp["mem_norm_g"])[:depth]),
                pvec=pvec, wup=wup, fng=f(inp["final_norm_g"]), cst=_consts()[0], cstb=_consts()[1],
                convw=f(np.asarray(inp["gdn_conv_w"])[:depth].reshape(depth, -1)))


def kernel(**inputs):
    depth = 4
    xp = np.asarray(inputs["x_prompt"], np.float32)
    xs = np.asarray(inputs["x_sample"], np.float32)
    mp = np.asarray(inputs["mem_prompt"], np.float32)
    ms = np.asarray(inputs["mem_sample"], np.float32)
    Sp, Ss = xp.shape[1], xs.shape[1]
    seqs = [Sp, Ss, Ss]
    shared = _prep_shared(inputs, depth)
    C, Sg = _rope_tables(max(seqs))
    shared["ropec"], shared["ropes"] = C, Sg
    nc = _build(seqs, depth)
    in_maps = []
    for c in range(8):
        m = dict(shared)
        m["x"] = np.ascontiguousarray(np.concatenate([xp[c], xs[2 * c], xs[2 * c + 1]], 0))
        m["mem"] = np.ascontiguousarray(np.concatenate([mp[c], ms[2 * c], ms[2 * c + 1]], 0))
        in_maps.append(m)
    res = run_bass_kernel_spmd(nc, in_maps, core_ids=list(range(8)))
    yp = np.stack([res.results[c]["y"][:Sp] for c in range(8)], 0)
    ysm = np.stack([res.results[c // 2]["y"][Sp + (c % 2) * Ss:Sp + (c % 2 + 1) * Ss] for c in range(16)], 0)
    return (yp.astype(np.float32), ysm.astype(np.float32))
```
